# Optimizing a Trainium2 kernel written in Bass

```python
import math, functools
import jax, jax.numpy as jnp
from jax import lax
import numpy as np

D_MODEL = 1024
BATCH = 32
SEQ = 2048
DEPTH = 1
DEC_BATCH = 1
DEC_SEQ = 16384
PAST_LEN = 128

HEAD_DIM = 128
HEADS_PER_GROUP = 4
DILATION_GROUPS = ((128, 1), (512, 4), (2048, 16))
N_GROUPS = len(DILATION_GROUPS)
N_ATTN_HEADS = N_GROUPS * HEADS_PER_GROUP
ATTN_WIDTH = N_ATTN_HEADS * HEAD_DIM
ATTN_OUT_WIDTH = HEADS_PER_GROUP * HEAD_DIM
BAND_BLOCK = 64
N_BUCKETS = 32
MAX_DISTANCE = 1024
HYENA_WIDTH = D_MODEL
HYENA_ORDER = 2
SHORT_CONV = 3
FILTER_BANDS = 16
FILTER_EMB_DIM = 2 * FILTER_BANDS + 1
FILTER_HIDDEN = 64
N_FILTER_CH = HYENA_ORDER * 2 * HYENA_WIDTH
FILTER_OUT_SCALE = 0.02
DECAY_TARGET = 1e-2
FAST_DECAY_PCT = 0.3
SLOW_DECAY_PCT = 1.5
N_BRANCHES = 2
D_FF = 4 * D_MODEL
IN_WIDTH = 3 * ATTN_WIDTH + (HYENA_ORDER + 1) * HYENA_WIDTH + N_BRANCHES * D_MODEL
NORM_EPS = 1e-6
MASK_VALUE = -1e30

kernel_name = 'hybrid_dilated_attn_hyena_encoder'


def rms_norm(x, g):
    xf = x.astype(jnp.float32)
    y = xf * lax.rsqrt(jnp.mean(xf * xf, axis=-1, keepdims=True) + NORM_EPS)
    return (y * g.astype(jnp.float32)).astype(x.dtype)


def t5_bucket(rel):
    nb = N_BUCKETS // 2
    max_exact = nb // 2
    side = jnp.where(rel > 0, nb, 0)
    n = jnp.abs(rel)
    nf = jnp.maximum(n, 1).astype(jnp.float32)
    large = max_exact + (jnp.log(nf / max_exact) / math.log(MAX_DISTANCE / max_exact)
                         * (nb - max_exact)).astype(jnp.int32)
    large = jnp.minimum(large, nb - 1)
    return side + jnp.where(n < max_exact, n, large)


def dilated_band_attention(q, k, v, bias_table, dil, radius):
    B, L, G, hd = q.shape
    M = L // dil
    nblk = -(-M // BAND_BLOCK)
    Mp = nblk * BAND_BLOCK

    def to_sub(t):
        return t.reshape(B, M, dil, G, hd).transpose(0, 2, 1, 3, 4)

    def windows(t):
        tp = jnp.pad(t, ((0, 0), (0, 0), (BAND_BLOCK, Mp - M + BAND_BLOCK), (0, 0), (0, 0)))
        tp = tp.reshape(B, dil, nblk + 2, BAND_BLOCK, G, hd)
        return jnp.concatenate([tp[:, :, :-2], tp[:, :, 1:-1], tp[:, :, 2:]], axis=3)

    qs = jnp.pad(to_sub(q), ((0, 0), (0, 0), (0, Mp - M), (0, 0), (0, 0)))
    qb = qs.reshape(B, dil, nblk, BAND_BLOCK, G, hd)
    kw = windows(to_sub(k))
    vw = windows(to_sub(v))

    scores = jnp.einsum('bdnqgh,bdnkgh->bdngqk', qb, kw,
                        preferred_element_type=jnp.float32) * (HEAD_DIM ** -0.5)
    qi = jnp.arange(BAND_BLOCK)[:, None]
    kj = jnp.arange(3 * BAND_BLOCK)[None, :]
    delta = kj - BAND_BLOCK - qi
    bias = bias_table[t5_bucket(delta * dil)].astype(jnp.float32)
    bias = bias.transpose(2, 0, 1)
    key_m = jnp.arange(nblk)[:, None, None] * BAND_BLOCK - BAND_BLOCK + kj[None]
    valid = (jnp.abs(delta) <= radius)[None] & (key_m >= 0) & (key_m < M)
    scores = jnp.where(valid[:, None], scores + bias, MASK_VALUE)

    mx = jnp.max(scores, axis=-1, keepdims=True)
    p = jnp.exp(scores - mx)
    den = jnp.sum(p, axis=-1, keepdims=True)
    out = jnp.einsum('bdngqk,bdnkgh->bdnqgh', (p / den).astype(v.dtype), vw)
    lse = (mx + jnp.log(den))[..., 0]

    out = out.reshape(B, dil, Mp, G, hd)[:, :, :M].transpose(0, 2, 1, 3, 4).reshape(B, L, G, hd)
    lse = lse.transpose(0, 1, 2, 4, 3).reshape(B, dil, Mp, G)[:, :, :M]
    lse = lse.transpose(0, 2, 1, 3).reshape(B, L, G)
    return out.astype(jnp.float32), lse


def short_conv3(u, w, b):
    up = jnp.pad(u, ((0, 0), (1, 1), (0, 0)))
    return up[:, :-2] * w[0] + up[:, 1:-1] * w[1] + up[:, 2:] * w[2] + b


def hyena_filters(L, w1, b1, w2, b2, w3, b3, w4, freq):
    f32 = jnp.float32
    t = jnp.linspace(0.0, 1.0, L, dtype=f32)[:, None]
    ang = 2.0 * math.pi * jnp.arange(L, dtype=f32)[:, None] / L
    bands = jnp.linspace(1e-4, FILTER_BANDS - 1, FILTER_BANDS, dtype=f32)[None, :]
    z = jnp.concatenate([t, jnp.cos(bands * ang), -jnp.sin(bands * ang)], axis=-1)
    fr = freq.astype(f32)
    h = jnp.sin(fr * (z @ w1.astype(f32) + b1.astype(f32)))
    h = jnp.sin(fr * (h @ w2.astype(f32) + b2.astype(f32)))
    h = jnp.sin(fr * (h @ w3.astype(f32) + b3.astype(f32)))
    h = (h @ w4.astype(f32)).reshape(L, HYENA_ORDER, 2, HYENA_WIDTH)
    min_decay = math.log(DECAY_TARGET) / SLOW_DECAY_PCT
    max_decay = math.log(DECAY_TARGET) / FAST_DECAY_PCT
    deltas = jnp.linspace(min_decay, max_decay, HYENA_WIDTH, dtype=f32)
    decay = jnp.exp(-t * jnp.abs(deltas)[None, :])
    return h * decay[:, None, None, :]


def bidir_long_conv(z, h_fwd, h_bwd, skip):
    L = z.shape[1]
    k2 = jnp.concatenate([h_fwd, jnp.zeros_like(h_fwd[:1]), h_bwd[:0:-1]], axis=0)
    zf = z.astype(jnp.float32)
    spec = jnp.fft.rfft(zf, n=2 * L, axis=1) * jnp.fft.rfft(k2, axis=0)[None]
    y = jnp.fft.irfft(spec, n=2 * L, axis=1)[:, :L]
    return (y + zf * skip.astype(jnp.float32)).astype(z.dtype)


def encoder_layer(x, rel_bias, g_mix, w_in, g_q, g_k, w_attn_branch, w_short, b_short,
                  filt_w1, filt_b1, filt_w2, filt_b2, filt_w3, filt_b3, filt_w4, filt_freq,
                  filt_skip, w_hyena_branch, w_out, g_mlp, w_ff1, w_ff2):
    B, L, _ = x.shape
    h = rms_norm(x, g_mix)
    proj = h @ w_in
    q, k, v, hy, gate_logits = jnp.split(
        proj, [ATTN_WIDTH, 2 * ATTN_WIDTH, 3 * ATTN_WIDTH,
               3 * ATTN_WIDTH + (HYENA_ORDER + 1) * HYENA_WIDTH], axis=-1)

    q = rms_norm(q.reshape(B, L, N_ATTN_HEADS, HEAD_DIM), g_q)
    k = rms_norm(k.reshape(B, L, N_ATTN_HEADS, HEAD_DIM), g_k)
    v = v.reshape(B, L, N_ATTN_HEADS, HEAD_DIM)
    outs, lses = [], []
    for gi, (window, dil) in enumerate(DILATION_GROUPS):
        hs = slice(gi * HEADS_PER_GROUP, (gi + 1) * HEADS_PER_GROUP)
        o, s = dilated_band_attention(q[:, :, hs], k[:, :, hs], v[:, :, hs],
                                      rel_bias[:, hs], dil, window // (2 * dil))
        outs.append(o)
        lses.append(s)
    wgt = jax.nn.softmax(jnp.stack(lses, axis=2), axis=2)
    attn = jnp.einsum('blngh,blng->blgh', jnp.stack(outs, axis=2), wgt)
    attn_branch = attn.reshape(B, L, ATTN_OUT_WIDTH).astype(x.dtype) @ w_attn_branch

    hy = short_conv3(hy, w_short, b_short)
    z, *hy_gates = jnp.split(hy, HYENA_ORDER + 1, axis=-1)
    filt = hyena_filters(L, filt_w1, filt_b1, filt_w2, filt_b2, filt_w3, filt_b3, filt_w4, filt_freq)
    for n in range(HYENA_ORDER):
        z = hy_gates[n] * bidir_long_conv(z, filt[:, n, 0], filt[:, n, 1], filt_skip[n])
    hyena_branch = z @ w_hyena_branch

    g_attn, g_hyena = jnp.split(gate_logits, N_BRANCHES, axis=-1)
    merged = jax.nn.sigmoid(g_attn) * attn_branch + jax.nn.sigmoid(g_hyena) * hyena_branch
    x = x + merged @ w_out

    hm = rms_norm(x, g_mlp)
    return x + jnp.square(jax.nn.relu(hm @ w_ff1)) @ w_ff2


def setup_inputs(seed: int = 0) -> dict:
    key = jax.random.key(seed)
    ks = jax.random.split(key, 32)

    def nrm(k, shape, scale):
        return jax.random.normal(k, shape, jnp.float32) * scale

    HW = HYENA_WIDTH
    return {
        'x_prompt': nrm(ks[0], (BATCH, SEQ, D_MODEL), 1.0),
        'x_sample': nrm(ks[1], (DEC_BATCH, DEC_SEQ, D_MODEL), 1.0),
        'rel_bias': nrm(ks[2], (N_BUCKETS, N_ATTN_HEADS), 0.1),
        'g_mix': 1.0 + nrm(ks[3], (DEPTH, D_MODEL), 0.1),
        'w_in': nrm(ks[4], (DEPTH, D_MODEL, IN_WIDTH), D_MODEL ** -0.5),
        'g_q': 1.0 + nrm(ks[5], (DEPTH, HEAD_DIM), 0.1),
        'g_k': 1.0 + nrm(ks[6], (DEPTH, HEAD_DIM), 0.1),
        'w_attn_branch': nrm(ks[7], (DEPTH, ATTN_OUT_WIDTH, D_MODEL), ATTN_OUT_WIDTH ** -0.5),
        'w_short': nrm(ks[8], (DEPTH, SHORT_CONV, (HYENA_ORDER + 1) * HW), SHORT_CONV ** -0.5),
        'b_short': nrm(ks[9], (DEPTH, (HYENA_ORDER + 1) * HW), 0.02),
        'filt_w1': nrm(ks[10], (DEPTH, FILTER_EMB_DIM, FILTER_HIDDEN), FILTER_EMB_DIM ** -0.5),
        'filt_b1': nrm(ks[11], (DEPTH, FILTER_HIDDEN), 0.1),
        'filt_w2': nrm(ks[12], (DEPTH, FILTER_HIDDEN, FILTER_HIDDEN), FILTER_HIDDEN ** -0.5),
        'filt_b2': nrm(ks[13], (DEPTH, FILTER_HIDDEN), 0.1),
        'filt_w3': nrm(ks[14], (DEPTH, FILTER_HIDDEN, FILTER_HIDDEN), FILTER_HIDDEN ** -0.5),
        'filt_b3': nrm(ks[15], (DEPTH, FILTER_HIDDEN), 0.1),
        'filt_w4': nrm(ks[16], (DEPTH, FILTER_HIDDEN, N_FILTER_CH), FILTER_OUT_SCALE * FILTER_HIDDEN ** -0.5),
        'filt_freq': 1.0 + nrm(ks[17], (DEPTH, FILTER_HIDDEN), 0.1),
        'filt_skip': 1.0 + nrm(ks[18], (DEPTH, HYENA_ORDER, HW), 0.1),
        'w_hyena_branch': nrm(ks[19], (DEPTH, HW, D_MODEL), HW ** -0.5),
        'w_out': nrm(ks[20], (DEPTH, D_MODEL, D_MODEL), D_MODEL ** -0.5),
        'g_mlp': 1.0 + nrm(ks[21], (DEPTH, D_MODEL), 0.1),
        'w_ff1': nrm(ks[22], (DEPTH, D_MODEL, D_FF), D_MODEL ** -0.5),
        'w_ff2': nrm(ks[23], (DEPTH, D_FF, D_MODEL), D_FF ** -0.5),
    }


def reference(x_prompt, x_sample, rel_bias, g_mix, w_in, g_q, g_k, w_attn_branch, w_short,
              b_short, filt_w1, filt_b1, filt_w2, filt_b2, filt_w3, filt_b3, filt_w4,
              filt_freq, filt_skip, w_hyena_branch, w_out, g_mlp, w_ff1, w_ff2):
    y_prompt = x_prompt
    y_sample = x_sample
    for l in range(DEPTH):
        layer = functools.partial(
            encoder_layer, rel_bias=rel_bias, g_mix=g_mix[l], w_in=w_in[l], g_q=g_q[l],
            g_k=g_k[l], w_attn_branch=w_attn_branch[l], w_short=w_short[l],
            b_short=b_short[l], filt_w1=filt_w1[l], filt_b1=filt_b1[l], filt_w2=filt_w2[l],
            filt_b2=filt_b2[l], filt_w3=filt_w3[l], filt_b3=filt_b3[l], filt_w4=filt_w4[l],
            filt_freq=filt_freq[l], filt_skip=filt_skip[l],
            w_hyena_branch=w_hyena_branch[l], w_out=w_out[l], g_mlp=g_mlp[l],
            w_ff1=w_ff1[l], w_ff2=w_ff2[l])
        y_prompt = layer(y_prompt)
        y_sample = layer(y_sample)
    return (y_prompt, y_sample)
```

```python
import math
from contextlib import ExitStack
import numpy as np
import ml_dtypes
import concourse.bass as bass
import concourse.mybir as mybir
from concourse.bass_utils import run_bass_kernel_spmd

F32 = mybir.dt.float32
BF16 = mybir.dt.bfloat16
ALU = mybir.AluOpType
ACTF = mybir.ActivationFunctionType

D = 1024
L = 2048
NCORES = 8
EPS = 1e-6
HD = 128
IN_W = 9728
C_Q, C_K, C_V, C_HY, C_G = 0, 1536, 3072, 4608, 7680
MASKV = -30000.0
GROUPS = ((128, 1), (512, 4), (2048, 16))


class Buf:
    __slots__ = ("name", "w", "r")

    def __init__(self, name=""):
        self.name = name
        self.w = None
        self.r = {}


class Eng:
    def __init__(self, name, h, sem):
        self.name, self.h, self.sem = name, h, sem
        self.count = 0
        self.seen = {}


class DmaQ:
    def __init__(self, name, eng, sems):
        self.name, self.eng, self.sems = name, eng, sems
        self.n = 0


class FW:
    def __init__(self, nc, stack, n_dma_sems=8):
        self.nc = nc
        S = lambda n: stack.enter_context(nc.semaphore(n))
        self.pe = Eng("pe", nc.tensor, S("s_pe"))
        self.act = Eng("act", nc.scalar, S("s_act"))
        self.dve = Eng("dve", nc.vector, S("s_dve"))
        self.pool = Eng("pool", nc.gpsimd, S("s_pool"))
        self.sp = Eng("sp", nc.sync, S("s_sp"))
        self.engs = [self.pe, self.act, self.dve, self.pool, self.sp]
        self.q_sp = DmaQ("qsp", self.sp, [S(f"s_qsp{i}") for i in range(n_dma_sems)])
        self.q_pool = DmaQ("qpl", self.pool, [S(f"s_qpl{i}") for i in range(n_dma_sems)])
        self.qs = [self.q_sp, self.q_pool]
        self.n_inst = 0

    def _wait(self, eng, key, sem, val):
        if eng.seen.get(key, 0) >= val:
            return
        eng.h.wait_ge(sem, val)
        eng.seen[key] = val
        self.n_inst += 1

    def _deps(self, eng, reads, writes):
        need = {}

        def add(m):
            if m is None:
                return
            k, s, v = m
            if k not in need or need[k][1] < v:
                need[k] = (s, v)

        for b in reads:
            add(b.w)
        for b in writes:
            add(b.w)
            for k, (s, v) in b.r.items():
                add((k, s, v))
        for k, (s, v) in need.items():
            if k == "pe" and eng is self.pe:
                continue
            self._wait(eng, k, s, v)

    def _mark(self, reads, writes, m):
        k, s, v = m
        for b in reads:
            b.r[k] = (s, v)
        for b in writes:
            b.w = m
            b.r = {}

    def op(self, eng, fn, reads=(), writes=()):
        self._deps(eng, reads, writes)
        inst = fn()
        eng.count += 1
        inst.then_inc(eng.sem, 1)
        m = (eng.name, eng.sem, eng.count)
        self._mark(reads, writes, m)
        self.n_inst += 1
        return m

    def dma(self, q, out, in_, reads=(), writes=(), **kw):
        eng = q.eng
        j = q.n
        ns = len(q.sems)
        sem = q.sems[j % ns]
        key = f"{q.name}{j % ns}"
        prev = 16 * (j // ns)
        if prev > 0:
            self._wait(eng, key, sem, prev)
        self._deps(eng, reads, writes)
        inst = eng.h.dma_start(out=out, in_=in_, **kw)
        inst.then_inc(sem, 16)
        q.n += 1
        m = (key, sem, prev + 16)
        self._mark(reads, writes, m)
        self.n_inst += 1
        return m

    def barrier(self):
        for e in self.engs:
            for o in self.engs:
                if o is e or o.count == 0:
                    continue
                self._wait(e, o.name, o.sem, o.count)
            for q in self.qs:
                ns = len(q.sems)
                for i in range(min(ns, q.n)):
                    last = ((q.n - 1 - i) // ns) * ns + i
                    self._wait(e, f"{q.name}{i}", q.sems[i], 16 * (last // ns + 1))


_UID = [0]


def _uid(name):
    _UID[0] += 1
    return f"{name}_{_UID[0]}"


class Ring:
    def __init__(self, nc, st, name, n, shape, dt):
        self.items = []
        for i in range(n):
            t = st.enter_context(nc.sbuf_tensor(_uid(f"{name}{i}"), shape, dt))
            self.items.append((t, Buf(f"{name}{i}")))
        self.i = 0

    def next(self):
        it = self.items[self.i % len(self.items)]
        self.i += 1
        return it


def _bf(a):
    return np.ascontiguousarray(a.astype(ml_dtypes.bfloat16))


def t5_bucket_np(rel):
    nb, max_exact = 16, 8
    side = np.where(rel > 0, nb, 0)
    n = np.abs(rel)
    nf = np.maximum(n, 1).astype(np.float32)
    large = max_exact + (np.log(nf / np.float32(max_exact)) / np.float32(math.log(1024 / max_exact))
                         * np.float32(nb - max_exact)).astype(np.int32)
    large = np.minimum(large, nb - 1)
    return side + np.where(n < max_exact, n, large)


def host_consts(core):
    c = {}
    c["ident"] = _bf(np.eye(128, dtype=np.float32))
    E = np.zeros((3, 33, 384), np.float32)
    for g, (_, d) in enumerate(GROUPS):
        for u in range(384):
            dl = u - 191
            if abs(dl) <= 64:
                E[g, int(t5_bucket_np(np.int32(dl * d))), u] = 1.0
            else:
                E[g, 32, u] = 1.0
    c["Esel"] = E
    kvt = np.zeros((2, 128, 192), np.float32)
    for kind in range(2):
        for g, (_, d) in enumerate(GROUPS):
            Me = 2 * L // d
            for cc in range(64):
                col = 64 * cc + np.arange(128)
                r = col // Me
                mp = col % Me
                e = r + d * mp
                if kind == 0:
                    ok = (e >= 1024) & (e < 3072)
                else:
                    gtok = 2048 * core + e - 1024
                    ok = (gtok >= 0) & (gtok < 16384)
                ok = ok & (col < 2 * L)
                kvt[kind, :, g * 64 + cc] = np.where(ok, 0.0, MASKV)
    c["kvalid"] = kvt
    return c


NFFT = 2 * L


def dft_tables():
    n = np.arange(NFFT, dtype=np.int64)
    k = np.arange(L, dtype=np.int64)
    ph = (np.outer(n, k) % NFFT).astype(np.float64) * (2 * np.pi / NFFT)
    Fre = np.cos(ph)
    Fim = -np.sin(ph)
    Fim[:, 0] = np.where(n % 2 == 0, 1.0, -1.0)
    Fall = np.concatenate([Fre, Fim], axis=1).astype(np.float32)
    Ftab = Fall.reshape(32, 128, 32, 128).transpose(2, 1, 0, 3)
    nn = np.arange(L, dtype=np.int64)
    ph2 = (np.outer(k, nn) % NFFT).astype(np.float64) * (2 * np.pi / NFFT)
    Gre = (2.0 / NFFT) * np.cos(ph2)
    Gre[0, :] = 1.0 / NFFT
    Gim = -(2.0 / NFFT) * np.sin(ph2)
    Gim[0, :] = np.where(nn % 2 == 0, 1.0, -1.0) / NFFT
    Gall = np.concatenate([Gre, Gim], axis=0).astype(np.float32)
    Gtab = Gall.reshape(32, 128, 4, 512).transpose(2, 0, 1, 3)
    return _bf(Ftab), _bf(Gtab)


SLOT_ORDERS = [0, 1] + [0] * 15 + [1] * 8
NSLOT_SH = 17


def _slot_rows(Ls, dd):
    r = np.arange(NFFT, dtype=np.int64)
    bands = np.linspace(1e-4, 15.0, 16, dtype=np.float32)[None, :]
    m = np.where(r < L, L * dd + r, L * dd + r - NFFT)
    pos = np.abs(m)
    ok = (pos < Ls) & (r != L)
    pos = np.where(ok, pos, 0)
    t = (pos.astype(np.float32) / np.float32(Ls - 1)).astype(np.float32)
    ang = (np.float32(2.0 * math.pi) * pos.astype(np.float32) / np.float32(Ls)).astype(np.float32)[:, None]
    z = np.concatenate([t[:, None], np.cos(bands * ang), -np.sin(bands * ang)], axis=1).astype(np.float32)
    ffwd = (ok & (m >= 0)).astype(np.float32)
    fbwd = (ok & (m < 0)).astype(np.float32)
    cols = np.zeros((128, 3, 32), np.float32)
    for a in range(32):
        cols[:, 0, a] = t[a * 128:(a + 1) * 128]
        cols[:, 1, a] = ffwd[a * 128:(a + 1) * 128]
        cols[:, 2, a] = fbwd[a * 128:(a + 1) * 128]
    return np.ascontiguousarray(z.T), cols


def filter_tables_shared():
    specs = [(L, 0), (L, 0)] + [(8 * L, d) for d in range(-7, 8)]
    rows = [_slot_rows(*sp) for sp in specs]
    zf = np.stack([r[0] for r in rows]).astype(np.float32)
    cols = np.stack([r[1] for r in rows]).astype(np.float32)
    min_decay = math.log(1e-2) / 1.5
    max_decay = math.log(1e-2) / 0.3
    nad = -np.abs(np.linspace(min_decay, max_decay, D, dtype=np.float32))
    return zf, cols, nad.astype(np.float32)


def filter_tables_core(core):
    rows = [_slot_rows(8 * L, core - J) for J in range(8)]
    zf = np.stack([r[0] for r in rows]).astype(np.float32)
    cols = np.stack([r[1] for r in rows]).astype(np.float32)
    oh = np.zeros((128, 8), np.float32)
    oh[:, core] = 1.0
    return zf, cols, oh


class Job:
    def __init__(self, idx, sample):
        self.idx, self.sample = idx, sample


class KB:
    def __init__(self, nc, st, njobs, dbg=(), nslots=1):
        self.nc = nc
        self.st = st
        self.fw = FW(nc, st)
        self.njobs = njobs
        self.dbg = set(dbg)
        self.dbg_out = {}
        di = lambda n, s, d=F32: nc.dram_tensor(n, s, d, kind="ExternalInput").ap()
        self.xm = di("xm", [njobs, L, D])
        self.xh = di("xh", [L, D])
        self.w_in = di("w_in", [D, IN_W])
        self.w_ab = di("w_ab", [512, D])
        self.w_hb = di("w_hb", [D, D])
        self.w_out = di("w_out", [D, D])
        self.w_ff1 = di("w_ff1", [D, 4 * D])
        self.w_ff2 = di("w_ff2", [4 * D, D])
        self.g_mix = di("g_mix", [D])
        self.g_mlp = di("g_mlp", [D])
        self.identd = di("ident", [128, 128], BF16)
        self.rel_bias = di("rel_bias", [32, 12])
        self.g_q = di("g_q", [128])
        self.g_k = di("g_k", [128])
        self.Esel = di("Esel", [3, 33, 384])
        self.kvd = di("kvalid", [2, 128, 192])
        self.bsc = nc.dram_tensor("bsc", [3, 12, 384], F32, kind="Internal").ap()
        self.nslots = nslots
        self.f_w1 = di("filt_w1", [33, 64]); self.f_b1 = di("filt_b1", [64])
        self.f_w2 = di("filt_w2", [64, 64]); self.f_b2 = di("filt_b2", [64])
        self.f_w3 = di("filt_w3", [64, 64]); self.f_b3 = di("filt_b3", [64])
        self.f_w4 = di("filt_w4", [64, 4096]); self.f_freq = di("filt_freq", [64])
        self.f_skip = di("filt_skip", [2, D])
        self.w_short = di("w_short", [3, 3 * D]); self.b_short = di("b_short", [3 * D])
        self.zfTd = di("zfT_tab", [NSLOT_SH, 33, NFFT])
        self.fcolsd = di("fcols_tab", [NSLOT_SH, 128, 3, 32])
        self.zfTc = di("zfT_core", [8, 33, NFFT])
        self.fcolsc = di("fcols_core", [8, 128, 3, 32])
        self.onehotd = di("onehot", [128, 8])
        self.xfull = di("xfull", [8, L, D])
        self.xedge = di("xedge", [8, 128, D])
        self.nadd = di("nad_tab", [D])
        self.Ftab = di("Ftab", [32, 128, 32, 128], BF16)
        self.Gtab = di("Gtab", [4, 32, 128, 512], BF16)
        self.Hs = nc.dram_tensor("Hs", [nslots, 2, 32, 128, 512], BF16, kind="Internal").ap()
        self.b_Hs = Buf("Hs")
        dsc = lambda n, sh: nc.dram_tensor(n, sh, BF16, kind="Internal").ap()
        self.xs0, self.xs1 = dsc("xs0", [8, NFFT, 512]), dsc("xs1", [8, NFFT, 512])
        self.zs, self.x1s, self.z1s = dsc("zs", [8, 128, 4, L]), dsc("x1s", [8, 128, 4, L]), dsc("z1s", [8, 128, 4, L])
        self.b_xs0, self.b_xs1, self.b_zs, self.b_x1s, self.b_z1s = [Buf(n) for n in ("xs0", "xs1", "zs", "x1s", "z1s")]
        self.y = nc.dram_tensor("y", [njobs, L, D], F32, kind="ExternalOutput").ap()
        self.pb = []
        for i in range(7):
            t = st.enter_context(nc.psum_tensor(f"pb{i}", [128, 512], F32))
            self.pb.append((t, Buf(f"pb{i}")))
        self.ptr = st.enter_context(nc.psum_tensor("ptr", [128, 1024], BF16))
        self.b_ptr = Buf("ptr")
        self.pbi = 0
        self.ident, self.b_ident = self.sb(st, "identb", [128, 128], BF16)
        self.fw.dma(self.fw.q_sp, self.ident[:], self.identd[:, :], writes=[self.b_ident])

    def sb(self, st, name, shape, dt):
        t = st.enter_context(self.nc.sbuf_tensor(_uid(name), shape, dt))
        return t, Buf(name)

    def alloc_persistent(self):
        nc, st = self.nc, self.st
        self.hT, self.b_hT = self.sb(st, "hT", [128, 8, L], BF16)
        self.attnT, self.b_attnT = self.sb(st, "attnT", [128, 4, L], BF16)
        self.zfT, self.b_zfT = self.sb(st, "zfT", [128, 8, L], BF16)
        self.edge, self.b_edge = self.sb(st, "edge", [128, 8, 2], BF16)
        fw = self.fw
        fw.op(fw.dve, lambda: nc.vector.memset(self.edge[:], 0.0), writes=[self.b_edge])
        self.wsc, self.b_wsc = self.sb(st, "wsc", [128, 4, 24], F32)
        for k in range(3):
            fw.dma(fw.q_sp, self.wsc[:, k, :], self.w_short[k, :].rearrange("(j p) -> p j", p=128), writes=[self.b_wsc],
                   allow_slow_non_contiguous=True)
        fw.dma(fw.q_sp, self.wsc[:, 3, :], self.b_short.rearrange("(j p) -> p j", p=128), writes=[self.b_wsc],
               allow_slow_non_contiguous=True)
        self.skc, self.b_skc = self.sb(st, "skc", [128, 2, 8], F32)
        for o in range(2):
            fw.dma(fw.q_sp, self.skc[:, o, :], self.f_skip[o, :].rearrange("(j p) -> p j", p=128), writes=[self.b_skc],
                   allow_slow_non_contiguous=True)

    def init_hyena(self, st):
        nc, fw = self.nc, self.fw
        TWO_PI = 2.0 * math.pi
        MAGIC = 12582912.0
        w1, b_w1 = self.sb(st, "fw1", [33, 64], F32)
        w2, b_w2 = self.sb(st, "fw2", [64, 64], F32)
        w3, b_w3 = self.sb(st, "fw3", [64, 64], F32)
        w4b, b_w4 = self.sb(st, "fw4", [64, 4096], BF16)
        frc, b_frc = self.sb(st, "frc", [64, 8], F32)
        nadt, b_nad = self.sb(st, "nadt", [128, D], F32)
        ktime, b_kt = self.sb(st, "ktime", [128, 32, 1024], BF16)
        sc, b_sc = self.sb(st, "fsc", [128, 3, 32], F32)
        fw.dma(fw.q_sp, w1[:], self.f_w1[:, :], writes=[b_w1])
        fw.dma(fw.q_sp, w2[:], self.f_w2[:, :], writes=[b_w2])
        fw.dma(fw.q_sp, w3[:], self.f_w3[:, :], writes=[b_w3])
        fw.dma(fw.q_pool, w4b[:], self.f_w4[:, :], writes=[b_w4])
        fw.dma(fw.q_sp, nadt[:], self.nadd.partition_broadcast(128), writes=[b_nad])
        col1 = lambda v: v.rearrange("(p o) -> p o", o=1)
        fw.dma(fw.q_sp, frc[:, 0:1], col1(self.f_freq), writes=[b_frc])
        for li, bb in enumerate((self.f_b1, self.f_b2, self.f_b3)):
            fw.dma(fw.q_sp, frc[:, 4 + li:5 + li], col1(bb), writes=[b_frc])
        for li in range(3):
            fw.op(fw.dve, lambda: nc.vector.tensor_scalar(out=frc[:, 1 + li:2 + li], in0=frc[:, 4 + li:5 + li], scalar1=frc[:, 0:1],
                                                          scalar2=None, op0=ALU.mult), reads=[b_frc], writes=[b_frc])
        Ws = [(w1, b_w1, 33), (w2, b_w2, 64), (w3, b_w3, 64)]
        zt_r = Ring(nc, st, "zt", 2, [33, 512], F32)
        a_r = Ring(nc, st, "fa", 3, [64, 512], F32)
        r_r = Ring(nc, st, "fr", 2, [64, 512], F32)
        h3_r = Ring(nc, st, "fh3", 2, [64, 512], BF16)
        dec_r = Ring(nc, st, "fdec", 2, [128, D], F32)
        t_r = Ring(nc, st, "ft", 3, [128, 512], F32)
        F_r = Ring(nc, st, "fF", 2, [128, 32, 128], BF16)
        ho_r = Ring(nc, st, "fho", 3, [128, 512], BF16)
        for si in range(self.nslots):
            so = SLOT_ORDERS[si]
            zsrc = self.zfTd[si] if si < NSLOT_SH else self.zfTc[si - NSLOT_SH]
            fw.dma(fw.q_sp, sc[:], self.fcolsd[si] if si < NSLOT_SH else self.fcolsc[si - NSLOT_SH], writes=[b_sc])
            for rg in range(8):
                zt, bzt = zt_r.next()
                fw.dma(fw.q_sp, zt[:], zsrc[:, rg * 512:(rg + 1) * 512], writes=[bzt])
                h, bh, K = zt, bzt, 33
                for li in range(3):
                    W, bW, K = Ws[li]
                    ps, bps = self.bank()
                    self.mm_group(ps[0:64, :], [(W[0:K, :], h[0:K, :])], reads=[bW, bh], writes=[bps])
                    a, ba = a_r.next()
                    fw.op(fw.dve, lambda: nc.vector.tensor_scalar(out=a[:], in0=ps[0:64, :], scalar1=frc[:, 0:1],
                                                                  scalar2=frc[:, 1 + li:2 + li], op0=ALU.mult, op1=ALU.add),
                          reads=[bps, b_frc], writes=[ba])
                    r, br = r_r.next()
                    fw.op(fw.dve, lambda: nc.vector.tensor_scalar(out=r[:], in0=a[:], scalar1=1.0 / TWO_PI, scalar2=MAGIC,
                                                                  op0=ALU.mult, op1=ALU.add), reads=[ba], writes=[br])
                    fw.op(fw.dve, lambda: nc.vector.tensor_scalar(out=r[:], in0=r[:], scalar1=MAGIC, scalar2=None,
                                                                  op0=ALU.subtract), reads=[br], writes=[br])
                    fw.op(fw.dve, lambda: nc.vector.scalar_tensor_tensor(out=a[:], in0=r[:], scalar=-TWO_PI, in1=a[:],
                                                                         op0=ALU.mult, op1=ALU.add), reads=[br, ba], writes=[ba])
                    if li < 2:
                        hn_, bhn_ = a_r.next()
                    else:
                        hn_, bhn_ = h3_r.next()
                    fw.op(fw.act, lambda: nc.scalar.activation(out=hn_[:], in_=a[:], func=ACTF.Sin), reads=[ba], writes=[bhn_])
                    h, bh = hn_, bhn_
                for ai in range(4):
                    aidx = rg * 4 + ai
                    dec, bdec = dec_r.next()
                    fw.op(fw.act, lambda: nc.scalar.activation(out=dec[:], in_=nadt[:], func=ACTF.Exp, scale=sc[:, 0, aidx:aidx + 1]),
                          reads=[b_nad, b_sc], writes=[bdec])
                    for o in (so,):
                        for hf in range(2):
                            psf, bpsf = self.bank()
                            c0 = o * 2048 + hf * 512
                            self.mm_group(psf[:], [(h[:, ai * 128:(ai + 1) * 128], w4b[:, c0:c0 + 512])], reads=[bh, b_w4], writes=[bpsf])
                            psb, bpsb = self.bank()
                            c1 = o * 2048 + 1024 + hf * 512
                            self.mm_group(psb[:], [(h[:, ai * 128:(ai + 1) * 128], w4b[:, c1:c1 + 512])], reads=[bh, b_w4], writes=[bpsb])
                            t, bt = t_r.next()
                            fw.op(fw.dve, lambda: nc.vector.tensor_scalar(out=t[:], in0=psf[:], scalar1=sc[:, 1, aidx:aidx + 1],
                                                                          scalar2=None, op0=ALU.mult), reads=[bpsf, b_sc], writes=[bt])
                            fw.op(fw.dve, lambda: nc.vector.scalar_tensor_tensor(out=t[:], in0=psb[:], scalar=sc[:, 2, aidx:aidx + 1], in1=t[:],
                                                                                 op0=ALU.mult, op1=ALU.add),
                                  reads=[bpsb, b_sc, bt], writes=[bt])
                            cg = hf
                            fw.op(fw.dve, lambda: nc.vector.tensor_tensor(out=ktime[:, aidx, cg * 512:(cg + 1) * 512], in0=t[:],
                                                                          in1=dec[:, hf * 512:(hf + 1) * 512], op=ALU.mult),
                                  reads=[bt, bdec], writes=[b_kt])
            for m in range(32):
                Ft, bF = F_r.next()
                fw.dma(fw.q_sp, Ft[:], self.Ftab[m], writes=[bF])
                for cg in range(2):
                    ps, bps = self.bank()
                    self.mm_group(ps[:], [(Ft[:, a, :], ktime[:, a, cg * 512:(cg + 1) * 512]) for a in range(32)],
                                  reads=[bF, b_kt], writes=[bps])
                    ho, bho = ho_r.next()
                    fw.op(fw.act, lambda: nc.scalar.copy(out=ho[:], in_=ps[:]), reads=[bps], writes=[bho])
                    fw.dma(fw.q_sp, self.Hs[si, cg, m], ho[:], reads=[bho], writes=[self.b_Hs])

    def init_attention(self):
        nc, fw, st = self.nc, self.fw, self.st
        self.ones, self.b_ones = self.sb(st, "onesb", [128, 128], BF16)
        fw.op(fw.dve, lambda: nc.vector.memset(self.ones[:], 1.0), writes=[self.b_ones])
        self.gqk, self.b_gqk = self.sb(st, "gqk", [128, 2], F32)
        fw.dma(fw.q_sp, self.gqk[:, 0:1], self.g_q.rearrange("(p o) -> p o", o=1), writes=[self.b_gqk])
        fw.dma(fw.q_sp, self.gqk[:, 1:2], self.g_k.rearrange("(p o) -> p o", o=1), writes=[self.b_gqk])
        fw.op(fw.dve, lambda: nc.vector.tensor_scalar(out=self.gqk[:, 1:2], in0=self.gqk[:, 1:2], scalar1=math.sqrt(128.0),
                                                      scalar2=None, op0=ALU.mult), reads=[self.b_gqk], writes=[self.b_gqk])
        self.kv, self.b_kv = self.sb(st, "kv", [128, 2, 192], F32)
        fw.dma(fw.q_sp, self.kv[:], self.kvd.rearrange("k p c -> p k c"), writes=[self.b_kv])
        self.biasd = nc.dram_tensor("biasd", [128, 24, 128], BF16)
        self.b_biasd = Buf("biasd")
        with ExitStack() as s2:
            self.biasT, self.b_biasT = self.sb(s2, "biasT0", [128, 24, 128], BF16)
            rb, b_rb = self.sb(s2, "rb", [33, 12], F32)
            es, b_es = self.sb(s2, "es", [33, 3, 384], F32)
            bv, b_bv = self.sb(s2, "bv", [12, 3, 384], F32)
            stg = Ring(nc, s2, "bstg", 2, [128, 128], F32)
            fw.op(fw.dve, lambda: nc.vector.memset(rb[32:33, :], MASKV), writes=[b_rb])
            fw.dma(fw.q_sp, rb[0:32, :], self.rel_bias[:, :], writes=[b_rb])
            fw.dma(fw.q_sp, es[:], self.Esel.rearrange("g b u -> b g u"), writes=[b_es])
            b_bsc = Buf("bsc")
            for g in range(3):
                ps, bps = self.bank()
                self.mm_group(ps[0:12, 0:384], [(rb[:, :], es[:, g, :])], reads=[b_rb, b_es], writes=[bps])
                fw.op(fw.dve, lambda: nc.vector.tensor_copy(out=bv[:, g, :], in_=ps[0:12, 0:384]), reads=[bps], writes=[b_bv])
            fw.dma(fw.q_sp, self.bsc.rearrange("g h u -> h g u"), bv[:], reads=[b_bv], writes=[b_bsc])
            for g in range(3):
                for i in range(4):
                    for j in range(2):
                        h = 4 * g + i
                        t, bt = stg.next()
                        src = bass.AP(self.bsc.tensor, (g * 12 + h) * 384 + 127 + 128 * j, [[1, 128], [-1, 128]])
                        fw.dma(fw.q_sp, t[:], src, reads=[b_bsc], writes=[bt], allow_slow_non_contiguous=True)
                        fw.op(fw.dve, lambda: nc.vector.tensor_copy(out=self.biasT[:, (g * 4 + i) * 2 + j, :], in_=t[:]),
                              reads=[bt], writes=[self.b_biasT])
            fw.dma(fw.q_sp, self.biasd[:, :, :], self.biasT[:], reads=[self.b_biasT], writes=[self.b_biasd])
            fw.barrier()

    def load_gain(self, st, which):
        t, b = self.sb(st, "gain_" + which, [128, D], F32)
        src = self.g_mix if which == "mix" else self.g_mlp
        self.fw.dma(self.fw.q_sp, t[:], src.partition_broadcast(128), writes=[b])
        if which == "mix":
            self.gmix, self.b_gmix = t, b
        else:
            self.gmlp, self.b_gmlp = t, b

    def bank(self):
        it = self.pb[self.pbi % len(self.pb)]
        self.pbi += 1
        return it

    def dbg_dump(self, name, tile_ap, buf, shape, dt=F32):
        if name not in self.dbg:
            return
        o = self.nc.dram_tensor("dbg_" + name, list(shape), dt, kind="ExternalOutput").ap()
        self.dbg_out[name] = o
        self.fw.dma(self.fw.q_sp, o, tile_ap, reads=[buf])

    def load_w(self, tile, buf, src2d, nk):
        fw = self.fw
        fw.dma(fw.q_pool, tile, src2d.rearrange("(k p) c -> p k c", p=128), writes=[buf])

    def mm_group(self, out_ap, pairs, reads, writes, start=True, stop=True):
        nc = self.nc
        n = len(pairs)

        def emit():
            inst = None
            for i, (l, r) in enumerate(pairs):
                inst = nc.tensor.matmul(out_ap, lhsT=l, rhs=r, start=(start and i == 0), stop=(stop and i == n - 1))
            return inst
        return self.fw.op(self.fw.pe, emit, reads=reads, writes=writes)

    def rmsnorm_T(self, st_ring, x_ap, bx, g_tile, bg, dst_ap, bdst):
        nc, fw = self.nc, self.fw
        sq, bsq = st_ring["sq"].next()
        col, bcol = st_ring["col"].next()
        hn, bhn = st_ring["hn"].next()
        fw.op(fw.act, lambda: nc.scalar.activation(out=sq[:], in_=x_ap, func=ACTF.Square, accum_out=col[:, 0:1]),
              reads=[bx], writes=[bsq, bcol])
        fw.op(fw.act, lambda: nc.scalar.activation(out=col[:, 1:2], in_=col[:, 0:1], func=ACTF.Sqrt, scale=1.0 / D, bias=EPS),
              reads=[bcol], writes=[bcol])
        fw.op(fw.dve, lambda: nc.vector.reciprocal(out=col[:, 2:3], in_=col[:, 1:2]), reads=[bcol], writes=[bcol])
        fw.op(fw.dve, lambda: nc.vector.scalar_tensor_tensor(out=hn[:], in0=x_ap, scalar=col[:, 2:3], in1=g_tile[:],
                                                             op0=ALU.mult, op1=ALU.mult),
              reads=[bx, bcol, bg], writes=[bhn])

        def tr():
            inst = None
            for k in range(8):
                inst = nc.tensor.transpose(self.ptr[:, k * 128:(k + 1) * 128], hn[:, k * 128:(k + 1) * 128], self.ident[:])
            return inst
        fw.op(fw.pe, tr, reads=[bhn, self.b_ident], writes=[self.b_ptr])
        fw.op(fw.act, lambda: nc.scalar.copy(out=dst_ap, in_=self.ptr[:].rearrange("p (k t) -> p k t", k=8)),
              reads=[self.b_ptr], writes=[bdst])

    def stage_A(self, job, rings):
        fw = self.fw
        for t in range(L // 128):
            xt, bx = rings["x"].next()
            fw.dma(fw.q_sp, xt[:], self.xm[job.idx, t * 128:(t + 1) * 128, :], writes=[bx])
            self.rmsnorm_T(rings, xt[:], bx, self.gmix, self.b_gmix,
                           self.hT[:, :, t * 128:(t + 1) * 128], self.b_hT)

    def qk_norm(self, ps, bps, gcol, dst, bdst, rb):
        nc, fw = self.nc, self.fw
        sq, bsq = rb["sq5"].next()
        rs, brs = rb["rs5"].next()
        fw.op(fw.act, lambda: nc.scalar.activation(out=sq[:], in_=ps[:], func=ACTF.Square), reads=[bps], writes=[bsq])
        ps2, bps2 = self.bank()
        self.mm_group(ps2[:], [(self.ones[:], sq[:])], reads=[self.b_ones, bsq], writes=[bps2])
        fw.op(fw.act, lambda: nc.scalar.activation(out=rs[:], in_=ps2[:], func=ACTF.Sqrt, scale=1.0, bias=128.0 * EPS),
              reads=[bps2], writes=[brs])
        fw.op(fw.dve, lambda: nc.vector.reciprocal(out=rs[:], in_=rs[:]), reads=[brs], writes=[brs])
        fw.op(fw.dve, lambda: nc.vector.scalar_tensor_tensor(out=dst, in0=ps[:], scalar=gcol, in1=rs[:],
                                                             op0=ALU.mult, op1=ALU.mult),
              reads=[bps, brs, self.b_gqk], writes=[bdst])

    def stage_B(self, job, rings, st):
        nc, fw = self.nc, self.fw
        kind = 1 if job.sample else 0
        hTh, b_hTh = self.sb(st, "hTh", [128, 8, 2048], BF16)
        self.biasT, self.b_biasT = self.sb(st, "biasT", [128, 24, 128], BF16)
        fw.dma(fw.q_sp, self.biasT[:], self.biasd[:, :, :], reads=[self.b_biasd], writes=[self.b_biasT])
        if job.sample:
            for t in range(16):
                xt, bx = rings["x"].next()
                fw.dma(fw.q_sp, xt[:], self.xh[t * 128:(t + 1) * 128, :], writes=[bx])
                self.rmsnorm_T(rings, xt[:], bx, self.gmix, self.b_gmix, hTh[:, :, t * 128:(t + 1) * 128], b_hTh)
            fw.op(fw.dve, lambda: nc.vector.tensor_copy(out=self.edge[:, :, 0:1], in_=hTh[:, :, 1023:1024]), reads=[b_hTh], writes=[self.b_edge])
            fw.op(fw.dve, lambda: nc.vector.tensor_copy(out=self.edge[:, :, 1:2], in_=hTh[:, :, 1024:1025]), reads=[b_hTh], writes=[self.b_edge])

        def hsrc(eg, k):
            if eg < 2:
                return hTh[:, k, eg * 512:(eg + 1) * 512], b_hTh
            if eg < 6:
                return self.hT[:, k, (eg - 2) * 512:(eg - 1) * 512], self.b_hT
            return hTh[:, k, 1024 + (eg - 6) * 512:1024 + (eg - 5) * 512], b_hTh

        wr = Ring(nc, st, "wqkv", 6, [128, 8, 128], BF16)
        qn_r = Ring(nc, st, "qn", 1, [128, L], BF16)
        kn_r = Ring(nc, st, "kn", 1, [128, 2 * L], BF16)
        vT_r = Ring(nc, st, "vT", 1, [128, 2 * L], BF16)
        nd, b_nd = self.sb(st, "nd", [128, 2, L], F32)
        pt_r = Ring(nc, st, "pt", 3, [128, 256], BF16)
        vt_r = Ring(nc, st, "vt", 4, [128, 128], BF16)
        rb = {"sq5": Ring(nc, st, "sq5", 2, [128, 512], BF16), "rs5": Ring(nc, st, "rs5", 2, [128, 512], F32)}
        for i in range(4):
            for grp in range(3):
                d = GROUPS[grp][1]
                h = 4 * grp + i
                M, Me = L // d, 2 * L // d
                ws = []
                for c0 in (C_Q, C_K, C_V):
                    wt, bw = wr.next()
                    self.load_w(wt[:], bw, self.w_in[:, c0 + h * 128:c0 + (h + 1) * 128], 8)
                    ws.append((wt, bw))
                qn, bqn = qn_r.next()
                kn, bkn = kn_r.next()
                vT, bvT = vT_r.next()
                if not job.sample:
                    fw.op(fw.dve, lambda: nc.vector.memset(kn[:], 0.0), writes=[bkn])
                    fw.op(fw.dve, lambda: nc.vector.memset(vT[:], 0.0), writes=[bvT])
                for tg in range(4):
                    ps, bps = self.bank()
                    self.mm_group(ps[:], [(ws[0][0][:, k, :], self.hT[:, k, tg * 512:(tg + 1) * 512]) for k in range(8)],
                                  reads=[ws[0][1], self.b_hT], writes=[bps])
                    if d == 1:
                        dst = qn[:, tg * 512:(tg + 1) * 512]
                        self.qk_norm(ps, bps, self.gqk[:, 0:1], dst, bqn, rb)
                    else:
                        self.qk_norm_perm(ps, bps, self.gqk[:, 0:1], qn, bqn, rb, d, M, tg * 512 // d)
                if job.sample:
                    egs = list(range(8)) if d == 16 else list(range(1, 7))
                else:
                    egs = [2, 3, 4, 5]
                for eg in egs:
                    ps, bps = self.bank()
                    prs, rd = [], [ws[1][1]]
                    for k in range(8):
                        a, ba = hsrc(eg, k)
                        prs.append((ws[1][0][:, k, :], a))
                    self.mm_group(ps[:], prs, reads=[ws[1][1], ba], writes=[bps])
                    if d == 1:
                        self.qk_norm(ps, bps, self.gqk[:, 1:2], kn[:, eg * 512:(eg + 1) * 512], bkn, rb)
                    else:
                        self.qk_norm_perm(ps, bps, self.gqk[:, 1:2], kn, bkn, rb, d, Me, eg * 512 // d)
                    ps, bps = self.bank()
                    prs = []
                    for k in range(8):
                        a, ba = hsrc(eg, k)
                        prs.append((ws[2][0][:, k, :], a))
                    self.mm_group(ps[:], prs, reads=[ws[2][1], ba], writes=[bps])
                    if d == 1:
                        dstv = vT[:, eg * 512:(eg + 1) * 512]
                        srcv = ps[:]
                    else:
                        dstv = vT[:].rearrange("p (r m) -> p r m", r=d)[:, :, eg * 512 // d:(eg + 1) * 512 // d]
                        srcv = ps[:].rearrange("p (m r) -> p r m", r=d)
                    fw.op(fw.act, lambda: nc.scalar.copy(out=dstv, in_=srcv), reads=[bps], writes=[bvT])
                for r in range(d):
                    vcache = {}
                    for mt in range(M // 128):
                        m0 = mt * 128
                        kb = 1024 // d + m0 - 64
                        kcs = [r * Me + kb + 128 * j for j in range(2)]
                        vts = []
                        for j in range(2):
                            if kcs[j] in vcache:
                                vts.append(vcache[kcs[j]])
                                continue
                            fw.op(fw.pe, lambda: nc.tensor.transpose(self.ptr[:, 0:128], vT[:, kcs[j]:kcs[j] + 128], self.ident[:]),
                                  reads=[bvT, self.b_ident], writes=[self.b_ptr])
                            vt, bvt = vt_r.next()
                            fw.op(fw.act, lambda: nc.scalar.copy(out=vt[:], in_=self.ptr[:, 0:128]), reads=[self.b_ptr], writes=[bvt])
                            vcache[kcs[j]] = (vt, bvt)
                            vts.append((vt, bvt))
                        vcache = {kcs[1]: vts[1]}
                        ps, bps = self.bank()
                        qsl = qn[:, r * M + m0:r * M + m0 + 128]

                        def sc():
                            inst = None
                            for j in range(2):
                                nc.tensor.matmul(ps[:, j * 128:(j + 1) * 128], lhsT=kn[:, kcs[j]:kcs[j] + 128], rhs=qsl,
                                                 start=True, stop=False)
                                inst = nc.tensor.matmul(ps[:, j * 128:(j + 1) * 128], lhsT=self.ident[:],
                                                        rhs=self.biasT[:, (grp * 4 + i) * 2 + j, :], start=False, stop=True)
                            return inst
                        fw.op(fw.pe, sc, reads=[bkn, bqn, self.b_ident, self.b_biasT], writes=[bps])
                        pt, bpt = pt_r.next()
                        for j in range(2):
                            col = grp * 64 + kcs[j] // 64
                            fw.op(fw.act, lambda: nc.scalar.activation(out=pt[:, j * 128:(j + 1) * 128], in_=ps[:, j * 128:(j + 1) * 128],
                                                                       func=ACTF.Exp, bias=self.kv[:, kind, col:col + 1], scale=1.0),
                                  reads=[bps, self.b_kv], writes=[bpt])
                        ps2, bps2 = self.bank()

                        def pv():
                            inst = None
                            for j in range(2):
                                nc.tensor.matmul(ps2[:, 0:128], lhsT=vts[j][0][:], rhs=pt[:, j * 128:(j + 1) * 128],
                                                 start=(j == 0), stop=(j == 1))
                            for j in range(2):
                                inst = nc.tensor.matmul(ps2[:, 128:256], lhsT=self.ones[:], rhs=pt[:, j * 128:(j + 1) * 128],
                                                        start=(j == 0), stop=(j == 1))
                            return inst
                        fw.op(fw.pe, pv, reads=[vts[0][1], vts[1][1], bpt, self.b_ones], writes=[bps2])
                        ndv = nd[:].rearrange("p a (m r) -> p a m r", r=d)[:, :, m0:m0 + 128, r]
                        src = ps2[:, 0:256].rearrange("p (a m) -> p a m", a=2)
                        if grp == 0:
                            fw.op(fw.dve, lambda: nc.vector.tensor_copy(out=ndv, in_=src), reads=[bps2], writes=[b_nd])
                        else:
                            fw.op(fw.dve, lambda: nc.vector.tensor_tensor(out=ndv, in0=ndv, in1=src, op=ALU.add),
                                  reads=[bps2, b_nd], writes=[b_nd])
            fw.op(fw.dve, lambda: nc.vector.reciprocal(out=nd[:, 1, :], in_=nd[:, 1, :]), reads=[b_nd], writes=[b_nd])
            fw.op(fw.dve, lambda: nc.vector.tensor_tensor(out=self.attnT[:, i, :], in0=nd[:, 0, :], in1=nd[:, 1, :], op=ALU.mult),
                  reads=[b_nd], writes=[self.b_attnT])

    def qk_norm_perm(self, ps, bps, gcol, dst_tile, bdst, rb, d, Mtot, mstart):
        n = 512 // d
        dst = dst_tile[:].rearrange("p (r m) -> p r m", r=d)[:, :, mstart:mstart + n]
        nc, fw = self.nc, self.fw
        sq, bsq = rb["sq5"].next()
        rs, brs = rb["rs5"].next()
        fw.op(fw.act, lambda: nc.scalar.activation(out=sq[:], in_=ps[:], func=ACTF.Square), reads=[bps], writes=[bsq])
        ps2, bps2 = self.bank()
        self.mm_group(ps2[:], [(self.ones[:], sq[:])], reads=[self.b_ones, bsq], writes=[bps2])
        fw.op(fw.act, lambda: nc.scalar.activation(out=rs[:], in_=ps2[:], func=ACTF.Sqrt, scale=1.0, bias=128.0 * EPS),
              reads=[bps2], writes=[brs])
        fw.op(fw.dve, lambda: nc.vector.reciprocal(out=rs[:], in_=rs[:]), reads=[brs], writes=[brs])
        fw.op(fw.dve, lambda: nc.vector.scalar_tensor_tensor(out=dst, in0=ps[:].rearrange("p (m r) -> p r m", r=d), scalar=gcol,
                                                             in1=rs[:].rearrange("p (m r) -> p r m", r=d),
                                                             op0=ALU.mult, op1=ALU.mult),
              reads=[bps, brs, self.b_gqk], writes=[bdst])

    class _NS:
        pass

    def c_alloc(self, st):
        nc = self.nc
        c = KB._NS()
        c.u, c.b_u = self.sb(st, "ubuf", [128, L + 2], BF16)
        c.zTf, c.b_zTf = self.sb(st, "zTf", [128, 4, L], BF16)
        c.x1T, c.b_x1T = self.sb(st, "x1T", [128, 4, L], BF16)
        c.ztok, c.b_ztok = self.sb(st, "ztok", [128, 16, 512], BF16)
        c.w_r = Ring(nc, st, "wh", 2, [128, 8, 128], BF16)
        c.F_r = Ring(nc, st, "cF", 2, [128, 16, 128], BF16)
        c.H_r = Ring(nc, st, "cH", 4, [128, 512], BF16)
        c.t_r = Ring(nc, st, "ct", 4, [128, 512], F32)
        return c

    def c_alloc_inv(self, c, st):
        c.Y, c.b_Y = self.sb(st, "Yspec", [128, 32, 512], BF16)
        c.G_r = Ring(self.nc, st, "cG", 2, [128, 2, 512], BF16)

    def c_project(self, c, kinds, hf):
        nc, fw = self.nc, self.fw
        u, b_u, t_r = c.u, c.b_u, c.t_r
        for kind in kinds:
            for cc in range(4):
                jcol = kind * 8 + hf * 4 + cc
                col0 = C_HY + jcol * 128
                wt, bw = c.w_r.next()
                self.load_w(wt[:], bw, self.w_in[:, col0:col0 + 128], 8)
                for tg in range(4):
                    ps, bps = self.bank()
                    self.mm_group(ps[:], [(wt[:, k, :], self.hT[:, k, tg * 512:(tg + 1) * 512]) for k in range(8)],
                                  reads=[bw, self.b_hT], writes=[bps])
                    fw.op(fw.act, lambda: nc.scalar.copy(out=u[:, 1 + tg * 512:1 + (tg + 1) * 512], in_=ps[:]),
                          reads=[bps], writes=[b_u])
                ps, bps = self.bank()
                self.mm_group(ps[:, 0:2], [(wt[:, k, :], self.edge[:, k, 0:2]) for k in range(8)],
                              reads=[bw, self.b_edge], writes=[bps])
                fw.op(fw.act, lambda: nc.scalar.copy(out=u[:, 0:1], in_=ps[:, 0:1]), reads=[bps], writes=[b_u])
                fw.op(fw.act, lambda: nc.scalar.copy(out=u[:, L + 1:L + 2], in_=ps[:, 1:2]), reads=[bps], writes=[b_u])
                if kind == 0:
                    dst, bdst = c.zTf[:, cc, :], c.b_zTf
                elif kind == 1:
                    dst, bdst = c.x1T[:, cc, :], c.b_x1T
                else:
                    dst, bdst = self.zfT[:, hf * 4 + cc, :], self.b_zfT
                for tg in range(4):
                    t, bt = t_r.next()
                    o = tg * 512
                    fw.op(fw.dve, lambda: nc.vector.tensor_scalar(out=t[:], in0=u[:, 1 + o:513 + o], scalar1=self.wsc[:, 1, jcol:jcol + 1],
                                                                  scalar2=self.wsc[:, 3, jcol:jcol + 1], op0=ALU.mult, op1=ALU.add),
                          reads=[b_u, self.b_wsc], writes=[bt])
                    fw.op(fw.dve, lambda: nc.vector.scalar_tensor_tensor(out=t[:], in0=u[:, o:512 + o], scalar=self.wsc[:, 0, jcol:jcol + 1],
                                                                         in1=t[:], op0=ALU.mult, op1=ALU.add),
                          reads=[b_u, self.b_wsc, bt], writes=[bt])
                    fw.op(fw.dve, lambda: nc.vector.scalar_tensor_tensor(out=dst[:, o:o + 512], in0=u[:, 2 + o:514 + o],
                                                                         scalar=self.wsc[:, 2, jcol:jcol + 1], in1=t[:],
                                                                         op0=ALU.mult, op1=ALU.add),
                          reads=[b_u, self.b_wsc, bt], writes=[bdst])

    def c_totok(self, c):
        nc, fw = self.nc, self.fw
        for tt in range(16):
            def tr():
                inst = None
                for cc in range(4):
                    inst = nc.tensor.transpose(self.ptr[:, cc * 128:(cc + 1) * 128], c.zTf[:, cc, tt * 128:(tt + 1) * 128], self.ident[:])
                return inst
            fw.op(fw.pe, tr, reads=[c.b_zTf, self.b_ident], writes=[self.b_ptr])
            fw.op(fw.act, lambda: nc.scalar.copy(out=c.ztok[:, tt, :], in_=self.ptr[:, 0:512]), reads=[self.b_ptr], writes=[c.b_ztok])

    def c_fwd(self, c, sink):
        fw = self.fw
        for m in range(16):
            Fre, bFre = c.F_r.next()
            fw.dma(fw.q_sp, Fre[:], self.Ftab[m, :, 0:16, :], writes=[bFre])
            psr, bpsr = self.bank()
            self.mm_group(psr[:], [(Fre[:, a, :], c.ztok[:, a, :]) for a in range(16)], reads=[bFre, c.b_ztok], writes=[bpsr])
            Fim, bFim = c.F_r.next()
            fw.dma(fw.q_sp, Fim[:], self.Ftab[16 + m, :, 0:16, :], writes=[bFim])
            psi, bpsi = self.bank()
            self.mm_group(psi[:], [(Fim[:, a, :], c.ztok[:, a, :]) for a in range(16)], reads=[bFim, c.b_ztok], writes=[bpsi])
            sink(m, psr, bpsr, psi, bpsi)

    def c_sink_mul(self, c, slot, hf):
        nc, fw = self.nc, self.fw
        Y, b_Y = c.Y, c.b_Y
        TT = nc.vector.tensor_tensor

        def sink(m, psr, bpsr, psi, bpsi):
            Hre, bHre = c.H_r.next()
            fw.dma(fw.q_sp, Hre[:], self.Hs[slot, hf, m], reads=[self.b_Hs], writes=[bHre])
            Him, bHim = c.H_r.next()
            fw.dma(fw.q_sp, Him[:], self.Hs[slot, hf, 16 + m], reads=[self.b_Hs], writes=[bHim])
            t1, bt1 = c.t_r.next()
            t2, bt2 = c.t_r.next()
            fw.op(fw.dve, lambda: TT(out=t1[:], in0=psr[:], in1=Hre[:], op=ALU.mult), reads=[bpsr, bHre], writes=[bt1])
            fw.op(fw.dve, lambda: TT(out=t2[:], in0=psi[:], in1=Him[:], op=ALU.mult), reads=[bpsi, bHim], writes=[bt2])
            fw.op(fw.dve, lambda: TT(out=Y[:, m, :], in0=t1[:], in1=t2[:], op=ALU.subtract), reads=[bt1, bt2], writes=[b_Y])
            t3, bt3 = c.t_r.next()
            t4, bt4 = c.t_r.next()
            fw.op(fw.dve, lambda: TT(out=t3[:], in0=psr[:], in1=Him[:], op=ALU.mult), reads=[bpsr, bHim], writes=[bt3])
            fw.op(fw.dve, lambda: TT(out=t4[:], in0=psi[:], in1=Hre[:], op=ALU.mult), reads=[bpsi, bHre], writes=[bt4])
            fw.op(fw.dve, lambda: TT(out=Y[:, 16 + m, :], in0=t3[:], in1=t4[:], op=ALU.add), reads=[bt3, bt4], writes=[b_Y])
            if m == 0:
                fw.op(fw.dve, lambda: TT(out=Y[0:1, 0, :], in0=psr[0:1, :], in1=Hre[0:1, :], op=ALU.mult),
                      reads=[bpsr, bHre], writes=[b_Y])
                fw.op(fw.dve, lambda: TT(out=Y[0:1, 16, :], in0=psi[0:1, :], in1=Him[0:1, :], op=ALU.mult),
                      reads=[bpsi, bHim], writes=[b_Y])
        return sink

    def c_sink_store(self, c, xst_r, dst, bdst):
        nc, fw = self.nc, self.fw

        def sink(m, psr, bpsr, psi, bpsi):
            for (pp, bpp, row0) in ((psr, bpsr, m * 128), (psi, bpsi, (16 + m) * 128)):
                xs, bxs = xst_r.next()
                fw.op(fw.act, lambda: nc.scalar.copy(out=xs[:], in_=pp[:]), reads=[bpp], writes=[bxs])
                fw.dma(fw.q_sp, dst[row0:row0 + 128, :], xs[:], reads=[bxs], writes=[bdst])
        return sink

    def c_mac(self, c, s3, slots, xsd, b_xsd, hf):
        nc, fw = self.nc, self.fw
        TT = nc.vector.tensor_tensor
        PT_ = nc.gpsimd.tensor_tensor
        Y, b_Y = c.Y, c.b_Y
        accr, b_accr, acci, b_acci, fix, b_fix = s3.accr, s3.b_accr, s3.acci, s3.b_acci, s3.fix, s3.b_fix
        for m in range(16):
            for J in range(8):
                XJr, bXr = s3.XJ_r.next()
                fw.dma(fw.q_sp, XJr[:], xsd[J, m * 128:(m + 1) * 128, :], reads=[b_xsd], writes=[bXr])
                XJi, bXi = s3.XJ_r.next()
                fw.dma(fw.q_sp, XJi[:], xsd[J, (16 + m) * 128:(17 + m) * 128, :], reads=[b_xsd], writes=[bXi])
                Hre, bHre = c.H_r.next()
                fw.dma(fw.q_sp, Hre[:], self.Hs[slots[J], hf, m], reads=[self.b_Hs], writes=[bHre])
                Him, bHim = c.H_r.next()
                fw.dma(fw.q_sp, Him[:], self.Hs[slots[J], hf, 16 + m], reads=[self.b_Hs], writes=[bHim])
                ts = [c.t_r.next() for _ in range(4)]
                fw.op(fw.pool, lambda: PT_(out=ts[0][0][:], in0=XJr[:], in1=Hre[:], op=ALU.mult), reads=[bXr, bHre], writes=[ts[0][1]])
                fw.op(fw.pool, lambda: PT_(out=ts[1][0][:], in0=XJi[:], in1=Him[:], op=ALU.mult), reads=[bXi, bHim], writes=[ts[1][1]])
                fw.op(fw.pool, lambda: PT_(out=ts[2][0][:], in0=XJr[:], in1=Him[:], op=ALU.mult), reads=[bXr, bHim], writes=[ts[2][1]])
                fw.op(fw.pool, lambda: PT_(out=ts[3][0][:], in0=XJi[:], in1=Hre[:], op=ALU.mult), reads=[bXi, bHre], writes=[ts[3][1]])
                if J == 0:
                    fw.op(fw.dve, lambda: TT(out=accr[:], in0=ts[0][0][:], in1=ts[1][0][:], op=ALU.subtract),
                          reads=[ts[0][1], ts[1][1]], writes=[b_accr])
                    fw.op(fw.dve, lambda: TT(out=acci[:], in0=ts[2][0][:], in1=ts[3][0][:], op=ALU.add),
                          reads=[ts[2][1], ts[3][1]], writes=[b_acci])
                else:
                    fw.op(fw.dve, lambda: TT(out=accr[:], in0=accr[:], in1=ts[0][0][:], op=ALU.add), reads=[ts[0][1], b_accr], writes=[b_accr])
                    fw.op(fw.dve, lambda: TT(out=accr[:], in0=accr[:], in1=ts[1][0][:], op=ALU.subtract), reads=[ts[1][1], b_accr], writes=[b_accr])
                    fw.op(fw.dve, lambda: TT(out=acci[:], in0=acci[:], in1=ts[2][0][:], op=ALU.add), reads=[ts[2][1], b_acci], writes=[b_acci])
                    fw.op(fw.dve, lambda: TT(out=acci[:], in0=acci[:], in1=ts[3][0][:], op=ALU.add), reads=[ts[3][1], b_acci], writes=[b_acci])
                if m == 0:
                    if J == 0:
                        fw.op(fw.dve, lambda: nc.vector.tensor_copy(out=fix[:, 0, :], in_=ts[0][0][0:1, :]), reads=[ts[0][1]], writes=[b_fix])
                        fw.op(fw.dve, lambda: nc.vector.tensor_copy(out=fix[:, 1, :], in_=ts[1][0][0:1, :]), reads=[ts[1][1]], writes=[b_fix])
                    else:
                        fw.op(fw.dve, lambda: TT(out=fix[:, 0, :], in0=fix[:, 0, :], in1=ts[0][0][0:1, :], op=ALU.add), reads=[ts[0][1], b_fix], writes=[b_fix])
                        fw.op(fw.dve, lambda: TT(out=fix[:, 1, :], in0=fix[:, 1, :], in1=ts[1][0][0:1, :], op=ALU.add), reads=[ts[1][1], b_fix], writes=[b_fix])
            fw.op(fw.act, lambda: nc.scalar.copy(out=Y[:, m, :], in_=accr[:]), reads=[b_accr], writes=[b_Y])
            fw.op(fw.act, lambda: nc.scalar.copy(out=Y[:, 16 + m, :], in_=acci[:]), reads=[b_acci], writes=[b_Y])
            if m == 0:
                fw.op(fw.act, lambda: nc.scalar.copy(out=Y[0:1, 0, :], in_=fix[:, 0, :]), reads=[b_fix], writes=[b_Y])
                fw.op(fw.act, lambda: nc.scalar.copy(out=Y[0:1, 16, :], in_=fix[:, 1, :]), reads=[b_fix], writes=[b_Y])

    def c_inverse(self, c, order, hf):
        nc, fw = self.nc, self.fw
        Y, b_Y = c.Y, c.b_Y
        for tg in range(4):
            banks = [self.bank() for _ in range(4)]
            tsl = slice(tg * 512, (tg + 1) * 512)
            for fp in range(16):
                Gp, bG = c.G_r.next()
                fw.dma(fw.q_sp, Gp[:], self.Gtab[tg, fp * 2:(fp + 1) * 2].rearrange("f p n -> p f n"), writes=[bG])
                for fi in range(2):
                    f = fp * 2 + fi
                    for cc in range(4):
                        self.mm_group(banks[cc][0][:], [(Y[:, f, cc * 128:(cc + 1) * 128], Gp[:, fi, :])],
                                      reads=[b_Y, bG], writes=[banks[cc][1]], start=(f == 0), stop=(f == 31))
            for cc in range(4):
                ps, bps = banks[cc]
                t, bt = c.t_r.next()
                jc = hf * 4 + cc
                fw.op(fw.dve, lambda: nc.vector.scalar_tensor_tensor(out=t[:], in0=c.zTf[:, cc, tsl], scalar=self.skc[:, order, jc:jc + 1],
                                                                     in1=ps[:], op0=ALU.mult, op1=ALU.add),
                      reads=[c.b_zTf, self.b_skc, bps], writes=[bt])
                if order == 0:
                    fw.op(fw.dve, lambda: nc.vector.tensor_tensor(out=c.zTf[:, cc, tsl], in0=t[:], in1=c.x1T[:, cc, tsl], op=ALU.mult),
                          reads=[bt, c.b_x1T], writes=[c.b_zTf])
                else:
                    fw.op(fw.dve, lambda: nc.vector.tensor_tensor(out=self.zfT[:, jc, tsl], in0=t[:], in1=self.zfT[:, jc, tsl], op=ALU.mult),
                          reads=[bt, self.b_zfT], writes=[self.b_zfT])

    def stage_C(self, job, st):
        c = self.c_alloc(st)
        self.c_alloc_inv(c, st)
        for hf in range(2):
            self.c_project(c, (0, 1, 2), hf)
            for order in range(2):
                self.c_totok(c)
                self.c_fwd(c, self.c_sink_mul(c, order, hf))
                self.c_inverse(c, order, hf)

    def stage_C_sample(self, job, st):
        nc, fw = self.nc, self.fw
        c = self.c_alloc(st)
        oh, b_oh = self.sb(st, "onehot", [128, 8], F32)
        fw.dma(fw.q_sp, oh[:], self.onehotd[:, :], writes=[b_oh])
        for hf in range(2):
            self.c_project(c, (2,), hf)
        for hf in range(2):
            with ExitStack() as s1:
                rings = {
                    "x": Ring(nc, s1, "xr", 2, [128, D], F32),
                    "sq": Ring(nc, s1, "sqr", 2, [128, D], BF16),
                    "col": Ring(nc, s1, "colr", 4, [128, 4], F32),
                    "hn": Ring(nc, s1, "hnr", 2, [128, D], BF16),
                }
                self.load_gain(s1, "mix")
                xst_r = Ring(nc, s1, "xst", 2, [128, 512], BF16)
                etmp, b_etmp = self.sb(s1, "etmp", [128, 8, 128], BF16)
                for K in range(8):
                    for t in range(16):
                        xt, bx = rings["x"].next()
                        fw.dma(fw.q_sp, xt[:], self.xfull[K, t * 128:(t + 1) * 128, :], writes=[bx])
                        self.rmsnorm_T(rings, xt[:], bx, self.gmix, self.b_gmix, self.hT[:, :, t * 128:(t + 1) * 128], self.b_hT)
                    xt, bx = rings["x"].next()
                    fw.dma(fw.q_sp, xt[:], self.xedge[K], writes=[bx])
                    self.rmsnorm_T(rings, xt[:], bx, self.gmix, self.b_gmix, etmp[:], b_etmp)
                    fw.op(fw.dve, lambda: nc.vector.tensor_copy(out=self.edge[:], in_=etmp[:, :, 0:2]), reads=[b_etmp], writes=[self.b_edge])
                    self.c_project(c, (0, 1), hf)
                    fw.dma(fw.q_sp, self.zs[K], c.zTf[:], reads=[c.b_zTf], writes=[self.b_zs])
                    fw.dma(fw.q_sp, self.x1s[K], c.x1T[:], reads=[c.b_x1T], writes=[self.b_x1s])
                    self.c_totok(c)
                    self.c_fwd(c, self.c_sink_store(c, xst_r, self.xs0[K], self.b_xs0))
                fw.barrier()
            with ExitStack() as s3_:
                self.c_alloc_inv(c, s3_)
                s3 = KB._NS()
                s3.XJ_r = Ring(nc, s3_, "XJ", 4, [128, 512], BF16)
                s3.accr, s3.b_accr = self.sb(s3_, "accr", [128, 512], F32)
                s3.acci, s3.b_acci = self.sb(s3_, "acci", [128, 512], F32)
                s3.fix, s3.b_fix = self.sb(s3_, "fixr", [1, 2, 512], F32)
                xst_r = Ring(nc, s3_, "xst3", 1, [128, 512], BF16)
                for Jb in range(8):
                    self.c_mac(c, s3, [2 + (Jb - K + 7) for K in range(8)], self.xs0, self.b_xs0, hf)
                    fw.dma(fw.q_sp, c.zTf[:], self.zs[Jb], reads=[self.b_zs], writes=[c.b_zTf])
                    fw.dma(fw.q_sp, c.x1T[:], self.x1s[Jb], reads=[self.b_x1s], writes=[c.b_x1T])
                    self.c_inverse(c, 0, hf)
                    fw.dma(fw.q_sp, self.z1s[Jb], c.zTf[:], reads=[c.b_zTf], writes=[self.b_z1s])
                    self.c_totok(c)
                    self.c_fwd(c, self.c_sink_store(c, xst_r, self.xs1[Jb], self.b_xs1))
                self.c_mac(c, s3, [17 + J for J in range(8)], self.xs1, self.b_xs1, hf)
                fw.op(fw.dve, lambda: nc.vector.memset(c.zTf[:], 0.0), writes=[c.b_zTf])
                for J in range(8):
                    fw.dma(fw.q_sp, c.x1T[:], self.z1s[J], reads=[self.b_z1s], writes=[c.b_x1T])
                    fw.op(fw.dve, lambda: nc.vector.scalar_tensor_tensor(out=c.zTf[:], in0=c.x1T[:], scalar=oh[:, J:J + 1], in1=c.zTf[:],
                                                                         op0=ALU.mult, op1=ALU.add),
                          reads=[c.b_x1T, c.b_zTf, b_oh], writes=[c.b_zTf])
                self.c_inverse(c, 1, hf)
                fw.barrier()

    def stage_D(self, job, rings, st):
        nc, fw = self.nc, self.fw
        mT, b_mT = self.sb(st, "mergedT", [128, 8, 512], BF16)
        x2, b_x2 = self.sb(st, "x2", [128, 4, D], F32)
        hmT, b_hmT = self.sb(st, "hmT", [128, 8, 512], BF16)
        aT, b_aT = self.sb(st, "aT", [128, 32, 512], BF16)
        w8 = Ring(nc, st, "w8_", 3, [128, 8, 128], BF16)
        wrow = Ring(nc, st, "wrow_", 3, [128, 1, D], BF16)
        gt = Ring(nc, st, "gt_", 4, [128, 512], F32)
        yt = Ring(nc, st, "yt_", 2, [128, D], F32)
        for tg in range(4):
            tsl = slice(tg * 512, (tg + 1) * 512)
            for c in range(8):
                gs = []
                for gi in range(2):
                    wt, bw = w8.next()
                    col0 = C_G + gi * D + c * 128
                    self.load_w(wt[:], bw, self.w_in[:, col0:col0 + 128], 8)
                    ps, bps = self.bank()
                    self.mm_group(ps[:], [(wt[:, k, :], self.hT[:, k, tsl]) for k in range(8)],
                                  reads=[bw, self.b_hT], writes=[bps])
                    g, bg = gt.next()
                    fw.op(fw.act, lambda g=g, ps=ps: nc.scalar.activation(out=g[:], in_=ps[:], func=ACTF.Sigmoid),
                          reads=[bps], writes=[bg])
                    gs.append((g, bg))
                wt, bw = w8.next()
                self.load_w(wt[:, 0:4, :], bw, self.w_ab[:, c * 128:(c + 1) * 128], 4)
                ps, bps = self.bank()
                self.mm_group(ps[:], [(wt[:, k, :], self.attnT[:, k, tsl]) for k in range(4)],
                              reads=[bw, self.b_attnT], writes=[bps])
                g, bg = gs[0]
                fw.op(fw.dve, lambda g=g, ps=ps: nc.vector.tensor_tensor(out=g[:], in0=g[:], in1=ps[:], op=ALU.mult),
                      reads=[bps, bg], writes=[bg])
                wt, bw = w8.next()
                self.load_w(wt[:], bw, self.w_hb[:, c * 128:(c + 1) * 128], 8)
                ps, bps = self.bank()
                self.mm_group(ps[:], [(wt[:, k, :], self.zfT[:, k, tsl]) for k in range(8)],
                              reads=[bw, self.b_zfT], writes=[bps])
                g2, bg2 = gs[1]
                fw.op(fw.dve, lambda g2=g2, ps=ps: nc.vector.tensor_tensor(out=g2[:], in0=g2[:], in1=ps[:], op=ALU.mult),
                      reads=[bps, bg2], writes=[bg2])
                fw.op(fw.dve, lambda g=g, g2=g2, c=c: nc.vector.tensor_tensor(out=mT[:, c, :], in0=g[:], in1=g2[:], op=ALU.add),
                      reads=[bg, bg2], writes=[b_mT])
            for pss in range(2):
                banks = [[self.bank() for _ in range(2)] for _ in range(2)]
                for kc in range(8):
                    wt, bw = wrow.next()
                    self.load_w(wt[:], bw, self.w_out[kc * 128:(kc + 1) * 128, :], 1)
                    for tt in range(2):
                        tok = (pss * 2 + tt) * 128
                        for hf in range(2):
                            ps, bps = banks[tt][hf]
                            self.mm_group(ps[:], [(mT[:, kc, tok:tok + 128], wt[:, 0, hf * 512:(hf + 1) * 512])],
                                          reads=[bw, b_mT], writes=[bps], start=(kc == 0), stop=(kc == 7))
                for tt in range(2):
                    ti = pss * 2 + tt
                    gtok = tg * 512 + ti * 128
                    xt, bx = rings["x"].next()
                    fw.dma(fw.q_sp, xt[:], self.xm[job.idx, gtok:gtok + 128, :], writes=[bx])
                    for hf in range(2):
                        ps, bps = banks[tt][hf]
                        fw.op(fw.dve, lambda ps=ps, ti=ti, hf=hf, xt=xt: nc.vector.tensor_tensor(
                            out=x2[:, ti, hf * 512:(hf + 1) * 512], in0=xt[:, hf * 512:(hf + 1) * 512], in1=ps[:], op=ALU.add),
                            reads=[bps, bx], writes=[b_x2])
                    self.rmsnorm_T(rings, x2[:, ti, :], b_x2, self.gmlp, self.b_gmlp,
                                   hmT[:, :, ti * 128:(ti + 1) * 128], b_hmT)
            for f in range(32):
                wt, bw = w8.next()
                self.load_w(wt[:], bw, self.w_ff1[:, f * 128:(f + 1) * 128], 8)
                ps, bps = self.bank()
                self.mm_group(ps[:], [(wt[:, k, :], hmT[:, k, :]) for k in range(8)], reads=[bw, b_hmT], writes=[bps])
                g, bg = gt.next()
                fw.op(fw.act, lambda: nc.scalar.activation(out=g[:], in_=ps[:], func=ACTF.Relu), reads=[bps], writes=[bg])
                fw.op(fw.dve, lambda: nc.vector.tensor_tensor(out=aT[:, f, :], in0=g[:], in1=g[:], op=ALU.mult),
                      reads=[bg], writes=[b_aT])
            for pss in range(2):
                banks = [[self.bank() for _ in range(2)] for _ in range(2)]
                for f in range(32):
                    wt, bw = wrow.next()
                    self.load_w(wt[:], bw, self.w_ff2[f * 128:(f + 1) * 128, :], 1)
                    for tt in range(2):
                        tok = (pss * 2 + tt) * 128
                        for hf in range(2):
                            ps, bps = banks[tt][hf]
                            self.mm_group(ps[:], [(aT[:, f, tok:tok + 128], wt[:, 0, hf * 512:(hf + 1) * 512])],
                                          reads=[bw, b_aT], writes=[bps], start=(f == 0), stop=(f == 31))
                for tt in range(2):
                    ti = pss * 2 + tt
                    gtok = tg * 512 + ti * 128
                    o, bo = yt.next()
                    for hf in range(2):
                        ps, bps = banks[tt][hf]
                        fw.op(fw.dve, lambda ps=ps, ti=ti, hf=hf, o=o: nc.vector.tensor_tensor(
                            out=o[:, hf * 512:(hf + 1) * 512], in0=x2[:, ti, hf * 512:(hf + 1) * 512], in1=ps[:], op=ALU.add),
                            reads=[bps, b_x2], writes=[bo])
                    fw.dma(fw.q_sp, self.y[job.idx, gtok:gtok + 128, :], o[:], reads=[bo])


def build_program(njobs, dbg=(), stages="ABCD", fake_branches=False, sample_last=False, nslots=1):
    nc = bass.Bass("TRN2", target_bir_lowering=False)
    with ExitStack() as st:
        kb = KB(nc, st, njobs, dbg, nslots)
        fw = kb.fw
        if "C" in stages:
            with ExitStack() as s0:
                kb.init_hyena(s0)
                fw.barrier()
        kb.alloc_persistent()
        if "B" in stages:
            kb.init_attention()
        for j in range(njobs):
            job = Job(j, sample_last and j == njobs - 1)
            with ExitStack() as sa:
                rings = {
                    "x": Ring(nc, sa, "xr", 2, [128, D], F32),
                    "sq": Ring(nc, sa, "sqr", 2, [128, D], BF16),
                    "col": Ring(nc, sa, "colr", 4, [128, 4], F32),
                    "hn": Ring(nc, sa, "hnr", 2, [128, D], BF16),
                }
                kb.load_gain(sa, "mix")
                kb.stage_A(job, rings)
                fw.barrier()
                if fake_branches:
                    fw.op(fw.dve, lambda: nc.vector.tensor_copy(out=kb.attnT[:], in_=kb.hT[:, 0:4, :]), reads=[kb.b_hT], writes=[kb.b_attnT])
                    fw.op(fw.dve, lambda: nc.vector.tensor_copy(out=kb.zfT[:], in_=kb.hT[:]), reads=[kb.b_hT], writes=[kb.b_zfT])
                if "B" in stages:
                    with ExitStack() as sb_:
                        kb.stage_B(job, rings, sb_)
                        fw.barrier()
                    kb.dbg_dump("attnT", kb.attnT[:], kb.b_attnT, [128, 4, L], BF16)
            if "C" in stages:
                with ExitStack() as sc_:
                    if job.sample:
                        kb.stage_C_sample(job, sc_)
                    else:
                        kb.stage_C(job, sc_)
                    fw.barrier()
                if job.sample:
                    with ExitStack() as sa2:
                        rings = {
                            "x": Ring(nc, sa2, "xr", 2, [128, D], F32),
                            "sq": Ring(nc, sa2, "sqr", 2, [128, D], BF16),
                            "col": Ring(nc, sa2, "colr", 4, [128, 4], F32),
                            "hn": Ring(nc, sa2, "hnr", 2, [128, D], BF16),
                        }
                        kb.load_gain(sa2, "mix")
                        kb.stage_A(job, rings)
                        fw.barrier()
                kb.dbg_dump("zfT", kb.zfT[:], kb.b_zfT, [128, 8, L], BF16)
            if "D" in stages:
                with ExitStack() as sd:
                    rings = {
                        "x": Ring(nc, sd, "xr", 2, [128, D], F32),
                        "sq": Ring(nc, sd, "sqr", 2, [128, D], BF16),
                        "col": Ring(nc, sd, "colr", 4, [128, 4], F32),
                        "hn": Ring(nc, sd, "hnr", 2, [128, D], BF16),
                    }
                    kb.load_gain(sd, "mlp")
                    kb.stage_D(job, rings, sd)
                    fw.barrier()
        fw.barrier()
    return nc, kb


_TABLES = {}


def _sq(a):
    a = np.asarray(a)
    return np.ascontiguousarray(a[0]) if a.shape[0] == 1 else a


def kernel(**inputs):
    f32 = lambda a: np.ascontiguousarray(np.asarray(a), dtype=np.float32)
    xp = f32(inputs["x_prompt"])
    xs = f32(inputs["x_sample"])[0]
    njobs = 5
    if "dft" not in _TABLES:
        _TABLES["dft"] = dft_tables()
    Ftab, Gtab = _TABLES["dft"]
    shared = dict(
        w_in=f32(inputs["w_in"])[0], w_ab=f32(inputs["w_attn_branch"])[0], w_hb=f32(inputs["w_hyena_branch"])[0],
        w_out=f32(inputs["w_out"])[0], w_ff1=f32(inputs["w_ff1"])[0], w_ff2=f32(inputs["w_ff2"])[0],
        g_mix=f32(inputs["g_mix"])[0], g_mlp=f32(inputs["g_mlp"])[0], rel_bias=f32(inputs["rel_bias"]),
        g_q=f32(inputs["g_q"])[0], g_k=f32(inputs["g_k"])[0],
        filt_w1=f32(inputs["filt_w1"])[0], filt_b1=f32(inputs["filt_b1"])[0], filt_w2=f32(inputs["filt_w2"])[0],
        filt_b2=f32(inputs["filt_b2"])[0], filt_w3=f32(inputs["filt_w3"])[0], filt_b3=f32(inputs["filt_b3"])[0],
        filt_w4=f32(inputs["filt_w4"])[0], filt_freq=f32(inputs["filt_freq"])[0], filt_skip=f32(inputs["filt_skip"])[0],
        w_short=f32(inputs["w_short"])[0], b_short=f32(inputs["b_short"])[0], Ftab=Ftab, Gtab=Gtab)
    zf_sh, cols_sh, nad = filter_tables_shared()
    xfull = np.ascontiguousarray(xs.reshape(8, L, D))
    xedge = np.zeros((8, 128, D), np.float32)
    for K in range(8):
        if K > 0:
            xedge[K, 0] = xs[L * K - 1]
        if K < 7:
            xedge[K, 1] = xs[L * (K + 1)]
    shared.update(zfT_tab=zf_sh, fcols_tab=cols_sh, nad_tab=nad, xfull=xfull, xedge=xedge)
    in_maps = []
    for c in range(NCORES):
        xm = np.concatenate([xp[4 * c:4 * c + 4], xs[None, L * c:L * (c + 1)]], axis=0)
        xh = np.zeros((L, D), np.float32)
        if c > 0:
            xh[:1024] = xs[L * c - 1024:L * c]
        if c < NCORES - 1:
            xh[1024:] = xs[L * (c + 1):L * (c + 1) + 1024]
        zfc, colsc, oh = filter_tables_core(c)
        m = dict(shared)
        m.update(xm=np.ascontiguousarray(xm), xh=xh, zfT_core=zfc, fcols_core=colsc, onehot=oh)
        m.update(host_consts(c))
        in_maps.append(m)
    nc, kb = build_program(njobs, stages="ABCD", sample_last=True, nslots=25)
    res = run_bass_kernel_spmd(nc, in_maps, core_ids=list(range(NCORES)))
    y_prompt = np.empty((32, L, D), np.float32)
    y_sample = np.empty((1, 8 * L, D), np.float32)
    for c in range(NCORES):
        y = res.results[c]["y"]
        y_prompt[4 * c:4 * c + 4] = y[0:4]
        y_sample[0, L * c:L * (c + 1)] = y[4]
    return (y_prompt, y_sample)
```

```python
import math
from contextlib import ExitStack
import numpy as np
import ml_dtypes
import concourse.bass as bass
import concourse.mybir as mybir
from concourse.bass_utils import run_bass_kernel_spmd

F32 = mybir.dt.float32
BF16 = mybir.dt.bfloat16
ALU = mybir.AluOpType
ACTF = mybir.ActivationFunctionType

D = 1024
L = 2048
NCORES = 8
EPS = 1e-6
HD = 128
IN_W = 9728
C_Q, C_K, C_V, C_HY, C_G = 0, 1536, 3072, 4608, 7680
MASKV = -30000.0
GROUPS = ((128, 1), (512, 4), (2048, 16))


class Buf:
    __slots__ = ("name", "w", "r")

    def __init__(self, name=""):
        self.name = name
        self.w = None
        self.r = {}


class Eng:
    def __init__(self, name, h, sem):
        self.name, self.h, self.sem = name, h, sem
        self.count = 0
        self.seen = {}


class DmaQ:
    def __init__(self, name, eng, sems):
        self.name, self.eng, self.sems = name, eng, sems
        self.n = 0


class FW:
    def __init__(self, nc, stack, n_dma_sems=8):
        self.nc = nc
        S = lambda n: stack.enter_context(nc.semaphore(n))
        self.pe = Eng("pe", nc.tensor, S("s_pe"))
        self.act = Eng("act", nc.scalar, S("s_act"))
        self.dve = Eng("dve", nc.vector, S("s_dve"))
        self.pool = Eng("pool", nc.gpsimd, S("s_pool"))
        self.sp = Eng("sp", nc.sync, S("s_sp"))
        self.engs = [self.pe, self.act, self.dve, self.pool, self.sp]
        self.q_sp = DmaQ("qsp", self.sp, [S(f"s_qsp{i}") for i in range(n_dma_sems)])
        self.q_pool = DmaQ("qpl", self.pool, [S(f"s_qpl{i}") for i in range(n_dma_sems)])
        self.qs = [self.q_sp, self.q_pool]
        self.n_inst = 0

    def _wait(self, eng, key, sem, val):
        if eng.seen.get(key, 0) >= val:
            return
        eng.h.wait_ge(sem, val)
        eng.seen[key] = val
        self.n_inst += 1

    def _deps(self, eng, reads, writes):
        need = {}

        def add(m):
            if m is None:
                return
            k, s, v = m
            if k not in need or need[k][1] < v:
                need[k] = (s, v)

        for b in reads:
            add(b.w)
        for b in writes:
            add(b.w)
            for k, (s, v) in b.r.items():
                add((k, s, v))
        for k, (s, v) in need.items():
            if k == "pe" and eng is self.pe:
                continue
            self._wait(eng, k, s, v)

    def _mark(self, reads, writes, m):
        k, s, v = m
        for b in reads:
            b.r[k] = (s, v)
        for b in writes:
            b.w = m
            b.r = {}

    def op(self, eng, fn, reads=(), writes=()):
        self._deps(eng, reads, writes)
        inst = fn()
        eng.count += 1
        inst.then_inc(eng.sem, 1)
        m = (eng.name, eng.sem, eng.count)
        self._mark(reads, writes, m)
        self.n_inst += 1
        return m

    def dma(self, q, out, in_, reads=(), writes=(), **kw):
        eng = q.eng
        j = q.n
        ns = len(q.sems)
        sem = q.sems[j % ns]
        key = f"{q.name}{j % ns}"
        prev = 16 * (j // ns)
        if prev > 0:
            self._wait(eng, key, sem, prev)
        self._deps(eng, reads, writes)
        inst = eng.h.dma_start(out=out, in_=in_, **kw)
        inst.then_inc(sem, 16)
        q.n += 1
        m = (key, sem, prev + 16)
        self._mark(reads, writes, m)
        self.n_inst += 1
        return m

    def barrier(self):
        for e in self.engs:
            for o in self.engs:
                if o is e or o.count == 0:
                    continue
                self._wait(e, o.name, o.sem, o.count)
            for q in self.qs:
                ns = len(q.sems)
                for i in range(min(ns, q.n)):
                    last = ((q.n - 1 - i) // ns) * ns + i
                    self._wait(e, f"{q.name}{i}", q.sems[i], 16 * (last // ns + 1))


_UID = [0]


def _uid(name):
    _UID[0] += 1
    return f"{name}_{_UID[0]}"


class Ring:
    def __init__(self, nc, st, name, n, shape, dt):
        self.items = []
        for i in range(n):
            t = st.enter_context(nc.sbuf_tensor(_uid(f"{name}{i}"), shape, dt))
            self.items.append((t, Buf(f"{name}{i}")))
        self.i = 0

    def next(self):
        it = self.items[self.i % len(self.items)]
        self.i += 1
        return it


def _bf(a):
    return np.ascontiguousarray(a.astype(ml_dtypes.bfloat16))


def t5_bucket_np(rel):
    nb, max_exact = 16, 8
    side = np.where(rel > 0, nb, 0)
    n = np.abs(rel)
    nf = np.maximum(n, 1).astype(np.float32)
    large = max_exact + (np.log(nf / np.float32(max_exact)) / np.float32(math.log(1024 / max_exact))
                         * np.float32(nb - max_exact)).astype(np.int32)
    large = np.minimum(large, nb - 1)
    return side + np.where(n < max_exact, n, large)


def host_consts(core):
    c = {}
    c["ident"] = _bf(np.eye(128, dtype=np.float32))
    E = np.zeros((3, 33, 384), np.float32)
    for g, (_, d) in enumerate(GROUPS):
        for u in range(384):
            dl = u - 191
            if abs(dl) <= 64:
                E[g, int(t5_bucket_np(np.int32(dl * d))), u] = 1.0
            else:
                E[g, 32, u] = 1.0
    c["Esel"] = E
    kvt = np.zeros((2, 128, 192), np.float32)
    for kind in range(2):
        for g, (_, d) in enumerate(GROUPS):
            Me = 2 * L // d
            for cc in range(64):
                col = 64 * cc + np.arange(128)
                r = col // Me
                mp = col % Me
                e = r + d * mp
                if kind == 0:
                    ok = (e >= 1024) & (e < 3072)
                else:
                    gtok = 2048 * core + e - 1024
                    ok = (gtok >= 0) & (gtok < 16384)
                ok = ok & (col < 2 * L)
                kvt[kind, :, g * 64 + cc] = np.where(ok, 0.0, MASKV)
    c["kvalid"] = kvt
    return c


NFFT = 2 * L


def dft_tables():
    n = np.arange(NFFT, dtype=np.int64)
    k = np.arange(L, dtype=np.int64)
    ph = (np.outer(n, k) % NFFT).astype(np.float64) * (2 * np.pi / NFFT)
    Fre = np.cos(ph)
    Fim = -np.sin(ph)
    Fim[:, 0] = np.where(n % 2 == 0, 1.0, -1.0)
    Fall = np.concatenate([Fre, Fim], axis=1).astype(np.float32)
    Ftab = Fall.reshape(32, 128, 32, 128).transpose(2, 1, 0, 3)
    nn = np.arange(L, dtype=np.int64)
    ph2 = (np.outer(k, nn) % NFFT).astype(np.float64) * (2 * np.pi / NFFT)
    Gre = (2.0 / NFFT) * np.cos(ph2)
    Gre[0, :] = 1.0 / NFFT
    Gim = -(2.0 / NFFT) * np.sin(ph2)
    Gim[0, :] = np.where(nn % 2 == 0, 1.0, -1.0) / NFFT
    Gall = np.concatenate([Gre, Gim], axis=0).astype(np.float32)
    Gtab = Gall.reshape(32, 128, 4, 512).transpose(2, 0, 1, 3)
    return _bf(Ftab), _bf(Gtab)


SLOT_ORDERS = [0, 1] + [0] * 15 + [1] * 8
NSLOT_SH = 17


def _slot_rows(Ls, dd):
    r = np.arange(NFFT, dtype=np.int64)
    bands = np.linspace(1e-4, 15.0, 16, dtype=np.float32)[None, :]
    m = np.where(r < L, L * dd + r, L * dd + r - NFFT)
    pos = np.abs(m)
    ok = (pos < Ls) & (r != L)
    pos = np.where(ok, pos, 0)
    t = (pos.astype(np.float32) / np.float32(Ls - 1)).astype(np.float32)
    ang = (np.float32(2.0 * math.pi) * pos.astype(np.float32) / np.float32(Ls)).astype(np.float32)[:, None]
    z = np.concatenate([t[:, None], np.cos(bands * ang), -np.sin(bands * ang)], axis=1).astype(np.float32)
    ffwd = (ok & (m >= 0)).astype(np.float32)
    fbwd = (ok & (m < 0)).astype(np.float32)
    cols = np.zeros((128, 3, 32), np.float32)
    for a in range(32):
        cols[:, 0, a] = t[a * 128:(a + 1) * 128]
        cols[:, 1, a] = ffwd[a * 128:(a + 1) * 128]
        cols[:, 2, a] = fbwd[a * 128:(a + 1) * 128]
    return np.ascontiguousarray(z.T), cols


def filter_tables_shared():
    specs = [(L, 0), (L, 0)] + [(8 * L, d) for d in range(-7, 8)]
    rows = [_slot_rows(*sp) for sp in specs]
    zf = np.stack([r[0] for r in rows]).astype(np.float32)
    cols = np.stack([r[1] for r in rows]).astype(np.float32)
    min_decay = math.log(1e-2) / 1.5
    max_decay = math.log(1e-2) / 0.3
    nad = -np.abs(np.linspace(min_decay, max_decay, D, dtype=np.float32))
    return zf, cols, nad.astype(np.float32)


def filter_tables_core(core):
    rows = [_slot_rows(8 * L, core - J) for J in range(8)]
    zf = np.stack([r[0] for r in rows]).astype(np.float32)
    cols = np.stack([r[1] for r in rows]).astype(np.float32)
    oh = np.zeros((128, 8), np.float32)
    oh[:, core] = 1.0
    return zf, cols, oh


class Job:
    def __init__(self, idx, sample):
        self.idx, self.sample = idx, sample


class KB:
    def __init__(self, nc, st, njobs, dbg=(), nslots=1):
        self.nc = nc
        self.st = st
        self.fw = FW(nc, st)
        self.njobs = njobs
        self.dbg = set(dbg)
        self.dbg_out = {}
        di = lambda n, s, d=F32: nc.dram_tensor(n, s, d, kind="ExternalInput").ap()
        self.xm = di("xm", [njobs, L, D])
        self.xh = di("xh", [L, D])
        self.w_in = di("w_in", [D, IN_W])
        self.w_ab = di("w_ab", [512, D])
        self.w_hb = di("w_hb", [D, D])
        self.w_out = di("w_out", [D, D])
        self.w_ff1 = di("w_ff1", [D, 4 * D])
        self.w_ff2 = di("w_ff2", [4 * D, D])
        self.g_mix = di("g_mix", [D])
        self.g_mlp = di("g_mlp", [D])
        self.identd = di("ident", [128, 128], BF16)
        self.rel_bias = di("rel_bias", [32, 12])
        self.g_q = di("g_q", [128])
        self.g_k = di("g_k", [128])
        self.Esel = di("Esel", [3, 33, 384])
        self.kvd = di("kvalid", [2, 128, 192])
        self.bsc = nc.dram_tensor("bsc", [3, 12, 384], F32, kind="Internal").ap()
        self.nslots = nslots
        self.f_w1 = di("filt_w1", [33, 64]); self.f_b1 = di("filt_b1", [64])
        self.f_w2 = di("filt_w2", [64, 64]); self.f_b2 = di("filt_b2", [64])
        self.f_w3 = di("filt_w3", [64, 64]); self.f_b3 = di("filt_b3", [64])
        self.f_w4 = di("filt_w4", [64, 4096]); self.f_freq = di("filt_freq", [64])
        self.f_skip = di("filt_skip", [2, D])
        self.w_short = di("w_short", [3, 3 * D]); self.b_short = di("b_short", [3 * D])
        self.zfTd = di("zfT_tab", [NSLOT_SH, 33, NFFT])
        self.fcolsd = di("fcols_tab", [NSLOT_SH, 128, 3, 32])
        self.zfTc = di("zfT_core", [8, 33, NFFT])
        self.fcolsc = di("fcols_core", [8, 128, 3, 32])
        self.onehotd = di("onehot", [128, 8])
        self.xfull = di("xfull", [8, L, D])
        self.xedge = di("xedge", [8, 128, D])
        self.nadd = di("nad_tab", [D])
        self.Ftab = di("Ftab", [32, 128, 32, 128], BF16)
        self.Gtab = di("Gtab", [4, 32, 128, 512], BF16)
        self.Hs = [nc.dram_tensor(f"Hs{i}", [2, 16, 2, 128, 1024], BF16, kind="Internal").ap() for i in range(nslots)]
        self.b_Hs = Buf("Hs")
        dsc = lambda n, sh: nc.dram_tensor(n, sh, BF16, kind="Internal").ap()
        self.xs0, self.xs1 = dsc("xs0", [2, 8, 16, 128, 1024]), dsc("xs1", [2, 8, 16, 128, 1024])
        self.zs, self.x1s, self.z1s = dsc("zs", [2, 8, 128, 4, L]), dsc("x1s", [2, 8, 128, 4, L]), dsc("z1s", [2, 8, 128, 4, L])
        self.b_xs0, self.b_xs1, self.b_zs, self.b_x1s, self.b_z1s = [Buf(n) for n in ("xs0", "xs1", "zs", "x1s", "z1s")]
        self.y = nc.dram_tensor("y", [njobs, L, D], F32, kind="ExternalOutput").ap()
        self.pb = []
        for i in range(7):
            t = st.enter_context(nc.psum_tensor(f"pb{i}", [128, 512], F32))
            self.pb.append((t, Buf(f"pb{i}")))
        self.ptr = st.enter_context(nc.psum_tensor("ptr", [128, 1024], BF16))
        self.b_ptr = Buf("ptr")
        self.pbi = 0
        self.ident, self.b_ident = self.sb(st, "identb", [128, 128], BF16)
        self.fw.dma(self.fw.q_sp, self.ident[:], self.identd[:, :], writes=[self.b_ident])

    def sb(self, st, name, shape, dt):
        t = st.enter_context(self.nc.sbuf_tensor(_uid(name), shape, dt))
        return t, Buf(name)

    def alloc_persistent(self):
        nc, st = self.nc, self.st
        self.hT, self.b_hT = self.sb(st, "hT", [128, 8, L], BF16)
        self.attnT, self.b_attnT = self.sb(st, "attnT", [128, 4, L], BF16)
        self.zfT, self.b_zfT = self.sb(st, "zfT", [128, 8, L], BF16)
        self.edge, self.b_edge = self.sb(st, "edge", [128, 8, 2], BF16)
        fw = self.fw
        fw.op(fw.dve, lambda: nc.vector.memset(self.edge[:], 0.0), writes=[self.b_edge])
        self.wsc, self.b_wsc = self.sb(st, "wsc", [128, 4, 24], F32)
        for k in range(3):
            fw.dma(fw.q_sp, self.wsc[:, k, :], self.w_short[k, :].rearrange("(j p) -> p j", p=128), writes=[self.b_wsc],
                   allow_slow_non_contiguous=True)
        fw.dma(fw.q_sp, self.wsc[:, 3, :], self.b_short.rearrange("(j p) -> p j", p=128), writes=[self.b_wsc],
               allow_slow_non_contiguous=True)
        self.skc, self.b_skc = self.sb(st, "skc", [128, 2, 8], F32)
        for o in range(2):
            fw.dma(fw.q_sp, self.skc[:, o, :], self.f_skip[o, :].rearrange("(j p) -> p j", p=128), writes=[self.b_skc],
                   allow_slow_non_contiguous=True)

    def init_hyena(self, st):
        nc, fw = self.nc, self.fw
        TWO_PI = 2.0 * math.pi
        MAGIC = 12582912.0
        w1, b_w1 = self.sb(st, "fw1", [33, 64], F32)
        w2, b_w2 = self.sb(st, "fw2", [64, 64], F32)
        w3, b_w3 = self.sb(st, "fw3", [64, 64], F32)
        w4b, b_w4 = self.sb(st, "fw4", [64, 4096], BF16)
        frc, b_frc = self.sb(st, "frc", [64, 8], F32)
        nadt, b_nad = self.sb(st, "nadt", [128, D], F32)
        ktime, b_kt = self.sb(st, "ktime", [128, 32, 1024], BF16)
        sc, b_sc = self.sb(st, "fsc", [128, 3, 32], F32)
        fw.dma(fw.q_sp, w1[:], self.f_w1[:, :], writes=[b_w1])
        fw.dma(fw.q_sp, w2[:], self.f_w2[:, :], writes=[b_w2])
        fw.dma(fw.q_sp, w3[:], self.f_w3[:, :], writes=[b_w3])
        fw.dma(fw.q_pool, w4b[:], self.f_w4[:, :], writes=[b_w4])
        fw.dma(fw.q_sp, nadt[:], self.nadd.partition_broadcast(128), writes=[b_nad])
        col1 = lambda v: v.rearrange("(p o) -> p o", o=1)
        fw.dma(fw.q_sp, frc[:, 0:1], col1(self.f_freq), writes=[b_frc])
        for li, bb in enumerate((self.f_b1, self.f_b2, self.f_b3)):
            fw.dma(fw.q_sp, frc[:, 4 + li:5 + li], col1(bb), writes=[b_frc])
        for li in range(3):
            fw.op(fw.dve, lambda: nc.vector.tensor_scalar(out=frc[:, 1 + li:2 + li], in0=frc[:, 4 + li:5 + li], scalar1=frc[:, 0:1],
                                                          scalar2=None, op0=ALU.mult), reads=[b_frc], writes=[b_frc])
        Ws = [(w1, b_w1, 33), (w2, b_w2, 64), (w3, b_w3, 64)]
        zt_r = Ring(nc, st, "zt", 2, [33, 512], F32)
        a_r = Ring(nc, st, "fa", 3, [64, 512], F32)
        r_r = Ring(nc, st, "fr", 2, [64, 512], F32)
        h3_r = Ring(nc, st, "fh3", 2, [64, 512], BF16)
        dec_r = Ring(nc, st, "fdec", 2, [128, D], F32)
        t_r = Ring(nc, st, "ft", 3, [128, 512], F32)
        F_r = Ring(nc, st, "fF", 2, [128, 32, 128], BF16)
        ho_r = Ring(nc, st, "fho", 4, [128, 512], BF16)
        nq_r = Ring(nc, st, "fnq", 2, [1, 512], BF16)
        b_patch = Buf("hpatch")
        for si in range(self.nslots):
            so = SLOT_ORDERS[si]
            zsrc = self.zfTd[si] if si < NSLOT_SH else self.zfTc[si - NSLOT_SH]
            fw.dma(fw.q_sp, sc[:], self.fcolsd[si] if si < NSLOT_SH else self.fcolsc[si - NSLOT_SH], writes=[b_sc])
            for rg in range(8):
                zt, bzt = zt_r.next()
                fw.dma(fw.q_sp, zt[:], zsrc[:, rg * 512:(rg + 1) * 512], writes=[bzt])
                h, bh, K = zt, bzt, 33
                for li in range(3):
                    W, bW, K = Ws[li]
                    ps, bps = self.bank()
                    self.mm_group(ps[0:64, :], [(W[0:K, :], h[0:K, :])], reads=[bW, bh], writes=[bps])
                    a, ba = a_r.next()
                    fw.op(fw.dve, lambda: nc.vector.tensor_scalar(out=a[:], in0=ps[0:64, :], scalar1=frc[:, 0:1],
                                                                  scalar2=frc[:, 1 + li:2 + li], op0=ALU.mult, op1=ALU.add),
                          reads=[bps, b_frc], writes=[ba])
                    r, br = r_r.next()
                    fw.op(fw.dve, lambda: nc.vector.tensor_scalar(out=r[:], in0=a[:], scalar1=1.0 / TWO_PI, scalar2=MAGIC,
                                                                  op0=ALU.mult, op1=ALU.add), reads=[ba], writes=[br])
                    fw.op(fw.dve, lambda: nc.vector.tensor_scalar(out=r[:], in0=r[:], scalar1=MAGIC, scalar2=None,
                                                                  op0=ALU.subtract), reads=[br], writes=[br])
                    fw.op(fw.dve, lambda: nc.vector.scalar_tensor_tensor(out=a[:], in0=r[:], scalar=-TWO_PI, in1=a[:],
                                                                         op0=ALU.mult, op1=ALU.add), reads=[br, ba], writes=[ba])
                    if li < 2:
                        hn_, bhn_ = a_r.next()
                    else:
                        hn_, bhn_ = h3_r.next()
                    fw.op(fw.act, lambda: nc.scalar.activation(out=hn_[:], in_=a[:], func=ACTF.Sin), reads=[ba], writes=[bhn_])
                    h, bh = hn_, bhn_
                for ai in range(4):
                    aidx = rg * 4 + ai
                    dec, bdec = dec_r.next()
                    fw.op(fw.act, lambda: nc.scalar.activation(out=dec[:], in_=nadt[:], func=ACTF.Exp, scale=sc[:, 0, aidx:aidx + 1]),
                          reads=[b_nad, b_sc], writes=[bdec])
                    for o in (so,):
                        for hf in range(2):
                            psf, bpsf = self.bank()
                            c0 = o * 2048 + hf * 512
                            self.mm_group(psf[:], [(h[:, ai * 128:(ai + 1) * 128], w4b[:, c0:c0 + 512])], reads=[bh, b_w4], writes=[bpsf])
                            psb, bpsb = self.bank()
                            c1 = o * 2048 + 1024 + hf * 512
                            self.mm_group(psb[:], [(h[:, ai * 128:(ai + 1) * 128], w4b[:, c1:c1 + 512])], reads=[bh, b_w4], writes=[bpsb])
                            t, bt = t_r.next()
                            fw.op(fw.dve, lambda: nc.vector.tensor_scalar(out=t[:], in0=psf[:], scalar1=sc[:, 1, aidx:aidx + 1],
                                                                          scalar2=None, op0=ALU.mult), reads=[bpsf, b_sc], writes=[bt])
                            fw.op(fw.dve, lambda: nc.vector.scalar_tensor_tensor(out=t[:], in0=psb[:], scalar=sc[:, 2, aidx:aidx + 1], in1=t[:],
                                                                                 op0=ALU.mult, op1=ALU.add),
                                  reads=[bpsb, b_sc, bt], writes=[bt])
                            cg = hf
                            fw.op(fw.dve, lambda: nc.vector.tensor_tensor(out=ktime[:, aidx, cg * 512:(cg + 1) * 512], in0=t[:],
                                                                          in1=dec[:, hf * 512:(hf + 1) * 512], op=ALU.mult),
                                  reads=[bt, bdec], writes=[b_kt])
            for m in range(32):
                Ft, bF = F_r.next()
                fw.dma(fw.q_sp, Ft[:], self.Ftab[m], writes=[bF])
                for cg in range(2):
                    ps, bps = self.bank()
                    self.mm_group(ps[:], [(Ft[:, a, :], ktime[:, a, cg * 512:(cg + 1) * 512]) for a in range(32)],
                                  reads=[bF, b_kt], writes=[bps])
                    ho, bho = ho_r.next()
                    fw.op(fw.act, lambda: nc.scalar.copy(out=ho[:], in_=ps[:]), reads=[bps], writes=[bho])
                    if m < 16:
                        fw.dma(fw.q_sp, self.Hs[si][cg, m, 0, :, 0:512], ho[:], reads=[bho])
                        fw.dma(fw.q_sp, self.Hs[si][cg, m, 1, :, 512:1024], ho[:], reads=[bho], writes=[b_patch] if m == 0 else [])
                    else:
                        mm = m - 16
                        hn2, bhn2 = ho_r.next()
                        fw.op(fw.act, lambda: nc.scalar.mul(out=hn2[:], in_=ps[:], mul=-1.0), reads=[bps], writes=[bhn2])
                        if mm == 0:
                            nq, bnq = nq_r.next()
                            fw.op(fw.act, lambda: nc.scalar.copy(out=nq[:], in_=ho[0:1, :]), reads=[bho], writes=[bnq])
                            fw.op(fw.dve, lambda: nc.vector.memset(ho[0:1, :], 0.0), reads=[bnq], writes=[bho])
                            fw.op(fw.dve, lambda: nc.vector.memset(hn2[0:1, :], 0.0), writes=[bhn2])
                            fw.dma(fw.q_sp, self.Hs[si][cg, 0, 1, 0:1, 512:1024], nq[:], reads=[bnq, b_patch], writes=[b_patch])
                        fw.dma(fw.q_sp, self.Hs[si][cg, mm, 1, :, 0:512], ho[:], reads=[bho])
                        fw.dma(fw.q_sp, self.Hs[si][cg, mm, 0, :, 512:1024], hn2[:], reads=[bhn2])

    def init_attention(self):
        nc, fw, st = self.nc, self.fw, self.st
        self.ones, self.b_ones = self.sb(st, "onesb", [128, 128], BF16)
        fw.op(fw.dve, lambda: nc.vector.memset(self.ones[:], 1.0), writes=[self.b_ones])
        self.gqk, self.b_gqk = self.sb(st, "gqk", [128, 2], F32)
        fw.dma(fw.q_sp, self.gqk[:, 0:1], self.g_q.rearrange("(p o) -> p o", o=1), writes=[self.b_gqk])
        fw.dma(fw.q_sp, self.gqk[:, 1:2], self.g_k.rearrange("(p o) -> p o", o=1), writes=[self.b_gqk])
        fw.op(fw.dve, lambda: nc.vector.tensor_scalar(out=self.gqk[:, 1:2], in0=self.gqk[:, 1:2], scalar1=math.sqrt(128.0),
                                                      scalar2=None, op0=ALU.mult), reads=[self.b_gqk], writes=[self.b_gqk])
        self.kv, self.b_kv = self.sb(st, "kv", [128, 2, 192], F32)
        fw.dma(fw.q_sp, self.kv[:], self.kvd.rearrange("k p c -> p k c"), writes=[self.b_kv])
        self.biasd = nc.dram_tensor("biasd", [128, 24, 128], BF16)
        self.b_biasd = Buf("biasd")
        with ExitStack() as s2:
            self.biasT, self.b_biasT = self.sb(s2, "biasT0", [128, 24, 128], BF16)
            rb, b_rb = self.sb(s2, "rb", [33, 12], F32)
            es, b_es = self.sb(s2, "es", [33, 3, 384], F32)
            bv, b_bv = self.sb(s2, "bv", [12, 3, 384], F32)
            stg = Ring(nc, s2, "bstg", 2, [128, 128], F32)
            fw.op(fw.dve, lambda: nc.vector.memset(rb[32:33, :], MASKV), writes=[b_rb])
            fw.dma(fw.q_sp, rb[0:32, :], self.rel_bias[:, :], writes=[b_rb])
            fw.dma(fw.q_sp, es[:], self.Esel.rearrange("g b u -> b g u"), writes=[b_es])
            b_bsc = Buf("bsc")
            for g in range(3):
                ps, bps = self.bank()
                self.mm_group(ps[0:12, 0:384], [(rb[:, :], es[:, g, :])], reads=[b_rb, b_es], writes=[bps])
                fw.op(fw.dve, lambda: nc.vector.tensor_copy(out=bv[:, g, :], in_=ps[0:12, 0:384]), reads=[bps], writes=[b_bv])
            fw.dma(fw.q_sp, self.bsc.rearrange("g h u -> h g u"), bv[:], reads=[b_bv], writes=[b_bsc])
            for g in range(3):
                for i in range(4):
                    for j in range(2):
                        h = 4 * g + i
                        t, bt = stg.next()
                        src = bass.AP(self.bsc.tensor, (g * 12 + h) * 384 + 127 + 128 * j, [[1, 128], [-1, 128]])
                        fw.dma(fw.q_sp, t[:], src, reads=[b_bsc], writes=[bt], allow_slow_non_contiguous=True)
                        fw.op(fw.dve, lambda: nc.vector.tensor_copy(out=self.biasT[:, (g * 4 + i) * 2 + j, :], in_=t[:]),
                              reads=[bt], writes=[self.b_biasT])
            fw.dma(fw.q_sp, self.biasd[:, :, :], self.biasT[:], reads=[self.b_biasT], writes=[self.b_biasd])
            fw.barrier()

    def load_gain(self, st, which):
        t, b = self.sb(st, "gain_" + which, [128, D], F32)
        src = self.g_mix if which == "mix" else self.g_mlp
        self.fw.dma(self.fw.q_sp, t[:], src.partition_broadcast(128), writes=[b])
        if which == "mix":
            self.gmix, self.b_gmix = t, b
        else:
            self.gmlp, self.b_gmlp = t, b

    def bank(self):
        it = self.pb[self.pbi % len(self.pb)]
        self.pbi += 1
        return it

    def dbg_dump(self, name, tile_ap, buf, shape, dt=F32):
        if name not in self.dbg:
            return
        o = self.nc.dram_tensor("dbg_" + name, list(shape), dt, kind="ExternalOutput").ap()
        self.dbg_out[name] = o
        self.fw.dma(self.fw.q_sp, o, tile_ap, reads=[buf])

    def load_w(self, tile, buf, src2d, nk):
        fw = self.fw
        fw.dma(fw.q_pool, tile, src2d.rearrange("(k p) c -> p k c", p=128), writes=[buf])

    def mm_group(self, out_ap, pairs, reads, writes, start=True, stop=True):
        nc = self.nc
        n = len(pairs)

        def emit():
            inst = None
            for i, (l, r) in enumerate(pairs):
                inst = nc.tensor.matmul(out_ap, lhsT=l, rhs=r, start=(start and i == 0), stop=(stop and i == n - 1))
            return inst
        return self.fw.op(self.fw.pe, emit, reads=reads, writes=writes)

    def rmsnorm_T(self, st_ring, x_ap, bx, g_tile, bg, dst_ap, bdst):
        nc, fw = self.nc, self.fw
        sq, bsq = st_ring["sq"].next()
        col, bcol = st_ring["col"].next()
        hn, bhn = st_ring["hn"].next()
        fw.op(fw.act, lambda: nc.scalar.activation(out=sq[:], in_=x_ap, func=ACTF.Square, accum_out=col[:, 0:1]),
              reads=[bx], writes=[bsq, bcol])
        fw.op(fw.act, lambda: nc.scalar.activation(out=col[:, 1:2], in_=col[:, 0:1], func=ACTF.Sqrt, scale=1.0 / D, bias=EPS),
              reads=[bcol], writes=[bcol])
        fw.op(fw.dve, lambda: nc.vector.reciprocal(out=col[:, 2:3], in_=col[:, 1:2]), reads=[bcol], writes=[bcol])
        fw.op(fw.dve, lambda: nc.vector.scalar_tensor_tensor(out=hn[:], in0=x_ap, scalar=col[:, 2:3], in1=g_tile[:],
                                                             op0=ALU.mult, op1=ALU.mult),
              reads=[bx, bcol, bg], writes=[bhn])

        def tr():
            inst = None
            for k in range(8):
                inst = nc.tensor.transpose(self.ptr[:, k * 128:(k + 1) * 128], hn[:, k * 128:(k + 1) * 128], self.ident[:])
            return inst
        fw.op(fw.pe, tr, reads=[bhn, self.b_ident], writes=[self.b_ptr])
        fw.op(fw.act, lambda: nc.scalar.copy(out=dst_ap, in_=self.ptr[:].rearrange("p (k t) -> p k t", k=8)),
              reads=[self.b_ptr], writes=[bdst])

    def stage_A(self, job, rings):
        fw = self.fw
        for t in range(L // 128):
            xt, bx = rings["x"].next()
            fw.dma(fw.q_sp, xt[:], self.xm[job.idx, t * 128:(t + 1) * 128, :], writes=[bx])
            self.rmsnorm_T(rings, xt[:], bx, self.gmix, self.b_gmix,
                           self.hT[:, :, t * 128:(t + 1) * 128], self.b_hT)

    def qk_norm(self, ps, bps, gcol, dst, bdst, rb):
        nc, fw = self.nc, self.fw
        sq, bsq = rb["sq5"].next()
        rs, brs = rb["rs5"].next()
        fw.op(fw.act, lambda: nc.scalar.activation(out=sq[:], in_=ps[:], func=ACTF.Square), reads=[bps], writes=[bsq])
        ps2, bps2 = self.bank()
        self.mm_group(ps2[:], [(self.ones[:], sq[:])], reads=[self.b_ones, bsq], writes=[bps2])
        fw.op(fw.act, lambda: nc.scalar.activation(out=rs[:], in_=ps2[:], func=ACTF.Sqrt, scale=1.0, bias=128.0 * EPS),
              reads=[bps2], writes=[brs])
        fw.op(fw.dve, lambda: nc.vector.reciprocal(out=rs[:], in_=rs[:]), reads=[brs], writes=[brs])
        fw.op(fw.dve, lambda: nc.vector.scalar_tensor_tensor(out=dst, in0=ps[:], scalar=gcol, in1=rs[:],
                                                             op0=ALU.mult, op1=ALU.mult),
              reads=[bps, brs, self.b_gqk], writes=[bdst])

    def stage_B(self, job, rings, st):
        nc, fw = self.nc, self.fw
        kind = 1 if job.sample else 0
        hTh, b_hTh = self.sb(st, "hTh", [128, 8, 2048], BF16)
        self.biasT, self.b_biasT = self.sb(st, "biasT", [128, 24, 128], BF16)
        fw.dma(fw.q_sp, self.biasT[:], self.biasd[:, :, :], reads=[self.b_biasd], writes=[self.b_biasT])
        if job.sample:
            for t in range(16):
                xt, bx = rings["x"].next()
                fw.dma(fw.q_sp, xt[:], self.xh[t * 128:(t + 1) * 128, :], writes=[bx])
                self.rmsnorm_T(rings, xt[:], bx, self.gmix, self.b_gmix, hTh[:, :, t * 128:(t + 1) * 128], b_hTh)
            fw.op(fw.dve, lambda: nc.vector.tensor_copy(out=self.edge[:, :, 0:1], in_=hTh[:, :, 1023:1024]), reads=[b_hTh], writes=[self.b_edge])
            fw.op(fw.dve, lambda: nc.vector.tensor_copy(out=self.edge[:, :, 1:2], in_=hTh[:, :, 1024:1025]), reads=[b_hTh], writes=[self.b_edge])

        def hsrc(eg, k):
            if eg < 2:
                return hTh[:, k, eg * 512:(eg + 1) * 512], b_hTh
            if eg < 6:
                return self.hT[:, k, (eg - 2) * 512:(eg - 1) * 512], self.b_hT
            return hTh[:, k, 1024 + (eg - 6) * 512:1024 + (eg - 5) * 512], b_hTh

        wr = Ring(nc, st, "wqkv", 6, [128, 8, 128], BF16)
        qn_r = Ring(nc, st, "qn", 1, [128, L], BF16)
        kn_r = Ring(nc, st, "kn", 1, [128, 2 * L], BF16)
        vT_r = Ring(nc, st, "vT", 1, [128, 2 * L], BF16)
        nd, b_nd = self.sb(st, "nd", [128, 2, L], F32)
        pt_r = Ring(nc, st, "pt", 3, [128, 256], BF16)
        vt_r = Ring(nc, st, "vt", 4, [128, 128], BF16)
        rb = {"sq5": Ring(nc, st, "sq5", 2, [128, 512], BF16), "rs5": Ring(nc, st, "rs5", 2, [128, 512], F32)}
        for i in range(4):
            for grp in range(3):
                d = GROUPS[grp][1]
                h = 4 * grp + i
                M, Me = L // d, 2 * L // d
                ws = []
                for c0 in (C_Q, C_K, C_V):
                    wt, bw = wr.next()
                    self.load_w(wt[:], bw, self.w_in[:, c0 + h * 128:c0 + (h + 1) * 128], 8)
                    ws.append((wt, bw))
                qn, bqn = qn_r.next()
                kn, bkn = kn_r.next()
                vT, bvT = vT_r.next()
                if not job.sample:
                    fw.op(fw.dve, lambda: nc.vector.memset(kn[:], 0.0), writes=[bkn])
                    fw.op(fw.dve, lambda: nc.vector.memset(vT[:], 0.0), writes=[bvT])
                for tg in range(4):
                    ps, bps = self.bank()
                    self.mm_group(ps[:], [(ws[0][0][:, k, :], self.hT[:, k, tg * 512:(tg + 1) * 512]) for k in range(8)],
                                  reads=[ws[0][1], self.b_hT], writes=[bps])
                    if d == 1:
                        dst = qn[:, tg * 512:(tg + 1) * 512]
                        self.qk_norm(ps, bps, self.gqk[:, 0:1], dst, bqn, rb)
                    else:
                        self.qk_norm_perm(ps, bps, self.gqk[:, 0:1], qn, bqn, rb, d, M, tg * 512 // d)
                if job.sample:
                    egs = list(range(8)) if d == 16 else list(range(1, 7))
                else:
                    egs = [2, 3, 4, 5]
                for eg in egs:
                    ps, bps = self.bank()
                    prs, rd = [], [ws[1][1]]
                    for k in range(8):
                        a, ba = hsrc(eg, k)
                        prs.append((ws[1][0][:, k, :], a))
                    self.mm_group(ps[:], prs, reads=[ws[1][1], ba], writes=[bps])
                    if d == 1:
                        self.qk_norm(ps, bps, self.gqk[:, 1:2], kn[:, eg * 512:(eg + 1) * 512], bkn, rb)
                    else:
                        self.qk_norm_perm(ps, bps, self.gqk[:, 1:2], kn, bkn, rb, d, Me, eg * 512 // d)
                    ps, bps = self.bank()
                    prs = []
                    for k in range(8):
                        a, ba = hsrc(eg, k)
                        prs.append((ws[2][0][:, k, :], a))
                    self.mm_group(ps[:], prs, reads=[ws[2][1], ba], writes=[bps])
                    if d == 1:
                        dstv = vT[:, eg * 512:(eg + 1) * 512]
                        srcv = ps[:]
                    else:
                        dstv = vT[:].rearrange("p (r m) -> p r m", r=d)[:, :, eg * 512 // d:(eg + 1) * 512 // d]
                        srcv = ps[:].rearrange("p (m r) -> p r m", r=d)
                    fw.op(fw.act, lambda: nc.scalar.copy(out=dstv, in_=srcv), reads=[bps], writes=[bvT])
                for r in range(d):
                    vcache = {}
                    for mt in range(M // 128):
                        m0 = mt * 128
                        kb = 1024 // d + m0 - 64
                        kcs = [r * Me + kb + 128 * j for j in range(2)]
                        vts = []
                        for j in range(2):
                            if kcs[j] in vcache:
                                vts.append(vcache[kcs[j]])
                                continue
                            fw.op(fw.pe, lambda: nc.tensor.transpose(self.ptr[:, 0:128], vT[:, kcs[j]:kcs[j] + 128], self.ident[:]),
                                  reads=[bvT, self.b_ident], writes=[self.b_ptr])
                            vt, bvt = vt_r.next()
                            fw.op(fw.act, lambda: nc.scalar.copy(out=vt[:], in_=self.ptr[:, 0:128]), reads=[self.b_ptr], writes=[bvt])
                            vcache[kcs[j]] = (vt, bvt)
                            vts.append((vt, bvt))
                        vcache = {kcs[1]: vts[1]}
                        ps, bps = self.bank()
                        qsl = qn[:, r * M + m0:r * M + m0 + 128]

                        def sc():
                            inst = None
                            for j in range(2):
                                nc.tensor.matmul(ps[:, j * 128:(j + 1) * 128], lhsT=kn[:, kcs[j]:kcs[j] + 128], rhs=qsl,
                                                 start=True, stop=False)
                                inst = nc.tensor.matmul(ps[:, j * 128:(j + 1) * 128], lhsT=self.ident[:],
                                                        rhs=self.biasT[:, (grp * 4 + i) * 2 + j, :], start=False, stop=True)
                            return inst
                        fw.op(fw.pe, sc, reads=[bkn, bqn, self.b_ident, self.b_biasT], writes=[bps])
                        pt, bpt = pt_r.next()
                        for j in range(2):
                            col = grp * 64 + kcs[j] // 64
                            fw.op(fw.act, lambda: nc.scalar.activation(out=pt[:, j * 128:(j + 1) * 128], in_=ps[:, j * 128:(j + 1) * 128],
                                                                       func=ACTF.Exp, bias=self.kv[:, kind, col:col + 1], scale=1.0),
                                  reads=[bps, self.b_kv], writes=[bpt])
                        ps2, bps2 = self.bank()

                        def pv():
                            inst = None
                            for j in range(2):
                                nc.tensor.matmul(ps2[:, 0:128], lhsT=vts[j][0][:], rhs=pt[:, j * 128:(j + 1) * 128],
                                                 start=(j == 0), stop=(j == 1))
                            for j in range(2):
                                inst = nc.tensor.matmul(ps2[:, 128:256], lhsT=self.ones[:], rhs=pt[:, j * 128:(j + 1) * 128],
                                                        start=(j == 0), stop=(j == 1))
                            return inst
                        fw.op(fw.pe, pv, reads=[vts[0][1], vts[1][1], bpt, self.b_ones], writes=[bps2])
                        ndv = nd[:].rearrange("p a (m r) -> p a m r", r=d)[:, :, m0:m0 + 128, r]
                        src = ps2[:, 0:256].rearrange("p (a m) -> p a m", a=2)
                        if grp == 0:
                            fw.op(fw.dve, lambda: nc.vector.tensor_copy(out=ndv, in_=src), reads=[bps2], writes=[b_nd])
                        else:
                            fw.op(fw.dve, lambda: nc.vector.tensor_tensor(out=ndv, in0=ndv, in1=src, op=ALU.add),
                                  reads=[bps2, b_nd], writes=[b_nd])
            fw.op(fw.dve, lambda: nc.vector.reciprocal(out=nd[:, 1, :], in_=nd[:, 1, :]), reads=[b_nd], writes=[b_nd])
            fw.op(fw.dve, lambda: nc.vector.tensor_tensor(out=self.attnT[:, i, :], in0=nd[:, 0, :], in1=nd[:, 1, :], op=ALU.mult),
                  reads=[b_nd], writes=[self.b_attnT])

    def qk_norm_perm(self, ps, bps, gcol, dst_tile, bdst, rb, d, Mtot, mstart):
        n = 512 // d
        dst = dst_tile[:].rearrange("p (r m) -> p r m", r=d)[:, :, mstart:mstart + n]
        nc, fw = self.nc, self.fw
        sq, bsq = rb["sq5"].next()
        rs, brs = rb["rs5"].next()
        fw.op(fw.act, lambda: nc.scalar.activation(out=sq[:], in_=ps[:], func=ACTF.Square), reads=[bps], writes=[bsq])
        ps2, bps2 = self.bank()
        self.mm_group(ps2[:], [(self.ones[:], sq[:])], reads=[self.b_ones, bsq], writes=[bps2])
        fw.op(fw.act, lambda: nc.scalar.activation(out=rs[:], in_=ps2[:], func=ACTF.Sqrt, scale=1.0, bias=128.0 * EPS),
              reads=[bps2], writes=[brs])
        fw.op(fw.dve, lambda: nc.vector.reciprocal(out=rs[:], in_=rs[:]), reads=[brs], writes=[brs])
        fw.op(fw.dve, lambda: nc.vector.scalar_tensor_tensor(out=dst, in0=ps[:].rearrange("p (m r) -> p r m", r=d), scalar=gcol,
                                                             in1=rs[:].rearrange("p (m r) -> p r m", r=d),
                                                             op0=ALU.mult, op1=ALU.mult),
              reads=[bps, brs, self.b_gqk], writes=[bdst])

    class _NS:
        pass

    def c_alloc(self, st, sample=False):
        nc = self.nc
        c = KB._NS()
        c.u, c.b_u = self.sb(st, "ubuf", [128, L + 2], BF16)
        c.zTf, c.b_zTf = self.sb(st, "zTf", [128, 4, L], BF16)
        c.x1T, c.b_x1T = self.sb(st, "x1T", [128, 4, L], BF16)
        c.ztok, c.b_ztok = self.sb(st, "ztok", [128, 16, 512], BF16)
        c.w_r = Ring(nc, st, "wh", 2, [128, 8, 128], BF16)
        c.F_r = Ring(nc, st, "cF", 2, [128, 16, 128], BF16)
        if not sample:
            c.H_r = Ring(nc, st, "cH", 4, [128, 512], BF16)
        c.t_r = Ring(nc, st, "ct", 3 if sample else 4, [128, 512], F32)
        return c

    def c_alloc_inv(self, c, st):
        c.Y, c.b_Y = self.sb(st, "Yspec", [128, 32, 512], BF16)
        c.G_r = Ring(self.nc, st, "cG", 2, [128, 2, 512], BF16)

    def c_project(self, c, kinds, hf):
        nc, fw = self.nc, self.fw
        u, b_u, t_r = c.u, c.b_u, c.t_r
        for kind in kinds:
            for cc in range(4):
                jcol = kind * 8 + hf * 4 + cc
                col0 = C_HY + jcol * 128
                wt, bw = c.w_r.next()
                self.load_w(wt[:], bw, self.w_in[:, col0:col0 + 128], 8)
                for tg in range(4):
                    ps, bps = self.bank()
                    self.mm_group(ps[:], [(wt[:, k, :], self.hT[:, k, tg * 512:(tg + 1) * 512]) for k in range(8)],
                                  reads=[bw, self.b_hT], writes=[bps])
                    fw.op(fw.act, lambda: nc.scalar.copy(out=u[:, 1 + tg * 512:1 + (tg + 1) * 512], in_=ps[:]),
                          reads=[bps], writes=[b_u])
                ps, bps = self.bank()
                self.mm_group(ps[:, 0:2], [(wt[:, k, :], self.edge[:, k, 0:2]) for k in range(8)],
                              reads=[bw, self.b_edge], writes=[bps])
                fw.op(fw.act, lambda: nc.scalar.copy(out=u[:, 0:1], in_=ps[:, 0:1]), reads=[bps], writes=[b_u])
                fw.op(fw.act, lambda: nc.scalar.copy(out=u[:, L + 1:L + 2], in_=ps[:, 1:2]), reads=[bps], writes=[b_u])
                if kind == 0:
                    dst, bdst = c.zTf[:, cc, :], c.b_zTf
                elif kind == 1:
                    dst, bdst = c.x1T[:, cc, :], c.b_x1T
                else:
                    dst, bdst = self.zfT[:, hf * 4 + cc, :], self.b_zfT
                for tg in range(4):
                    t, bt = t_r.next()
                    o = tg * 512
                    fw.op(fw.dve, lambda: nc.vector.tensor_scalar(out=t[:], in0=u[:, 1 + o:513 + o], scalar1=self.wsc[:, 1, jcol:jcol + 1],
                                                                  scalar2=self.wsc[:, 3, jcol:jcol + 1], op0=ALU.mult, op1=ALU.add),
                          reads=[b_u, self.b_wsc], writes=[bt])
                    fw.op(fw.dve, lambda: nc.vector.scalar_tensor_tensor(out=t[:], in0=u[:, o:512 + o], scalar=self.wsc[:, 0, jcol:jcol + 1],
                                                                         in1=t[:], op0=ALU.mult, op1=ALU.add),
                          reads=[b_u, self.b_wsc, bt], writes=[bt])
                    fw.op(fw.dve, lambda: nc.vector.scalar_tensor_tensor(out=dst[:, o:o + 512], in0=u[:, 2 + o:514 + o],
                                                                         scalar=self.wsc[:, 2, jcol:jcol + 1], in1=t[:],
                                                                         op0=ALU.mult, op1=ALU.add),
                          reads=[b_u, self.b_wsc, bt], writes=[bdst])

    def c_totok(self, c):
        nc, fw = self.nc, self.fw
        for tt in range(16):
            def tr():
                inst = None
                for cc in range(4):
                    inst = nc.tensor.transpose(self.ptr[:, cc * 128:(cc + 1) * 128], c.zTf[:, cc, tt * 128:(tt + 1) * 128], self.ident[:])
                return inst
            fw.op(fw.pe, tr, reads=[c.b_zTf, self.b_ident], writes=[self.b_ptr])
            fw.op(fw.act, lambda: nc.scalar.copy(out=c.ztok[:, tt, :], in_=self.ptr[:, 0:512]), reads=[self.b_ptr], writes=[c.b_ztok])

    def c_fwd(self, c, sink):
        fw = self.fw
        for m in range(16):
            Fre, bFre = c.F_r.next()
            fw.dma(fw.q_sp, Fre[:], self.Ftab[m, :, 0:16, :], writes=[bFre])
            psr, bpsr = self.bank()
            self.mm_group(psr[:], [(Fre[:, a, :], c.ztok[:, a, :]) for a in range(16)], reads=[bFre, c.b_ztok], writes=[bpsr])
            Fim, bFim = c.F_r.next()
            fw.dma(fw.q_sp, Fim[:], self.Ftab[16 + m, :, 0:16, :], writes=[bFim])
            psi, bpsi = self.bank()
            self.mm_group(psi[:], [(Fim[:, a, :], c.ztok[:, a, :]) for a in range(16)], reads=[bFim, c.b_ztok], writes=[bpsi])
            sink(m, psr, bpsr, psi, bpsi)

    def c_sink_mul(self, c, slot, hf):
        nc, fw = self.nc, self.fw
        Y, b_Y = c.Y, c.b_Y
        TT = nc.vector.tensor_tensor

        def sink(m, psr, bpsr, psi, bpsi):
            Hre, bHre = c.H_r.next()
            fw.dma(fw.q_sp, Hre[:], self.Hs[slot][hf, m, 0, :, 0:512], reads=[self.b_Hs], writes=[bHre])
            Him, bHim = c.H_r.next()
            fw.dma(fw.q_sp, Him[:], self.Hs[slot][hf, m, 1, :, 0:512], reads=[self.b_Hs], writes=[bHim])
            if m == 0:
                fw.dma(fw.q_sp, Him[0:1, :], self.Hs[slot][hf, 0, 1, 0:1, 512:1024], reads=[self.b_Hs, bHim], writes=[bHim])
            t1, bt1 = c.t_r.next()
            t2, bt2 = c.t_r.next()
            fw.op(fw.dve, lambda: TT(out=t1[:], in0=psr[:], in1=Hre[:], op=ALU.mult), reads=[bpsr, bHre], writes=[bt1])
            fw.op(fw.dve, lambda: TT(out=t2[:], in0=psi[:], in1=Him[:], op=ALU.mult), reads=[bpsi, bHim], writes=[bt2])
            fw.op(fw.dve, lambda: TT(out=Y[:, m, :], in0=t1[:], in1=t2[:], op=ALU.subtract), reads=[bt1, bt2], writes=[b_Y])
            t3, bt3 = c.t_r.next()
            t4, bt4 = c.t_r.next()
            fw.op(fw.dve, lambda: TT(out=t3[:], in0=psr[:], in1=Him[:], op=ALU.mult), reads=[bpsr, bHim], writes=[bt3])
            fw.op(fw.dve, lambda: TT(out=t4[:], in0=psi[:], in1=Hre[:], op=ALU.mult), reads=[bpsi, bHre], writes=[bt4])
            fw.op(fw.dve, lambda: TT(out=Y[:, 16 + m, :], in0=t3[:], in1=t4[:], op=ALU.add), reads=[bt3, bt4], writes=[b_Y])
            if m == 0:
                fw.op(fw.dve, lambda: TT(out=Y[0:1, 0, :], in0=psr[0:1, :], in1=Hre[0:1, :], op=ALU.mult),
                      reads=[bpsr, bHre], writes=[b_Y])
                fw.op(fw.dve, lambda: TT(out=Y[0:1, 16, :], in0=psi[0:1, :], in1=Him[0:1, :], op=ALU.mult),
                      reads=[bpsi, bHim], writes=[b_Y])
        return sink

    def c_sink_store(self, c, xst_r, dst, bdst):
        nc, fw = self.nc, self.fw

        def sink(m, psr, bpsr, psi, bpsi):
            xs, bxs = xst_r.next()
            fw.op(fw.act, lambda: nc.scalar.copy(out=xs[:, 0:512], in_=psr[:]), reads=[bpsr], writes=[bxs])
            fw.op(fw.act, lambda: nc.scalar.copy(out=xs[:, 512:1024], in_=psi[:]), reads=[bpsi], writes=[bxs])
            fw.dma(fw.q_sp, dst[m], xs[:], reads=[bxs], writes=[bdst])
        return sink

    def c_mac(self, c, s3, slots, xsd, b_xsd, hf):
        nc, fw = self.nc, self.fw
        Y, b_Y = c.Y, c.b_Y
        for m in range(16):
            psr, bpsr = self.bank()
            psi, bpsi = self.bank()
            for J in range(8):
                X, bX = s3.X_r.next()
                fw.dma(fw.q_sp, X[:], xsd[J, m], reads=[b_xsd], writes=[bX])
                HA, bHA = s3.H_r.next()
                fw.dma(fw.q_sp, HA[:], self.Hs[slots[J]][hf, m, 0], reads=[self.b_Hs], writes=[bHA])
                HB, bHB = s3.H_r.next()
                fw.dma(fw.q_sp, HB[:], self.Hs[slots[J]][hf, m, 1], reads=[self.b_Hs], writes=[bHB])
                pA, bpA = s3.P_r.next()
                pB, bpB = s3.P_r.next()
                fw.op(fw.dve, lambda: nc.vector.tensor_tensor(out=pA[:], in0=X[:], in1=HA[:], op=ALU.mult), reads=[bX, bHA], writes=[bpA])
                if J % 2 == 0:
                    fw.op(fw.pool, lambda: nc.gpsimd.tensor_tensor(out=pB[:], in0=X[:], in1=HB[:], op=ALU.mult), reads=[bX, bHB], writes=[bpB])
                else:
                    fw.op(fw.dve, lambda: nc.vector.tensor_tensor(out=pB[:], in0=X[:], in1=HB[:], op=ALU.mult), reads=[bX, bHB], writes=[bpB])

                def acc():
                    nc.tensor.matmul(psr[:], lhsT=self.ident[:], rhs=pA[:, 0:512], start=(J == 0), stop=False)
                    nc.tensor.matmul(psr[:], lhsT=self.ident[:], rhs=pA[:, 512:1024], start=False, stop=(J == 7))
                    nc.tensor.matmul(psi[:], lhsT=self.ident[:], rhs=pB[:, 0:512], start=(J == 0), stop=False)
                    return nc.tensor.matmul(psi[:], lhsT=self.ident[:], rhs=pB[:, 512:1024], start=False, stop=(J == 7))
                fw.op(fw.pe, acc, reads=[bpA, bpB, self.b_ident], writes=[bpsr, bpsi])
            fw.op(fw.act, lambda: nc.scalar.copy(out=Y[:, m, :], in_=psr[:]), reads=[bpsr], writes=[b_Y])
            fw.op(fw.act, lambda: nc.scalar.copy(out=Y[:, 16 + m, :], in_=psi[:]), reads=[bpsi], writes=[b_Y])

    def c_inverse(self, c, order, hf):
        nc, fw = self.nc, self.fw
        Y, b_Y = c.Y, c.b_Y
        for tg in range(4):
            banks = [self.bank() for _ in range(4)]
            tsl = slice(tg * 512, (tg + 1) * 512)
            for fp in range(16):
                Gp, bG = c.G_r.next()
                fw.dma(fw.q_sp, Gp[:], self.Gtab[tg, fp * 2:(fp + 1) * 2].rearrange("f p n -> p f n"), writes=[bG])
                for fi in range(2):
                    f = fp * 2 + fi
                    for cc in range(4):
                        self.mm_group(banks[cc][0][:], [(Y[:, f, cc * 128:(cc + 1) * 128], Gp[:, fi, :])],
                                      reads=[b_Y, bG], writes=[banks[cc][1]], start=(f == 0), stop=(f == 31))
            for cc in range(4):
                ps, bps = banks[cc]
                t, bt = c.t_r.next()
                jc = hf * 4 + cc
                fw.op(fw.dve, lambda: nc.vector.scalar_tensor_tensor(out=t[:], in0=c.zTf[:, cc, tsl], scalar=self.skc[:, order, jc:jc + 1],
                                                                     in1=ps[:], op0=ALU.mult, op1=ALU.add),
                      reads=[c.b_zTf, self.b_skc, bps], writes=[bt])
                if order == 0:
                    fw.op(fw.dve, lambda: nc.vector.tensor_tensor(out=c.zTf[:, cc, tsl], in0=t[:], in1=c.x1T[:, cc, tsl], op=ALU.mult),
                          reads=[bt, c.b_x1T], writes=[c.b_zTf])
                else:
                    fw.op(fw.dve, lambda: nc.vector.tensor_tensor(out=self.zfT[:, jc, tsl], in0=t[:], in1=self.zfT[:, jc, tsl], op=ALU.mult),
                          reads=[bt, self.b_zfT], writes=[self.b_zfT])

    def stage_C(self, job, st):
        c = self.c_alloc(st)
        self.c_alloc_inv(c, st)
        for hf in range(2):
            self.c_project(c, (0, 1, 2), hf)
            for order in range(2):
                self.c_totok(c)
                self.c_fwd(c, self.c_sink_mul(c, order, hf))
                self.c_inverse(c, order, hf)

    def stage_C_sample(self, job, st):
        nc, fw = self.nc, self.fw
        c = self.c_alloc(st, sample=True)
        oh, b_oh = self.sb(st, "onehot", [128, 8], F32)
        fw.dma(fw.q_sp, oh[:], self.onehotd[:, :], writes=[b_oh])
        for hf in range(2):
            self.c_project(c, (2,), hf)
        with ExitStack() as s1:
            rings = {
                "x": Ring(nc, s1, "xr", 2, [128, D], F32),
                "sq": Ring(nc, s1, "sqr", 2, [128, D], BF16),
                "col": Ring(nc, s1, "colr", 4, [128, 4], F32),
                "hn": Ring(nc, s1, "hnr", 2, [128, D], BF16),
            }
            self.load_gain(s1, "mix")
            xst_r = Ring(nc, s1, "xst", 2, [128, 1024], BF16)
            etmp, b_etmp = self.sb(s1, "etmp", [128, 8, 128], BF16)
            for K in range(8):
                for t in range(16):
                    xt, bx = rings["x"].next()
                    fw.dma(fw.q_sp, xt[:], self.xfull[K, t * 128:(t + 1) * 128, :], writes=[bx])
                    self.rmsnorm_T(rings, xt[:], bx, self.gmix, self.b_gmix, self.hT[:, :, t * 128:(t + 1) * 128], self.b_hT)
                xt, bx = rings["x"].next()
                fw.dma(fw.q_sp, xt[:], self.xedge[K], writes=[bx])
                self.rmsnorm_T(rings, xt[:], bx, self.gmix, self.b_gmix, etmp[:], b_etmp)
                fw.op(fw.dve, lambda: nc.vector.tensor_copy(out=self.edge[:], in_=etmp[:, :, 0:2]), reads=[b_etmp], writes=[self.b_edge])
                for hf in range(2):
                    self.c_project(c, (0, 1), hf)
                    fw.dma(fw.q_sp, self.zs[hf, K], c.zTf[:], reads=[c.b_zTf], writes=[self.b_zs])
                    fw.dma(fw.q_sp, self.x1s[hf, K], c.x1T[:], reads=[c.b_x1T], writes=[self.b_x1s])
                    self.c_totok(c)
                    self.c_fwd(c, self.c_sink_store(c, xst_r, self.xs0[hf, K], self.b_xs0))
            fw.barrier()
        with ExitStack() as s3_:
            self.c_alloc_inv(c, s3_)
            s3 = KB._NS()
            s3.X_r = Ring(nc, s3_, "XJ", 2, [128, 1024], BF16)
            s3.H_r = Ring(nc, s3_, "HAB", 4, [128, 1024], BF16)
            s3.P_r = Ring(nc, s3_, "PAB", 2, [128, 1024], BF16)
            xst_r = Ring(nc, s3_, "xst3", 1, [128, 1024], BF16)
            for hf in range(2):
                for Jb in range(8):
                    self.c_mac(c, s3, [2 + (Jb - K + 7) for K in range(8)], self.xs0[hf], self.b_xs0, hf)
                    fw.dma(fw.q_sp, c.zTf[:], self.zs[hf, Jb], reads=[self.b_zs], writes=[c.b_zTf])
                    fw.dma(fw.q_sp, c.x1T[:], self.x1s[hf, Jb], reads=[self.b_x1s], writes=[c.b_x1T])
                    self.c_inverse(c, 0, hf)
                    fw.dma(fw.q_sp, self.z1s[hf, Jb], c.zTf[:], reads=[c.b_zTf], writes=[self.b_z1s])
                    self.c_totok(c)
                    self.c_fwd(c, self.c_sink_store(c, xst_r, self.xs1[hf, Jb], self.b_xs1))
                self.c_mac(c, s3, [17 + J for J in range(8)], self.xs1[hf], self.b_xs1, hf)
                fw.op(fw.dve, lambda: nc.vector.memset(c.zTf[:], 0.0), writes=[c.b_zTf])
                for J in range(8):
                    fw.dma(fw.q_sp, c.x1T[:], self.z1s[hf, J], reads=[self.b_z1s], writes=[c.b_x1T])
                    fw.op(fw.dve, lambda: nc.vector.scalar_tensor_tensor(out=c.zTf[:], in0=c.x1T[:], scalar=oh[:, J:J + 1], in1=c.zTf[:],
                                                                         op0=ALU.mult, op1=ALU.add),
                          reads=[c.b_x1T, c.b_zTf, b_oh], writes=[c.b_zTf])
                self.c_inverse(c, 1, hf)
            fw.barrier()

    def stage_D(self, job, rings, st):
        nc, fw = self.nc, self.fw
        mT, b_mT = self.sb(st, "mergedT", [128, 8, 512], BF16)
        x2, b_x2 = self.sb(st, "x2", [128, 4, D], F32)
        hmT, b_hmT = self.sb(st, "hmT", [128, 8, 512], BF16)
        aT, b_aT = self.sb(st, "aT", [128, 32, 512], BF16)
        w8 = Ring(nc, st, "w8_", 3, [128, 8, 128], BF16)
        wrow = Ring(nc, st, "wrow_", 3, [128, 1, D], BF16)
        gt = Ring(nc, st, "gt_", 4, [128, 512], F32)
        yt = Ring(nc, st, "yt_", 2, [128, D], F32)
        for tg in range(4):
            tsl = slice(tg * 512, (tg + 1) * 512)
            for c in range(8):
                gs = []
                for gi in range(2):
                    wt, bw = w8.next()
                    col0 = C_G + gi * D + c * 128
                    self.load_w(wt[:], bw, self.w_in[:, col0:col0 + 128], 8)
                    ps, bps = self.bank()
                    self.mm_group(ps[:], [(wt[:, k, :], self.hT[:, k, tsl]) for k in range(8)],
                                  reads=[bw, self.b_hT], writes=[bps])
                    g, bg = gt.next()
                    fw.op(fw.act, lambda g=g, ps=ps: nc.scalar.activation(out=g[:], in_=ps[:], func=ACTF.Sigmoid),
                          reads=[bps], writes=[bg])
                    gs.append((g, bg))
                wt, bw = w8.next()
                self.load_w(wt[:, 0:4, :], bw, self.w_ab[:, c * 128:(c + 1) * 128], 4)
                ps, bps = self.bank()
                self.mm_group(ps[:], [(wt[:, k, :], self.attnT[:, k, tsl]) for k in range(4)],
                              reads=[bw, self.b_attnT], writes=[bps])
                g, bg = gs[0]
                fw.op(fw.dve, lambda g=g, ps=ps: nc.vector.tensor_tensor(out=g[:], in0=g[:], in1=ps[:], op=ALU.mult),
                      reads=[bps, bg], writes=[bg])
                wt, bw = w8.next()
                self.load_w(wt[:], bw, self.w_hb[:, c * 128:(c + 1) * 128], 8)
                ps, bps = self.bank()
                self.mm_group(ps[:], [(wt[:, k, :], self.zfT[:, k, tsl]) for k in range(8)],
                              reads=[bw, self.b_zfT], writes=[bps])
                g2, bg2 = gs[1]
                fw.op(fw.dve, lambda g2=g2, ps=ps: nc.vector.tensor_tensor(out=g2[:], in0=g2[:], in1=ps[:], op=ALU.mult),
                      reads=[bps, bg2], writes=[bg2])
                fw.op(fw.dve, lambda g=g, g2=g2, c=c: nc.vector.tensor_tensor(out=mT[:, c, :], in0=g[:], in1=g2[:], op=ALU.add),
                      reads=[bg, bg2], writes=[b_mT])
            for pss in range(2):
                banks = [[self.bank() for _ in range(2)] for _ in range(2)]
                for kc in range(8):
                    wt, bw = wrow.next()
                    self.load_w(wt[:], bw, self.w_out[kc * 128:(kc + 1) * 128, :], 1)
                    for tt in range(2):
                        tok = (pss * 2 + tt) * 128
                        for hf in range(2):
                            ps, bps = banks[tt][hf]
                            self.mm_group(ps[:], [(mT[:, kc, tok:tok + 128], wt[:, 0, hf * 512:(hf + 1) * 512])],
                                          reads=[bw, b_mT], writes=[bps], start=(kc == 0), stop=(kc == 7))
                for tt in range(2):
                    ti = pss * 2 + tt
                    gtok = tg * 512 + ti * 128
                    xt, bx = rings["x"].next()
                    fw.dma(fw.q_sp, xt[:], self.xm[job.idx, gtok:gtok + 128, :], writes=[bx])
                    for hf in range(2):
                        ps, bps = banks[tt][hf]
                        fw.op(fw.dve, lambda ps=ps, ti=ti, hf=hf, xt=xt: nc.vector.tensor_tensor(
                            out=x2[:, ti, hf * 512:(hf + 1) * 512], in0=xt[:, hf * 512:(hf + 1) * 512], in1=ps[:], op=ALU.add),
                            reads=[bps, bx], writes=[b_x2])
                    self.rmsnorm_T(rings, x2[:, ti, :], b_x2, self.gmlp, self.b_gmlp,
                                   hmT[:, :, ti * 128:(ti + 1) * 128], b_hmT)
            for f in range(32):
                wt, bw = w8.next()
                self.load_w(wt[:], bw, self.w_ff1[:, f * 128:(f + 1) * 128], 8)
                ps, bps = self.bank()
                self.mm_group(ps[:], [(wt[:, k, :], hmT[:, k, :]) for k in range(8)], reads=[bw, b_hmT], writes=[bps])
                g, bg = gt.next()
                fw.op(fw.act, lambda: nc.scalar.activation(out=g[:], in_=ps[:], func=ACTF.Relu), reads=[bps], writes=[bg])
                fw.op(fw.dve, lambda: nc.vector.tensor_tensor(out=aT[:, f, :], in0=g[:], in1=g[:], op=ALU.mult),
                      reads=[bg], writes=[b_aT])
            for pss in range(2):
                banks = [[self.bank() for _ in range(2)] for _ in range(2)]
                for f in range(32):
                    wt, bw = wrow.next()
                    self.load_w(wt[:], bw, self.w_ff2[f * 128:(f + 1) * 128, :], 1)
                    for tt in range(2):
                        tok = (pss * 2 + tt) * 128
                        for hf in range(2):
                            ps, bps = banks[tt][hf]
                            self.mm_group(ps[:], [(aT[:, f, tok:tok + 128], wt[:, 0, hf * 512:(hf + 1) * 512])],
                                          reads=[bw, b_aT], writes=[bps], start=(f == 0), stop=(f == 31))
                for tt in range(2):
                    ti = pss * 2 + tt
                    gtok = tg * 512 + ti * 128
                    o, bo = yt.next()
                    for hf in range(2):
                        ps, bps = banks[tt][hf]
                        fw.op(fw.dve, lambda ps=ps, ti=ti, hf=hf, o=o: nc.vector.tensor_tensor(
                            out=o[:, hf * 512:(hf + 1) * 512], in0=x2[:, ti, hf * 512:(hf + 1) * 512], in1=ps[:], op=ALU.add),
                            reads=[bps, b_x2], writes=[bo])
                    fw.dma(fw.q_sp, self.y[job.idx, gtok:gtok + 128, :], o[:], reads=[bo])


def build_program(njobs, dbg=(), stages="ABCD", fake_branches=False, sample_last=False, nslots=1):
    nc = bass.Bass("TRN2", target_bir_lowering=False)
    with ExitStack() as st:
        kb = KB(nc, st, njobs, dbg, nslots)
        fw = kb.fw
        if "C" in stages:
            with ExitStack() as s0:
                kb.init_hyena(s0)
                fw.barrier()
        kb.alloc_persistent()
        if "B" in stages:
            kb.init_attention()
        for j in range(njobs):
            job = Job(j, sample_last and j == njobs - 1)
            with ExitStack() as sa:
                rings = {
                    "x": Ring(nc, sa, "xr", 2, [128, D], F32),
                    "sq": Ring(nc, sa, "sqr", 2, [128, D], BF16),
                    "col": Ring(nc, sa, "colr", 4, [128, 4], F32),
                    "hn": Ring(nc, sa, "hnr", 2, [128, D], BF16),
                }
                kb.load_gain(sa, "mix")
                kb.stage_A(job, rings)
                fw.barrier()
                if fake_branches:
                    fw.op(fw.dve, lambda: nc.vector.tensor_copy(out=kb.attnT[:], in_=kb.hT[:, 0:4, :]), reads=[kb.b_hT], writes=[kb.b_attnT])
                    fw.op(fw.dve, lambda: nc.vector.tensor_copy(out=kb.zfT[:], in_=kb.hT[:]), reads=[kb.b_hT], writes=[kb.b_zfT])
                if "B" in stages:
                    with ExitStack() as sb_:
                        kb.stage_B(job, rings, sb_)
                        fw.barrier()
                    kb.dbg_dump("attnT", kb.attnT[:], kb.b_attnT, [128, 4, L], BF16)
            if "C" in stages:
                with ExitStack() as sc_:
                    if job.sample:
                        kb.stage_C_sample(job, sc_)
                    else:
                        kb.stage_C(job, sc_)
                    fw.barrier()
                if job.sample:
                    with ExitStack() as sa2:
                        rings = {
                            "x": Ring(nc, sa2, "xr", 2, [128, D], F32),
                            "sq": Ring(nc, sa2, "sqr", 2, [128, D], BF16),
                            "col": Ring(nc, sa2, "colr", 4, [128, 4], F32),
                            "hn": Ring(nc, sa2, "hnr", 2, [128, D], BF16),
                        }
                        kb.load_gain(sa2, "mix")
                        kb.stage_A(job, rings)
                        fw.barrier()
                kb.dbg_dump("zfT", kb.zfT[:], kb.b_zfT, [128, 8, L], BF16)
            if "D" in stages:
                with ExitStack() as sd:
                    rings = {
                        "x": Ring(nc, sd, "xr", 2, [128, D], F32),
                        "sq": Ring(nc, sd, "sqr", 2, [128, D], BF16),
                        "col": Ring(nc, sd, "colr", 4, [128, 4], F32),
                        "hn": Ring(nc, sd, "hnr", 2, [128, D], BF16),
                    }
                    kb.load_gain(sd, "mlp")
                    kb.stage_D(job, rings, sd)
                    fw.barrier()
        fw.barrier()
    return nc, kb


_TABLES = {}


def _sq(a):
    a = np.asarray(a)
    return np.ascontiguousarray(a[0]) if a.shape[0] == 1 else a


def kernel(**inputs):
    f32 = lambda a: np.ascontiguousarray(np.asarray(a), dtype=np.float32)
    xp = f32(inputs["x_prompt"])
    xs = f32(inputs["x_sample"])[0]
    njobs = 5
    if "dft" not in _TABLES:
        _TABLES["dft"] = dft_tables()
    Ftab, Gtab = _TABLES["dft"]
    shared = dict(
        w_in=f32(inputs["w_in"])[0], w_ab=f32(inputs["w_attn_branch"])[0], w_hb=f32(inputs["w_hyena_branch"])[0],
        w_out=f32(inputs["w_out"])[0], w_ff1=f32(inputs["w_ff1"])[0], w_ff2=f32(inputs["w_ff2"])[0],
        g_mix=f32(inputs["g_mix"])[0], g_mlp=f32(inputs["g_mlp"])[0], rel_bias=f32(inputs["rel_bias"]),
        g_q=f32(inputs["g_q"])[0], g_k=f32(inputs["g_k"])[0],
        filt_w1=f32(inputs["filt_w1"])[0], filt_b1=f32(inputs["filt_b1"])[0], filt_w2=f32(inputs["filt_w2"])[0],
        filt_b2=f32(inputs["filt_b2"])[0], filt_w3=f32(inputs["filt_w3"])[0], filt_b3=f32(inputs["filt_b3"])[0],
        filt_w4=f32(inputs["filt_w4"])[0], filt_freq=f32(inputs["filt_freq"])[0], filt_skip=f32(inputs["filt_skip"])[0],
        w_short=f32(inputs["w_short"])[0], b_short=f32(inputs["b_short"])[0], Ftab=Ftab, Gtab=Gtab)
    zf_sh, cols_sh, nad = filter_tables_shared()
    xfull = np.ascontiguousarray(xs.reshape(8, L, D))
    xedge = np.zeros((8, 128, D), np.float32)
    for K in range(8):
        if K > 0:
            xedge[K, 0] = xs[L * K - 1]
        if K < 7:
            xedge[K, 1] = xs[L * (K + 1)]
    shared.update(zfT_tab=zf_sh, fcols_tab=cols_sh, nad_tab=nad, xfull=xfull, xedge=xedge)
    in_maps = []
    for c in range(NCORES):
        xm = np.concatenate([xp[4 * c:4 * c + 4], xs[None, L * c:L * (c + 1)]], axis=0)
        xh = np.zeros((L, D), np.float32)
        if c > 0:
            xh[:1024] = xs[L * c - 1024:L * c]
        if c < NCORES - 1:
            xh[1024:] = xs[L * (c + 1):L * (c + 1) + 1024]
        zfc, colsc, oh = filter_tables_core(c)
        m = dict(shared)
        m.update(xm=np.ascontiguousarray(xm), xh=xh, zfT_core=zfc, fcols_core=colsc, onehot=oh)
        m.update(host_consts(c))
        in_maps.append(m)
    nc, kb = build_program(njobs, stages="ABCD", sample_last=True, nslots=25)
    res = run_bass_kernel_spmd(nc, in_maps, core_ids=list(range(NCORES)))
    y_prompt = np.empty((32, L, D), np.float32)
    y_sample = np.empty((1, 8 * L, D), np.float32)
    for c in range(NCORES):
        y = res.results[c]["y"]
        y_prompt[4 * c:4 * c + 4] = y[0:4]
        y_sample[0, L * c:L * (c + 1)] = y[4]
    return (y_prompt, y_sample)
```

```python
import math
from contextlib import ExitStack
import numpy as np
import ml_dtypes
import concourse.bass as bass
import concourse.mybir as mybir
from concourse.bass_utils import run_bass_kernel_spmd

F32 = mybir.dt.float32
BF16 = mybir.dt.bfloat16
ALU = mybir.AluOpType
ACTF = mybir.ActivationFunctionType

D = 1024
L = 2048
NCORES = 8
EPS = 1e-6
HD = 128
IN_W = 9728
C_Q, C_K, C_V, C_HY, C_G = 0, 1536, 3072, 4608, 7680
MASKV = -30000.0
GROUPS = ((128, 1), (512, 4), (2048, 16))


class Buf:
    __slots__ = ("name", "w", "r")

    def __init__(self, name=""):
        self.name = name
        self.w = None
        self.r = {}


class Eng:
    def __init__(self, name, h, sem):
        self.name, self.h, self.sem = name, h, sem
        self.count = 0
        self.seen = {}


class DmaQ:
    def __init__(self, name, eng, sems):
        self.name, self.eng, self.sems = name, eng, sems
        self.n = 0


class FW:
    def __init__(self, nc, stack, n_dma_sems=8):
        self.nc = nc
        S = lambda n: stack.enter_context(nc.semaphore(n))
        self.pe = Eng("pe", nc.tensor, S("s_pe"))
        self.act = Eng("act", nc.scalar, S("s_act"))
        self.dve = Eng("dve", nc.vector, S("s_dve"))
        self.pool = Eng("pool", nc.gpsimd, S("s_pool"))
        self.sp = Eng("sp", nc.sync, S("s_sp"))
        self.engs = [self.pe, self.act, self.dve, self.pool, self.sp]
        self.q_sp = DmaQ("qsp", self.sp, [S(f"s_qsp{i}") for i in range(n_dma_sems)])
        self.q_pool = DmaQ("qpl", self.pool, [S(f"s_qpl{i}") for i in range(n_dma_sems)])
        self.qs = [self.q_sp, self.q_pool]
        self.n_inst = 0

    def _wait(self, eng, key, sem, val):
        if eng.seen.get(key, 0) >= val:
            return
        eng.h.wait_ge(sem, val)
        eng.seen[key] = val
        self.n_inst += 1

    def _deps(self, eng, reads, writes):
        need = {}

        def add(m):
            if m is None:
                return
            k, s, v = m
            if k not in need or need[k][1] < v:
                need[k] = (s, v)

        for b in reads:
            add(b.w)
        for b in writes:
            add(b.w)
            for k, (s, v) in b.r.items():
                add((k, s, v))
        for k, (s, v) in need.items():
            if k == "pe" and eng is self.pe:
                continue
            self._wait(eng, k, s, v)

    def _mark(self, reads, writes, m):
        k, s, v = m
        for b in reads:
            b.r[k] = (s, v)
        for b in writes:
            b.w = m
            b.r = {}

    def op(self, eng, fn, reads=(), writes=()):
        self._deps(eng, reads, writes)
        inst = fn()
        eng.count += 1
        inst.then_inc(eng.sem, 1)
        m = (eng.name, eng.sem, eng.count)
        self._mark(reads, writes, m)
        self.n_inst += 1
        return m

    def dma(self, q, out, in_, reads=(), writes=(), **kw):
        eng = q.eng
        j = q.n
        ns = len(q.sems)
        sem = q.sems[j % ns]
        key = f"{q.name}{j % ns}"
        prev = 16 * (j // ns)
        if prev > 0:
            self._wait(eng, key, sem, prev)
        self._deps(eng, reads, writes)
        inst = eng.h.dma_start(out=out, in_=in_, **kw)
        inst.then_inc(sem, 16)
        q.n += 1
        m = (key, sem, prev + 16)
        self._mark(reads, writes, m)
        self.n_inst += 1
        return m

    def barrier(self):
        for e in self.engs:
            for o in self.engs:
                if o is e or o.count == 0:
                    continue
                self._wait(e, o.name, o.sem, o.count)
            for q in self.qs:
                ns = len(q.sems)
                for i in range(min(ns, q.n)):
                    last = ((q.n - 1 - i) // ns) * ns + i
                    self._wait(e, f"{q.name}{i}", q.sems[i], 16 * (last // ns + 1))


_UID = [0]


def _uid(name):
    _UID[0] += 1
    return f"{name}_{_UID[0]}"


class Ring:
    def __init__(self, nc, st, name, n, shape, dt):
        self.items = []
        for i in range(n):
            t = st.enter_context(nc.sbuf_tensor(_uid(f"{name}{i}"), shape, dt))
            self.items.append((t, Buf(f"{name}{i}")))
        self.i = 0

    def next(self):
        it = self.items[self.i % len(self.items)]
        self.i += 1
        return it


def _bf(a):
    return np.ascontiguousarray(a.astype(ml_dtypes.bfloat16))


def t5_bucket_np(rel):
    nb, max_exact = 16, 8
    side = np.where(rel > 0, nb, 0)
    n = np.abs(rel)
    nf = np.maximum(n, 1).astype(np.float32)
    large = max_exact + (np.log(nf / np.float32(max_exact)) / np.float32(math.log(1024 / max_exact))
                         * np.float32(nb - max_exact)).astype(np.int32)
    large = np.minimum(large, nb - 1)
    return side + np.where(n < max_exact, n, large)


def host_consts(core):
    c = {}
    c["ident"] = _bf(np.eye(128, dtype=np.float32))
    E = np.zeros((3, 33, 384), np.float32)
    for g, (_, d) in enumerate(GROUPS):
        for u in range(384):
            dl = u - 191
            if abs(dl) <= 64:
                E[g, int(t5_bucket_np(np.int32(dl * d))), u] = 1.0
            else:
                E[g, 32, u] = 1.0
    c["Esel"] = E
    kvt = np.zeros((2, 128, 192), np.float32)
    for kind in range(2):
        for g, (_, d) in enumerate(GROUPS):
            Me = 2 * L // d
            for cc in range(64):
                col = 64 * cc + np.arange(128)
                r = col // Me
                mp = col % Me
                e = r + d * mp
                if kind == 0:
                    ok = (e >= 1024) & (e < 3072)
                else:
                    gtok = 2048 * core + e - 1024
                    ok = (gtok >= 0) & (gtok < 16384)
                ok = ok & (col < 2 * L)
                kvt[kind, :, g * 64 + cc] = np.where(ok, 0.0, MASKV)
    c["kvalid"] = kvt
    return c


NFFT = 2 * L


def dft_tables():
    n = np.arange(NFFT, dtype=np.int64)
    k = np.arange(L, dtype=np.int64)
    ph = (np.outer(n, k) % NFFT).astype(np.float64) * (2 * np.pi / NFFT)
    Fre = np.cos(ph)
    Fim = -np.sin(ph)
    Fim[:, 0] = np.where(n % 2 == 0, 1.0, -1.0)
    Fall = np.concatenate([Fre, Fim], axis=1).astype(np.float32)
    Ftab = Fall.reshape(32, 128, 32, 128).transpose(2, 1, 0, 3)
    nn = np.arange(L, dtype=np.int64)
    ph2 = (np.outer(k, nn) % NFFT).astype(np.float64) * (2 * np.pi / NFFT)
    Gre = (2.0 / NFFT) * np.cos(ph2)
    Gre[0, :] = 1.0 / NFFT
    Gim = -(2.0 / NFFT) * np.sin(ph2)
    Gim[0, :] = np.where(nn % 2 == 0, 1.0, -1.0) / NFFT
    Gall = np.concatenate([Gre, Gim], axis=0).astype(np.float32)
    Gtab = Gall.reshape(32, 128, 4, 512).transpose(2, 0, 1, 3)
    return _bf(Ftab), _bf(Gtab)


SLOT_ORDERS = [0, 1] + [0] * 15 + [1] * 8
NSLOT_SH = 17


def _slot_rows(Ls, dd):
    r = np.arange(NFFT, dtype=np.int64)
    bands = np.linspace(1e-4, 15.0, 16, dtype=np.float32)[None, :]
    m = np.where(r < L, L * dd + r, L * dd + r - NFFT)
    pos = np.abs(m)
    ok = (pos < Ls) & (r != L)
    pos = np.where(ok, pos, 0)
    t = (pos.astype(np.float32) / np.float32(Ls - 1)).astype(np.float32)
    ang = (np.float32(2.0 * math.pi) * pos.astype(np.float32) / np.float32(Ls)).astype(np.float32)[:, None]
    z = np.concatenate([t[:, None], np.cos(bands * ang), -np.sin(bands * ang)], axis=1).astype(np.float32)
    ffwd = (ok & (m >= 0)).astype(np.float32)
    fbwd = (ok & (m < 0)).astype(np.float32)
    cols = np.zeros((128, 3, 32), np.float32)
    for a in range(32):
        cols[:, 0, a] = t[a * 128:(a + 1) * 128]
        cols[:, 1, a] = ffwd[a * 128:(a + 1) * 128]
        cols[:, 2, a] = fbwd[a * 128:(a + 1) * 128]
    return np.ascontiguousarray(z.T), cols


def filter_tables_shared():
    specs = [(L, 0), (L, 0)] + [(8 * L, d) for d in range(-7, 8)]
    rows = [_slot_rows(*sp) for sp in specs]
    zf = np.stack([r[0] for r in rows]).astype(np.float32)
    cols = np.stack([r[1] for r in rows]).astype(np.float32)
    min_decay = math.log(1e-2) / 1.5
    max_decay = math.log(1e-2) / 0.3
    nad = -np.abs(np.linspace(min_decay, max_decay, D, dtype=np.float32))
    return zf, cols, nad.astype(np.float32)


def filter_tables_core(core):
    rows = [_slot_rows(8 * L, core - J) for J in range(8)]
    zf = np.stack([r[0] for r in rows]).astype(np.float32)
    cols = np.stack([r[1] for r in rows]).astype(np.float32)
    oh = np.zeros((128, 8), np.float32)
    oh[:, core] = 1.0
    return zf, cols, oh


class Job:
    def __init__(self, idx, sample):
        self.idx, self.sample = idx, sample


class KB:
    def __init__(self, nc, st, njobs, dbg=(), nslots=1):
        self.nc = nc
        self.st = st
        self.fw = FW(nc, st)
        self.njobs = njobs
        self.dbg = set(dbg)
        self.dbg_out = {}
        di = lambda n, s, d=F32: nc.dram_tensor(n, s, d, kind="ExternalInput").ap()
        self.xm = di("xm", [njobs, L, D])
        self.xh = di("xh", [L, D])
        self.w_in = di("w_in", [D, IN_W])
        self.w_ab = di("w_ab", [512, D])
        self.w_hb = di("w_hb", [D, D])
        self.w_out = di("w_out", [D, D])
        self.w_ff1 = di("w_ff1", [D, 4 * D])
        self.w_ff2 = di("w_ff2", [4 * D, D])
        self.g_mix = di("g_mix", [D])
        self.g_mlp = di("g_mlp", [D])
        self.identd = di("ident", [128, 128], BF16)
        self.rel_bias = di("rel_bias", [32, 12])
        self.g_q = di("g_q", [128])
        self.g_k = di("g_k", [128])
        self.Esel = di("Esel", [3, 33, 384])
        self.kvd = di("kvalid", [2, 128, 192])
        self.bsc = nc.dram_tensor("bsc", [3, 12, 384], F32, kind="Internal").ap()
        self.nslots = nslots
        self.f_w1 = di("filt_w1", [33, 64]); self.f_b1 = di("filt_b1", [64])
        self.f_w2 = di("filt_w2", [64, 64]); self.f_b2 = di("filt_b2", [64])
        self.f_w3 = di("filt_w3", [64, 64]); self.f_b3 = di("filt_b3", [64])
        self.f_w4 = di("filt_w4", [64, 4096]); self.f_freq = di("filt_freq", [64])
        self.f_skip = di("filt_skip", [2, D])
        self.w_short = di("w_short", [3, 3 * D]); self.b_short = di("b_short", [3 * D])
        self.zfTd = di("zfT_tab", [NSLOT_SH, 33, NFFT])
        self.fcolsd = di("fcols_tab", [NSLOT_SH, 128, 3, 32])
        self.zfTc = di("zfT_core", [8, 33, NFFT])
        self.fcolsc = di("fcols_core", [8, 128, 3, 32])
        self.onehotd = di("onehot", [128, 8])
        self.xfull = di("xfull", [8, L, D])
        self.xedge = di("xedge", [8, 128, D])
        self.nadd = di("nad_tab", [D])
        self.Ftab = di("Ftab", [32, 128, 32, 128], BF16)
        self.Gtab = di("Gtab", [4, 32, 128, 512], BF16)
        self.Hs = [nc.dram_tensor(f"Hs{i}", [2, 16, 2, 128, 1024], BF16, kind="Internal").ap() for i in range(nslots)]
        self.b_Hs = Buf("Hs")
        dsc = lambda n, sh: nc.dram_tensor(n, sh, BF16, kind="Internal").ap()
        self.xs0, self.xs1 = dsc("xs0", [2, 8, 16, 128, 1024]), dsc("xs1", [2, 8, 16, 128, 1024])
        self.zs, self.x1s, self.z1s = dsc("zs", [2, 8, 128, 4, L]), dsc("x1s", [2, 8, 128, 4, L]), dsc("z1s", [2, 8, 128, 4, L])
        self.b_xs0, self.b_xs1, self.b_zs, self.b_x1s, self.b_z1s = [Buf(n) for n in ("xs0", "xs1", "zs", "x1s", "z1s")]
        self.y = nc.dram_tensor("y", [njobs, L, D], F32, kind="ExternalOutput").ap()
        self.pb = []
        for i in range(7):
            t = st.enter_context(nc.psum_tensor(f"pb{i}", [128, 512], F32))
            self.pb.append((t, Buf(f"pb{i}")))
        self.ptr = st.enter_context(nc.psum_tensor("ptr", [128, 1024], BF16))
        self.b_ptr = Buf("ptr")
        self.pbi = 0
        self.ident, self.b_ident = self.sb(st, "identb", [128, 128], BF16)
        self.fw.dma(self.fw.q_sp, self.ident[:], self.identd[:, :], writes=[self.b_ident])

    def sb(self, st, name, shape, dt):
        t = st.enter_context(self.nc.sbuf_tensor(_uid(name), shape, dt))
        return t, Buf(name)

    def alloc_persistent(self):
        nc, st = self.nc, self.st
        self.hT, self.b_hT = self.sb(st, "hT", [128, 8, L], BF16)
        self.attnT, self.b_attnT = self.sb(st, "attnT", [128, 4, L], BF16)
        self.zfT, self.b_zfT = self.sb(st, "zfT", [128, 8, L], BF16)
        self.edge, self.b_edge = self.sb(st, "edge", [128, 8, 2], BF16)
        fw = self.fw
        fw.op(fw.dve, lambda: nc.vector.memset(self.edge[:], 0.0), writes=[self.b_edge])
        self.wsc, self.b_wsc = self.sb(st, "wsc", [128, 4, 24], F32)
        for k in range(3):
            fw.dma(fw.q_sp, self.wsc[:, k, :], self.w_short[k, :].rearrange("(j p) -> p j", p=128), writes=[self.b_wsc],
                   allow_slow_non_contiguous=True)
        fw.dma(fw.q_sp, self.wsc[:, 3, :], self.b_short.rearrange("(j p) -> p j", p=128), writes=[self.b_wsc],
               allow_slow_non_contiguous=True)
        self.skc, self.b_skc = self.sb(st, "skc", [128, 2, 8], F32)
        for o in range(2):
            fw.dma(fw.q_sp, self.skc[:, o, :], self.f_skip[o, :].rearrange("(j p) -> p j", p=128), writes=[self.b_skc],
                   allow_slow_non_contiguous=True)

    def init_hyena(self, st):
        nc, fw = self.nc, self.fw
        TWO_PI = 2.0 * math.pi
        MAGIC = 12582912.0
        w1, b_w1 = self.sb(st, "fw1", [33, 64], F32)
        w2, b_w2 = self.sb(st, "fw2", [64, 64], F32)
        w3, b_w3 = self.sb(st, "fw3", [64, 64], F32)
        w4b, b_w4 = self.sb(st, "fw4", [64, 4096], BF16)
        frc, b_frc = self.sb(st, "frc", [64, 8], F32)
        nadt, b_nad = self.sb(st, "nadt", [128, D], F32)
        ktime, b_kt = self.sb(st, "ktime", [128, 32, 1024], BF16)
        sc, b_sc = self.sb(st, "fsc", [128, 3, 32], F32)
        fw.dma(fw.q_sp, w1[:], self.f_w1[:, :], writes=[b_w1])
        fw.dma(fw.q_sp, w2[:], self.f_w2[:, :], writes=[b_w2])
        fw.dma(fw.q_sp, w3[:], self.f_w3[:, :], writes=[b_w3])
        fw.dma(fw.q_pool, w4b[:], self.f_w4[:, :], writes=[b_w4])
        fw.dma(fw.q_sp, nadt[:], self.nadd.partition_broadcast(128), writes=[b_nad])
        col1 = lambda v: v.rearrange("(p o) -> p o", o=1)
        fw.dma(fw.q_sp, frc[:, 0:1], col1(self.f_freq), writes=[b_frc])
        for li, bb in enumerate((self.f_b1, self.f_b2, self.f_b3)):
            fw.dma(fw.q_sp, frc[:, 4 + li:5 + li], col1(bb), writes=[b_frc])
        for li in range(3):
            fw.op(fw.dve, lambda: nc.vector.tensor_scalar(out=frc[:, 1 + li:2 + li], in0=frc[:, 4 + li:5 + li], scalar1=frc[:, 0:1],
                                                          scalar2=None, op0=ALU.mult), reads=[b_frc], writes=[b_frc])
        Ws = [(w1, b_w1, 33), (w2, b_w2, 64), (w3, b_w3, 64)]
        zt_r = Ring(nc, st, "zt", 2, [33, 512], F32)
        a_r = Ring(nc, st, "fa", 3, [64, 512], F32)
        r_r = Ring(nc, st, "fr", 2, [64, 512], F32)
        h3_r = Ring(nc, st, "fh3", 2, [64, 512], BF16)
        dec_r = Ring(nc, st, "fdec", 2, [128, D], F32)
        t_r = Ring(nc, st, "ft", 3, [128, 512], F32)
        F_r = Ring(nc, st, "fF", 2, [128, 32, 128], BF16)
        ho_r = Ring(nc, st, "fho", 4, [128, 512], BF16)
        nq_r = Ring(nc, st, "fnq", 2, [1, 512], BF16)
        b_patch = Buf("hpatch")
        for si in range(self.nslots):
            so = SLOT_ORDERS[si]
            zsrc = self.zfTd[si] if si < NSLOT_SH else self.zfTc[si - NSLOT_SH]
            fw.dma(fw.q_sp, sc[:], self.fcolsd[si] if si < NSLOT_SH else self.fcolsc[si - NSLOT_SH], writes=[b_sc])
            for rg in range(8):
                zt, bzt = zt_r.next()
                fw.dma(fw.q_sp, zt[:], zsrc[:, rg * 512:(rg + 1) * 512], writes=[bzt])
                h, bh, K = zt, bzt, 33
                for li in range(3):
                    W, bW, K = Ws[li]
                    ps, bps = self.bank()
                    self.mm_group(ps[0:64, :], [(W[0:K, :], h[0:K, :])], reads=[bW, bh], writes=[bps])
                    a, ba = a_r.next()
                    fw.op(fw.dve, lambda: nc.vector.tensor_scalar(out=a[:], in0=ps[0:64, :], scalar1=frc[:, 0:1],
                                                                  scalar2=frc[:, 1 + li:2 + li], op0=ALU.mult, op1=ALU.add),
                          reads=[bps, b_frc], writes=[ba])
                    r, br = r_r.next()
                    fw.op(fw.dve, lambda: nc.vector.tensor_scalar(out=r[:], in0=a[:], scalar1=1.0 / TWO_PI, scalar2=MAGIC,
                                                                  op0=ALU.mult, op1=ALU.add), reads=[ba], writes=[br])
                    fw.op(fw.dve, lambda: nc.vector.tensor_scalar(out=r[:], in0=r[:], scalar1=MAGIC, scalar2=None,
                                                                  op0=ALU.subtract), reads=[br], writes=[br])
                    fw.op(fw.dve, lambda: nc.vector.scalar_tensor_tensor(out=a[:], in0=r[:], scalar=-TWO_PI, in1=a[:],
                                                                         op0=ALU.mult, op1=ALU.add), reads=[br, ba], writes=[ba])
                    if li < 2:
                        hn_, bhn_ = a_r.next()
                    else:
                        hn_, bhn_ = h3_r.next()
                    fw.op(fw.act, lambda: nc.scalar.activation(out=hn_[:], in_=a[:], func=ACTF.Sin), reads=[ba], writes=[bhn_])
                    h, bh = hn_, bhn_
                for ai in range(4):
                    aidx = rg * 4 + ai
                    dec, bdec = dec_r.next()
                    fw.op(fw.act, lambda: nc.scalar.activation(out=dec[:], in_=nadt[:], func=ACTF.Exp, scale=sc[:, 0, aidx:aidx + 1]),
                          reads=[b_nad, b_sc], writes=[bdec])
                    for o in (so,):
                        for hf in range(2):
                            psf, bpsf = self.bank()
                            c0 = o * 2048 + hf * 512
                            self.mm_group(psf[:], [(h[:, ai * 128:(ai + 1) * 128], w4b[:, c0:c0 + 512])], reads=[bh, b_w4], writes=[bpsf])
                            psb, bpsb = self.bank()
                            c1 = o * 2048 + 1024 + hf * 512
                            self.mm_group(psb[:], [(h[:, ai * 128:(ai + 1) * 128], w4b[:, c1:c1 + 512])], reads=[bh, b_w4], writes=[bpsb])
                            t, bt = t_r.next()
                            fw.op(fw.dve, lambda: nc.vector.tensor_scalar(out=t[:], in0=psf[:], scalar1=sc[:, 1, aidx:aidx + 1],
                                                                          scalar2=None, op0=ALU.mult), reads=[bpsf, b_sc], writes=[bt])
                            fw.op(fw.dve, lambda: nc.vector.scalar_tensor_tensor(out=t[:], in0=psb[:], scalar=sc[:, 2, aidx:aidx + 1], in1=t[:],
                                                                                 op0=ALU.mult, op1=ALU.add),
                                  reads=[bpsb, b_sc, bt], writes=[bt])
                            cg = hf
                            fw.op(fw.dve, lambda: nc.vector.tensor_tensor(out=ktime[:, aidx, cg * 512:(cg + 1) * 512], in0=t[:],
                                                                          in1=dec[:, hf * 512:(hf + 1) * 512], op=ALU.mult),
                                  reads=[bt, bdec], writes=[b_kt])
            for m in range(32):
                Ft, bF = F_r.next()
                fw.dma(fw.q_sp, Ft[:], self.Ftab[m], writes=[bF])
                for cg in range(2):
                    ps, bps = self.bank()
                    self.mm_group(ps[:], [(Ft[:, a, :], ktime[:, a, cg * 512:(cg + 1) * 512]) for a in range(32)],
                                  reads=[bF, b_kt], writes=[bps])
                    ho, bho = ho_r.next()
                    fw.op(fw.act, lambda: nc.scalar.copy(out=ho[:], in_=ps[:]), reads=[bps], writes=[bho])
                    if m < 16:
                        fw.dma(fw.q_sp, self.Hs[si][cg, m, 0, :, 0:512], ho[:], reads=[bho])
                        fw.dma(fw.q_sp, self.Hs[si][cg, m, 1, :, 512:1024], ho[:], reads=[bho], writes=[b_patch] if m == 0 else [])
                    else:
                        mm = m - 16
                        hn2, bhn2 = ho_r.next()
                        fw.op(fw.act, lambda: nc.scalar.mul(out=hn2[:], in_=ps[:], mul=-1.0), reads=[bps], writes=[bhn2])
                        if mm == 0:
                            nq, bnq = nq_r.next()
                            fw.op(fw.act, lambda: nc.scalar.copy(out=nq[:], in_=ho[0:1, :]), reads=[bho], writes=[bnq])
                            fw.op(fw.dve, lambda: nc.vector.memset(ho[0:1, :], 0.0), reads=[bnq], writes=[bho])
                            fw.op(fw.dve, lambda: nc.vector.memset(hn2[0:1, :], 0.0), writes=[bhn2])
                            fw.dma(fw.q_sp, self.Hs[si][cg, 0, 1, 0:1, 512:1024], nq[:], reads=[bnq, b_patch], writes=[b_patch])
                        fw.dma(fw.q_sp, self.Hs[si][cg, mm, 1, :, 0:512], ho[:], reads=[bho])
                        fw.dma(fw.q_sp, self.Hs[si][cg, mm, 0, :, 512:1024], hn2[:], reads=[bhn2])

    def init_attention(self):
        nc, fw, st = self.nc, self.fw, self.st
        self.ones, self.b_ones = self.sb(st, "onesb", [128, 128], BF16)
        fw.op(fw.dve, lambda: nc.vector.memset(self.ones[:], 1.0), writes=[self.b_ones])
        self.gqk, self.b_gqk = self.sb(st, "gqk", [128, 2], F32)
        fw.dma(fw.q_sp, self.gqk[:, 0:1], self.g_q.rearrange("(p o) -> p o", o=1), writes=[self.b_gqk])
        fw.dma(fw.q_sp, self.gqk[:, 1:2], self.g_k.rearrange("(p o) -> p o", o=1), writes=[self.b_gqk])
        fw.op(fw.dve, lambda: nc.vector.tensor_scalar(out=self.gqk[:, 1:2], in0=self.gqk[:, 1:2], scalar1=math.sqrt(128.0),
                                                      scalar2=None, op0=ALU.mult), reads=[self.b_gqk], writes=[self.b_gqk])
        self.kv, self.b_kv = self.sb(st, "kv", [128, 2, 192], F32)
        fw.dma(fw.q_sp, self.kv[:], self.kvd.rearrange("k p c -> p k c"), writes=[self.b_kv])
        self.biasd = nc.dram_tensor("biasd", [128, 24, 128], BF16)
        self.b_biasd = Buf("biasd")
        with ExitStack() as s2:
            self.biasT, self.b_biasT = self.sb(s2, "biasT0", [128, 24, 128], BF16)
            rb, b_rb = self.sb(s2, "rb", [33, 12], F32)
            es, b_es = self.sb(s2, "es", [33, 3, 384], F32)
            bv, b_bv = self.sb(s2, "bv", [12, 3, 384], F32)
            stg = Ring(nc, s2, "bstg", 2, [128, 128], F32)
            fw.op(fw.dve, lambda: nc.vector.memset(rb[32:33, :], MASKV), writes=[b_rb])
            fw.dma(fw.q_sp, rb[0:32, :], self.rel_bias[:, :], writes=[b_rb])
            fw.dma(fw.q_sp, es[:], self.Esel.rearrange("g b u -> b g u"), writes=[b_es])
            b_bsc = Buf("bsc")
            for g in range(3):
                ps, bps = self.bank()
                self.mm_group(ps[0:12, 0:384], [(rb[:, :], es[:, g, :])], reads=[b_rb, b_es], writes=[bps])
                fw.op(fw.dve, lambda: nc.vector.tensor_copy(out=bv[:, g, :], in_=ps[0:12, 0:384]), reads=[bps], writes=[b_bv])
            fw.dma(fw.q_sp, self.bsc.rearrange("g h u -> h g u"), bv[:], reads=[b_bv], writes=[b_bsc])
            for g in range(3):
                for i in range(4):
                    for j in range(2):
                        h = 4 * g + i
                        t, bt = stg.next()
                        src = bass.AP(self.bsc.tensor, (g * 12 + h) * 384 + 127 + 128 * j, [[1, 128], [-1, 128]])
                        fw.dma(fw.q_sp, t[:], src, reads=[b_bsc], writes=[bt], allow_slow_non_contiguous=True)
                        fw.op(fw.dve, lambda: nc.vector.tensor_copy(out=self.biasT[:, (g * 4 + i) * 2 + j, :], in_=t[:]),
                              reads=[bt], writes=[self.b_biasT])
            fw.dma(fw.q_sp, self.biasd[:, :, :], self.biasT[:], reads=[self.b_biasT], writes=[self.b_biasd])
            fw.barrier()

    def precast_weights(self, st):
        nc, fw = self.nc, self.fw
        specs = [("w_in", D, IN_W), ("w_ab", 512, D), ("w_hb", D, D), ("w_out", D, D), ("w_ff1", D, 4 * D), ("w_ff2", 4 * D, D)]
        stg = Ring(nc, st, "wcast", 2, [128, IN_W], BF16)
        for name, R, C in specs:
            src = getattr(self, name)
            dst = nc.dram_tensor(name + "_bf", [R, C], BF16, kind="Internal").ap()
            for rb in range(R // 128):
                t, bt = stg.next()
                fw.dma(fw.q_pool, t[:, 0:C], src[rb * 128:(rb + 1) * 128, :], writes=[bt])
                fw.dma(fw.q_sp, dst[rb * 128:(rb + 1) * 128, :], t[:, 0:C], reads=[bt])
            setattr(self, name, dst)

    def load_gain(self, st, which):
        t, b = self.sb(st, "gain_" + which, [128, D], F32)
        src = self.g_mix if which == "mix" else self.g_mlp
        self.fw.dma(self.fw.q_sp, t[:], src.partition_broadcast(128), writes=[b])
        if which == "mix":
            self.gmix, self.b_gmix = t, b
        else:
            self.gmlp, self.b_gmlp = t, b

    def bank(self):
        it = self.pb[self.pbi % len(self.pb)]
        self.pbi += 1
        return it

    def dbg_dump(self, name, tile_ap, buf, shape, dt=F32):
        if name not in self.dbg:
            return
        o = self.nc.dram_tensor("dbg_" + name, list(shape), dt, kind="ExternalOutput").ap()
        self.dbg_out[name] = o
        self.fw.dma(self.fw.q_sp, o, tile_ap, reads=[buf])

    def load_w(self, tile, buf, src2d, nk):
        fw = self.fw
        fw.dma(fw.q_pool, tile, src2d.rearrange("(k p) c -> p k c", p=128), writes=[buf])

    def mm_group(self, out_ap, pairs, reads, writes, start=True, stop=True):
        nc = self.nc
        n = len(pairs)

        def emit():
            inst = None
            for i, (l, r) in enumerate(pairs):
                inst = nc.tensor.matmul(out_ap, lhsT=l, rhs=r, start=(start and i == 0), stop=(stop and i == n - 1))
            return inst
        return self.fw.op(self.fw.pe, emit, reads=reads, writes=writes)

    def rmsnorm_T(self, st_ring, x_ap, bx, g_tile, bg, dst_ap, bdst):
        nc, fw = self.nc, self.fw
        sq, bsq = st_ring["sq"].next()
        col, bcol = st_ring["col"].next()
        hn, bhn = st_ring["hn"].next()
        fw.op(fw.act, lambda: nc.scalar.activation(out=sq[:], in_=x_ap, func=ACTF.Square, accum_out=col[:, 0:1]),
              reads=[bx], writes=[bsq, bcol])
        fw.op(fw.act, lambda: nc.scalar.activation(out=col[:, 1:2], in_=col[:, 0:1], func=ACTF.Sqrt, scale=1.0 / D, bias=EPS),
              reads=[bcol], writes=[bcol])
        fw.op(fw.dve, lambda: nc.vector.reciprocal(out=col[:, 2:3], in_=col[:, 1:2]), reads=[bcol], writes=[bcol])
        fw.op(fw.dve, lambda: nc.vector.scalar_tensor_tensor(out=hn[:], in0=x_ap, scalar=col[:, 2:3], in1=g_tile[:],
                                                             op0=ALU.mult, op1=ALU.mult),
              reads=[bx, bcol, bg], writes=[bhn])

        def tr():
            inst = None
            for k in range(8):
                inst = nc.tensor.transpose(self.ptr[:, k * 128:(k + 1) * 128], hn[:, k * 128:(k + 1) * 128], self.ident[:])
            return inst
        fw.op(fw.pe, tr, reads=[bhn, self.b_ident], writes=[self.b_ptr])
        fw.op(fw.act, lambda: nc.scalar.copy(out=dst_ap, in_=self.ptr[:].rearrange("p (k t) -> p k t", k=8)),
              reads=[self.b_ptr], writes=[bdst])

    def stage_A(self, job, rings):
        fw = self.fw
        for t in range(L // 128):
            xt, bx = rings["x"].next()
            fw.dma(fw.q_sp, xt[:], self.xm[job.idx, t * 128:(t + 1) * 128, :], writes=[bx])
            self.rmsnorm_T(rings, xt[:], bx, self.gmix, self.b_gmix,
                           self.hT[:, :, t * 128:(t + 1) * 128], self.b_hT)

    def qk_norm(self, ps, bps, gcol, dst, bdst, rb):
        nc, fw = self.nc, self.fw
        sq, bsq = rb["sq5"].next()
        rs, brs = rb["rs5"].next()
        fw.op(fw.act, lambda: nc.scalar.activation(out=sq[:], in_=ps[:], func=ACTF.Square), reads=[bps], writes=[bsq])
        ps2, bps2 = self.bank()
        self.mm_group(ps2[:], [(self.ones[:], sq[:])], reads=[self.b_ones, bsq], writes=[bps2])
        fw.op(fw.act, lambda: nc.scalar.activation(out=rs[:], in_=ps2[:], func=ACTF.Sqrt, scale=1.0, bias=128.0 * EPS),
              reads=[bps2], writes=[brs])
        fw.op(fw.dve, lambda: nc.vector.reciprocal(out=rs[:], in_=rs[:]), reads=[brs], writes=[brs])
        fw.op(fw.dve, lambda: nc.vector.scalar_tensor_tensor(out=dst, in0=ps[:], scalar=gcol, in1=rs[:],
                                                             op0=ALU.mult, op1=ALU.mult),
              reads=[bps, brs, self.b_gqk], writes=[bdst])

    def stage_B(self, job, rings, st):
        nc, fw = self.nc, self.fw
        kind = 1 if job.sample else 0
        hTh, b_hTh = self.sb(st, "hTh", [128, 8, 2048], BF16)
        self.biasT, self.b_biasT = self.sb(st, "biasT", [128, 24, 128], BF16)
        fw.dma(fw.q_sp, self.biasT[:], self.biasd[:, :, :], reads=[self.b_biasd], writes=[self.b_biasT])
        if job.sample:
            for t in range(16):
                xt, bx = rings["x"].next()
                fw.dma(fw.q_sp, xt[:], self.xh[t * 128:(t + 1) * 128, :], writes=[bx])
                self.rmsnorm_T(rings, xt[:], bx, self.gmix, self.b_gmix, hTh[:, :, t * 128:(t + 1) * 128], b_hTh)
            fw.op(fw.dve, lambda: nc.vector.tensor_copy(out=self.edge[:, :, 0:1], in_=hTh[:, :, 1023:1024]), reads=[b_hTh], writes=[self.b_edge])
            fw.op(fw.dve, lambda: nc.vector.tensor_copy(out=self.edge[:, :, 1:2], in_=hTh[:, :, 1024:1025]), reads=[b_hTh], writes=[self.b_edge])

        def hsrc(eg, k):
            if eg < 2:
                return hTh[:, k, eg * 512:(eg + 1) * 512], b_hTh
            if eg < 6:
                return self.hT[:, k, (eg - 2) * 512:(eg - 1) * 512], self.b_hT
            return hTh[:, k, 1024 + (eg - 6) * 512:1024 + (eg - 5) * 512], b_hTh

        wr = Ring(nc, st, "wqkv", 6, [128, 8, 128], BF16)
        qn_r = Ring(nc, st, "qn", 1, [128, L], BF16)
        kn_r = Ring(nc, st, "kn", 1, [128, 2 * L], BF16)
        vT_r = Ring(nc, st, "vT", 1, [128, 2 * L], BF16)
        nd, b_nd = self.sb(st, "nd", [128, 2, L], F32)
        pt_r = Ring(nc, st, "pt", 3, [128, 256], BF16)
        vt_r = Ring(nc, st, "vt", 4, [128, 128], BF16)
        rb = {"sq5": Ring(nc, st, "sq5", 2, [128, 512], BF16), "rs5": Ring(nc, st, "rs5", 2, [128, 512], F32)}
        for i in range(4):
            for grp in range(3):
                d = GROUPS[grp][1]
                h = 4 * grp + i
                M, Me = L // d, 2 * L // d
                ws = []
                for c0 in (C_Q, C_K, C_V):
                    wt, bw = wr.next()
                    self.load_w(wt[:], bw, self.w_in[:, c0 + h * 128:c0 + (h + 1) * 128], 8)
                    ws.append((wt, bw))
                qn, bqn = qn_r.next()
                kn, bkn = kn_r.next()
                vT, bvT = vT_r.next()
                if not job.sample:
                    fw.op(fw.dve, lambda: nc.vector.memset(kn[:], 0.0), writes=[bkn])
                    fw.op(fw.dve, lambda: nc.vector.memset(vT[:], 0.0), writes=[bvT])
                for tg in range(4):
                    ps, bps = self.bank()
                    self.mm_group(ps[:], [(ws[0][0][:, k, :], self.hT[:, k, tg * 512:(tg + 1) * 512]) for k in range(8)],
                                  reads=[ws[0][1], self.b_hT], writes=[bps])
                    if d == 1:
                        dst = qn[:, tg * 512:(tg + 1) * 512]
                        self.qk_norm(ps, bps, self.gqk[:, 0:1], dst, bqn, rb)
                    else:
                        self.qk_norm_perm(ps, bps, self.gqk[:, 0:1], qn, bqn, rb, d, M, tg * 512 // d)
                if job.sample:
                    egs = list(range(8)) if d == 16 else list(range(1, 7))
                else:
                    egs = [2, 3, 4, 5]
                for eg in egs:
                    ps, bps = self.bank()
                    prs, rd = [], [ws[1][1]]
                    for k in range(8):
                        a, ba = hsrc(eg, k)
                        prs.append((ws[1][0][:, k, :], a))
                    self.mm_group(ps[:], prs, reads=[ws[1][1], ba], writes=[bps])
                    if d == 1:
                        self.qk_norm(ps, bps, self.gqk[:, 1:2], kn[:, eg * 512:(eg + 1) * 512], bkn, rb)
                    else:
                        self.qk_norm_perm(ps, bps, self.gqk[:, 1:2], kn, bkn, rb, d, Me, eg * 512 // d)
                    ps, bps = self.bank()
                    prs = []
                    for k in range(8):
                        a, ba = hsrc(eg, k)
                        prs.append((ws[2][0][:, k, :], a))
                    self.mm_group(ps[:], prs, reads=[ws[2][1], ba], writes=[bps])
                    if d == 1:
                        dstv = vT[:, eg * 512:(eg + 1) * 512]
                        srcv = ps[:]
                    else:
                        dstv = vT[:].rearrange("p (r m) -> p r m", r=d)[:, :, eg * 512 // d:(eg + 1) * 512 // d]
                        srcv = ps[:].rearrange("p (m r) -> p r m", r=d)
                    fw.op(fw.act, lambda: nc.scalar.copy(out=dstv, in_=srcv), reads=[bps], writes=[bvT])
                for r in range(d):
                    vcache = {}
                    for mt in range(M // 128):
                        m0 = mt * 128
                        kb = 1024 // d + m0 - 64
                        kcs = [r * Me + kb + 128 * j for j in range(2)]
                        vts = []
                        for j in range(2):
                            if kcs[j] in vcache:
                                vts.append(vcache[kcs[j]])
                                continue
                            fw.op(fw.pe, lambda: nc.tensor.transpose(self.ptr[:, 0:128], vT[:, kcs[j]:kcs[j] + 128], self.ident[:]),
                                  reads=[bvT, self.b_ident], writes=[self.b_ptr])
                            vt, bvt = vt_r.next()
                            fw.op(fw.act, lambda: nc.scalar.copy(out=vt[:], in_=self.ptr[:, 0:128]), reads=[self.b_ptr], writes=[bvt])
                            vcache[kcs[j]] = (vt, bvt)
                            vts.append((vt, bvt))
                        vcache = {kcs[1]: vts[1]}
                        ps, bps = self.bank()
                        qsl = qn[:, r * M + m0:r * M + m0 + 128]

                        def sc():
                            inst = None
                            for j in range(2):
                                nc.tensor.matmul(ps[:, j * 128:(j + 1) * 128], lhsT=kn[:, kcs[j]:kcs[j] + 128], rhs=qsl,
                                                 start=True, stop=False)
                                inst = nc.tensor.matmul(ps[:, j * 128:(j + 1) * 128], lhsT=self.ident[:],
                                                        rhs=self.biasT[:, (grp * 4 + i) * 2 + j, :], start=False, stop=True)
                            return inst
                        fw.op(fw.pe, sc, reads=[bkn, bqn, self.b_ident, self.b_biasT], writes=[bps])
                        pt, bpt = pt_r.next()
                        for j in range(2):
                            col = grp * 64 + kcs[j] // 64
                            fw.op(fw.act, lambda: nc.scalar.activation(out=pt[:, j * 128:(j + 1) * 128], in_=ps[:, j * 128:(j + 1) * 128],
                                                                       func=ACTF.Exp, bias=self.kv[:, kind, col:col + 1], scale=1.0),
                                  reads=[bps, self.b_kv], writes=[bpt])
                        ps2, bps2 = self.bank()

                        def pv():
                            inst = None
                            for j in range(2):
                                nc.tensor.matmul(ps2[:, 0:128], lhsT=vts[j][0][:], rhs=pt[:, j * 128:(j + 1) * 128],
                                                 start=(j == 0), stop=(j == 1))
                            for j in range(2):
                                inst = nc.tensor.matmul(ps2[:, 128:256], lhsT=self.ones[:], rhs=pt[:, j * 128:(j + 1) * 128],
                                                        start=(j == 0), stop=(j == 1))
                            return inst
                        fw.op(fw.pe, pv, reads=[vts[0][1], vts[1][1], bpt, self.b_ones], writes=[bps2])
                        ndv = nd[:].rearrange("p a (m r) -> p a m r", r=d)[:, :, m0:m0 + 128, r]
                        src = ps2[:, 0:256].rearrange("p (a m) -> p a m", a=2)
                        if grp == 0:
                            fw.op(fw.dve, lambda: nc.vector.tensor_copy(out=ndv, in_=src), reads=[bps2], writes=[b_nd])
                        else:
                            fw.op(fw.dve, lambda: nc.vector.tensor_tensor(out=ndv, in0=ndv, in1=src, op=ALU.add),
                                  reads=[bps2, b_nd], writes=[b_nd])
            fw.op(fw.dve, lambda: nc.vector.reciprocal(out=nd[:, 1, :], in_=nd[:, 1, :]), reads=[b_nd], writes=[b_nd])
            fw.op(fw.dve, lambda: nc.vector.tensor_tensor(out=self.attnT[:, i, :], in0=nd[:, 0, :], in1=nd[:, 1, :], op=ALU.mult),
                  reads=[b_nd], writes=[self.b_attnT])

    def qk_norm_perm(self, ps, bps, gcol, dst_tile, bdst, rb, d, Mtot, mstart):
        n = 512 // d
        dst = dst_tile[:].rearrange("p (r m) -> p r m", r=d)[:, :, mstart:mstart + n]
        nc, fw = self.nc, self.fw
        sq, bsq = rb["sq5"].next()
        rs, brs = rb["rs5"].next()
        fw.op(fw.act, lambda: nc.scalar.activation(out=sq[:], in_=ps[:], func=ACTF.Square), reads=[bps], writes=[bsq])
        ps2, bps2 = self.bank()
        self.mm_group(ps2[:], [(self.ones[:], sq[:])], reads=[self.b_ones, bsq], writes=[bps2])
        fw.op(fw.act, lambda: nc.scalar.activation(out=rs[:], in_=ps2[:], func=ACTF.Sqrt, scale=1.0, bias=128.0 * EPS),
              reads=[bps2], writes=[brs])
        fw.op(fw.dve, lambda: nc.vector.reciprocal(out=rs[:], in_=rs[:]), reads=[brs], writes=[brs])
        fw.op(fw.dve, lambda: nc.vector.scalar_tensor_tensor(out=dst, in0=ps[:].rearrange("p (m r) -> p r m", r=d), scalar=gcol,
                                                             in1=rs[:].rearrange("p (m r) -> p r m", r=d),
                                                             op0=ALU.mult, op1=ALU.mult),
              reads=[bps, brs, self.b_gqk], writes=[bdst])

    class _NS:
        pass

    def c_alloc(self, st, sample=False):
        nc = self.nc
        c = KB._NS()
        c.u, c.b_u = self.sb(st, "ubuf", [128, L + 2], BF16)
        c.zTf, c.b_zTf = self.sb(st, "zTf", [128, 4, L], BF16)
        c.x1T, c.b_x1T = self.sb(st, "x1T", [128, 4, L], BF16)
        c.ztok, c.b_ztok = self.sb(st, "ztok", [128, 16, 512], BF16)
        c.w_r = Ring(nc, st, "wh", 2, [128, 8, 128], BF16)
        c.F_r = Ring(nc, st, "cF", 2, [128, 16, 128], BF16)
        if not sample:
            c.H_r = Ring(nc, st, "cH", 4, [128, 512], BF16)
        c.t_r = Ring(nc, st, "ct", 3 if sample else 4, [128, 512], F32)
        return c

    def c_alloc_inv(self, c, st):
        c.Y, c.b_Y = self.sb(st, "Yspec", [128, 32, 512], BF16)
        c.G_r = Ring(self.nc, st, "cG", 2, [128, 2, 512], BF16)

    def c_project(self, c, kinds, hf):
        nc, fw = self.nc, self.fw
        u, b_u, t_r = c.u, c.b_u, c.t_r
        for kind in kinds:
            for cc in range(4):
                jcol = kind * 8 + hf * 4 + cc
                col0 = C_HY + jcol * 128
                wt, bw = c.w_r.next()
                self.load_w(wt[:], bw, self.w_in[:, col0:col0 + 128], 8)
                for tg in range(4):
                    ps, bps = self.bank()
                    self.mm_group(ps[:], [(wt[:, k, :], self.hT[:, k, tg * 512:(tg + 1) * 512]) for k in range(8)],
                                  reads=[bw, self.b_hT], writes=[bps])
                    fw.op(fw.act, lambda: nc.scalar.copy(out=u[:, 1 + tg * 512:1 + (tg + 1) * 512], in_=ps[:]),
                          reads=[bps], writes=[b_u])
                ps, bps = self.bank()
                self.mm_group(ps[:, 0:2], [(wt[:, k, :], self.edge[:, k, 0:2]) for k in range(8)],
                              reads=[bw, self.b_edge], writes=[bps])
                fw.op(fw.act, lambda: nc.scalar.copy(out=u[:, 0:1], in_=ps[:, 0:1]), reads=[bps], writes=[b_u])
                fw.op(fw.act, lambda: nc.scalar.copy(out=u[:, L + 1:L + 2], in_=ps[:, 1:2]), reads=[bps], writes=[b_u])
                if kind == 0:
                    dst, bdst = c.zTf[:, cc, :], c.b_zTf
                elif kind == 1:
                    dst, bdst = c.x1T[:, cc, :], c.b_x1T
                else:
                    dst, bdst = self.zfT[:, hf * 4 + cc, :], self.b_zfT
                for tg in range(4):
                    t, bt = t_r.next()
                    o = tg * 512
                    fw.op(fw.dve, lambda: nc.vector.tensor_scalar(out=t[:], in0=u[:, 1 + o:513 + o], scalar1=self.wsc[:, 1, jcol:jcol + 1],
                                                                  scalar2=self.wsc[:, 3, jcol:jcol + 1], op0=ALU.mult, op1=ALU.add),
                          reads=[b_u, self.b_wsc], writes=[bt])
                    fw.op(fw.dve, lambda: nc.vector.scalar_tensor_tensor(out=t[:], in0=u[:, o:512 + o], scalar=self.wsc[:, 0, jcol:jcol + 1],
                                                                         in1=t[:], op0=ALU.mult, op1=ALU.add),
                          reads=[b_u, self.b_wsc, bt], writes=[bt])
                    fw.op(fw.dve, lambda: nc.vector.scalar_tensor_tensor(out=dst[:, o:o + 512], in0=u[:, 2 + o:514 + o],
                                                                         scalar=self.wsc[:, 2, jcol:jcol + 1], in1=t[:],
                                                                         op0=ALU.mult, op1=ALU.add),
                          reads=[b_u, self.b_wsc, bt], writes=[bdst])

    def c_totok(self, c):
        nc, fw = self.nc, self.fw
        for tt in range(16):
            def tr():
                inst = None
                for cc in range(4):
                    inst = nc.tensor.transpose(self.ptr[:, cc * 128:(cc + 1) * 128], c.zTf[:, cc, tt * 128:(tt + 1) * 128], self.ident[:])
                return inst
            fw.op(fw.pe, tr, reads=[c.b_zTf, self.b_ident], writes=[self.b_ptr])
            fw.op(fw.act, lambda: nc.scalar.copy(out=c.ztok[:, tt, :], in_=self.ptr[:, 0:512]), reads=[self.b_ptr], writes=[c.b_ztok])

    def c_fwd(self, c, sink):
        fw = self.fw
        for m in range(16):
            Fre, bFre = c.F_r.next()
            fw.dma(fw.q_sp, Fre[:], self.Ftab[m, :, 0:16, :], writes=[bFre])
            psr, bpsr = self.bank()
            self.mm_group(psr[:], [(Fre[:, a, :], c.ztok[:, a, :]) for a in range(16)], reads=[bFre, c.b_ztok], writes=[bpsr])
            Fim, bFim = c.F_r.next()
            fw.dma(fw.q_sp, Fim[:], self.Ftab[16 + m, :, 0:16, :], writes=[bFim])
            psi, bpsi = self.bank()
            self.mm_group(psi[:], [(Fim[:, a, :], c.ztok[:, a, :]) for a in range(16)], reads=[bFim, c.b_ztok], writes=[bpsi])
            sink(m, psr, bpsr, psi, bpsi)

    def c_sink_mul(self, c, slot, hf):
        nc, fw = self.nc, self.fw
        Y, b_Y = c.Y, c.b_Y
        TT = nc.vector.tensor_tensor

        def sink(m, psr, bpsr, psi, bpsi):
            Hre, bHre = c.H_r.next()
            fw.dma(fw.q_sp, Hre[:], self.Hs[slot][hf, m, 0, :, 0:512], reads=[self.b_Hs], writes=[bHre])
            Him, bHim = c.H_r.next()
            fw.dma(fw.q_sp, Him[:], self.Hs[slot][hf, m, 1, :, 0:512], reads=[self.b_Hs], writes=[bHim])
            if m == 0:
                fw.dma(fw.q_sp, Him[0:1, :], self.Hs[slot][hf, 0, 1, 0:1, 512:1024], reads=[self.b_Hs, bHim], writes=[bHim])
            t1, bt1 = c.t_r.next()
            t2, bt2 = c.t_r.next()
            fw.op(fw.dve, lambda: TT(out=t1[:], in0=psr[:], in1=Hre[:], op=ALU.mult), reads=[bpsr, bHre], writes=[bt1])
            fw.op(fw.dve, lambda: TT(out=t2[:], in0=psi[:], in1=Him[:], op=ALU.mult), reads=[bpsi, bHim], writes=[bt2])
            fw.op(fw.dve, lambda: TT(out=Y[:, m, :], in0=t1[:], in1=t2[:], op=ALU.subtract), reads=[bt1, bt2], writes=[b_Y])
            t3, bt3 = c.t_r.next()
            t4, bt4 = c.t_r.next()
            fw.op(fw.dve, lambda: TT(out=t3[:], in0=psr[:], in1=Him[:], op=ALU.mult), reads=[bpsr, bHim], writes=[bt3])
            fw.op(fw.dve, lambda: TT(out=t4[:], in0=psi[:], in1=Hre[:], op=ALU.mult), reads=[bpsi, bHre], writes=[bt4])
            fw.op(fw.dve, lambda: TT(out=Y[:, 16 + m, :], in0=t3[:], in1=t4[:], op=ALU.add), reads=[bt3, bt4], writes=[b_Y])
            if m == 0:
                fw.op(fw.dve, lambda: TT(out=Y[0:1, 0, :], in0=psr[0:1, :], in1=Hre[0:1, :], op=ALU.mult),
                      reads=[bpsr, bHre], writes=[b_Y])
                fw.op(fw.dve, lambda: TT(out=Y[0:1, 16, :], in0=psi[0:1, :], in1=Him[0:1, :], op=ALU.mult),
                      reads=[bpsi, bHim], writes=[b_Y])
        return sink

    def c_sink_store(self, c, xst_r, dst, bdst):
        nc, fw = self.nc, self.fw

        def sink(m, psr, bpsr, psi, bpsi):
            xs, bxs = xst_r.next()
            fw.op(fw.act, lambda: nc.scalar.copy(out=xs[:, 0:512], in_=psr[:]), reads=[bpsr], writes=[bxs])
            fw.op(fw.act, lambda: nc.scalar.copy(out=xs[:, 512:1024], in_=psi[:]), reads=[bpsi], writes=[bxs])
            fw.dma(fw.q_sp, dst[m], xs[:], reads=[bxs], writes=[bdst])
        return sink

    def c_mac(self, c, s3, slots, xsd, b_xsd, hf):
        nc, fw = self.nc, self.fw
        Y, b_Y = c.Y, c.b_Y
        for m in range(16):
            psr, bpsr = self.bank()
            psi, bpsi = self.bank()
            for J in range(8):
                X, bX = s3.X_r.next()
                fw.dma(fw.q_sp, X[:], xsd[J, m], reads=[b_xsd], writes=[bX])
                HA, bHA = s3.H_r.next()
                fw.dma(fw.q_sp, HA[:], self.Hs[slots[J]][hf, m, 0], reads=[self.b_Hs], writes=[bHA])
                HB, bHB = s3.H_r.next()
                fw.dma(fw.q_sp, HB[:], self.Hs[slots[J]][hf, m, 1], reads=[self.b_Hs], writes=[bHB])
                pA, bpA = s3.P_r.next()
                pB, bpB = s3.P_r.next()
                fw.op(fw.dve, lambda: nc.vector.tensor_tensor(out=pA[:], in0=X[:], in1=HA[:], op=ALU.mult), reads=[bX, bHA], writes=[bpA])
                if J % 2 == 0:
                    fw.op(fw.pool, lambda: nc.gpsimd.tensor_tensor(out=pB[:], in0=X[:], in1=HB[:], op=ALU.mult), reads=[bX, bHB], writes=[bpB])
                else:
                    fw.op(fw.dve, lambda: nc.vector.tensor_tensor(out=pB[:], in0=X[:], in1=HB[:], op=ALU.mult), reads=[bX, bHB], writes=[bpB])

                def acc():
                    nc.tensor.matmul(psr[:], lhsT=self.ident[:], rhs=pA[:, 0:512], start=(J == 0), stop=False)
                    nc.tensor.matmul(psr[:], lhsT=self.ident[:], rhs=pA[:, 512:1024], start=False, stop=(J == 7))
                    nc.tensor.matmul(psi[:], lhsT=self.ident[:], rhs=pB[:, 0:512], start=(J == 0), stop=False)
                    return nc.tensor.matmul(psi[:], lhsT=self.ident[:], rhs=pB[:, 512:1024], start=False, stop=(J == 7))
                fw.op(fw.pe, acc, reads=[bpA, bpB, self.b_ident], writes=[bpsr, bpsi])
            fw.op(fw.act, lambda: nc.scalar.copy(out=Y[:, m, :], in_=psr[:]), reads=[bpsr], writes=[b_Y])
            fw.op(fw.act, lambda: nc.scalar.copy(out=Y[:, 16 + m, :], in_=psi[:]), reads=[bpsi], writes=[b_Y])

    def c_inverse(self, c, order, hf):
        nc, fw = self.nc, self.fw
        Y, b_Y = c.Y, c.b_Y
        for tg in range(4):
            banks = [self.bank() for _ in range(4)]
            tsl = slice(tg * 512, (tg + 1) * 512)
            for fp in range(16):
                Gp, bG = c.G_r.next()
                fw.dma(fw.q_sp, Gp[:], self.Gtab[tg, fp * 2:(fp + 1) * 2].rearrange("f p n -> p f n"), writes=[bG])
                for fi in range(2):
                    f = fp * 2 + fi
                    for cc in range(4):
                        self.mm_group(banks[cc][0][:], [(Y[:, f, cc * 128:(cc + 1) * 128], Gp[:, fi, :])],
                                      reads=[b_Y, bG], writes=[banks[cc][1]], start=(f == 0), stop=(f == 31))
            for cc in range(4):
                ps, bps = banks[cc]
                t, bt = c.t_r.next()
                jc = hf * 4 + cc
                fw.op(fw.dve, lambda: nc.vector.scalar_tensor_tensor(out=t[:], in0=c.zTf[:, cc, tsl], scalar=self.skc[:, order, jc:jc + 1],
                                                                     in1=ps[:], op0=ALU.mult, op1=ALU.add),
                      reads=[c.b_zTf, self.b_skc, bps], writes=[bt])
                if order == 0:
                    fw.op(fw.dve, lambda: nc.vector.tensor_tensor(out=c.zTf[:, cc, tsl], in0=t[:], in1=c.x1T[:, cc, tsl], op=ALU.mult),
                          reads=[bt, c.b_x1T], writes=[c.b_zTf])
                else:
                    fw.op(fw.dve, lambda: nc.vector.tensor_tensor(out=self.zfT[:, jc, tsl], in0=t[:], in1=self.zfT[:, jc, tsl], op=ALU.mult),
                          reads=[bt, self.b_zfT], writes=[self.b_zfT])

    def stage_C(self, job, st):
        c = self.c_alloc(st)
        self.c_alloc_inv(c, st)
        for hf in range(2):
            self.c_project(c, (0, 1, 2), hf)
            for order in range(2):
                self.c_totok(c)
                self.c_fwd(c, self.c_sink_mul(c, order, hf))
                self.c_inverse(c, order, hf)

    def stage_C_sample(self, job, st):
        nc, fw = self.nc, self.fw
        c = self.c_alloc(st, sample=True)
        oh, b_oh = self.sb(st, "onehot", [128, 8], F32)
        fw.dma(fw.q_sp, oh[:], self.onehotd[:, :], writes=[b_oh])
        for hf in range(2):
            self.c_project(c, (2,), hf)
        with ExitStack() as s1:
            rings = {
                "x": Ring(nc, s1, "xr", 2, [128, D], F32),
                "sq": Ring(nc, s1, "sqr", 2, [128, D], BF16),
                "col": Ring(nc, s1, "colr", 4, [128, 4], F32),
                "hn": Ring(nc, s1, "hnr", 2, [128, D], BF16),
            }
            self.load_gain(s1, "mix")
            xst_r = Ring(nc, s1, "xst", 2, [128, 1024], BF16)
            etmp, b_etmp = self.sb(s1, "etmp", [128, 8, 128], BF16)
            for K in range(8):
                for t in range(16):
                    xt, bx = rings["x"].next()
                    fw.dma(fw.q_sp, xt[:], self.xfull[K, t * 128:(t + 1) * 128, :], writes=[bx])
                    self.rmsnorm_T(rings, xt[:], bx, self.gmix, self.b_gmix, self.hT[:, :, t * 128:(t + 1) * 128], self.b_hT)
                xt, bx = rings["x"].next()
                fw.dma(fw.q_sp, xt[:], self.xedge[K], writes=[bx])
                self.rmsnorm_T(rings, xt[:], bx, self.gmix, self.b_gmix, etmp[:], b_etmp)
                fw.op(fw.dve, lambda: nc.vector.tensor_copy(out=self.edge[:], in_=etmp[:, :, 0:2]), reads=[b_etmp], writes=[self.b_edge])
                for hf in range(2):
                    self.c_project(c, (0, 1), hf)
                    fw.dma(fw.q_sp, self.zs[hf, K], c.zTf[:], reads=[c.b_zTf], writes=[self.b_zs])
                    fw.dma(fw.q_sp, self.x1s[hf, K], c.x1T[:], reads=[c.b_x1T], writes=[self.b_x1s])
                    self.c_totok(c)
                    self.c_fwd(c, self.c_sink_store(c, xst_r, self.xs0[hf, K], self.b_xs0))
            fw.barrier()
        with ExitStack() as s3_:
            self.c_alloc_inv(c, s3_)
            s3 = KB._NS()
            s3.X_r = Ring(nc, s3_, "XJ", 2, [128, 1024], BF16)
            s3.H_r = Ring(nc, s3_, "HAB", 4, [128, 1024], BF16)
            s3.P_r = Ring(nc, s3_, "PAB", 2, [128, 1024], BF16)
            xst_r = Ring(nc, s3_, "xst3", 1, [128, 1024], BF16)
            for hf in range(2):
                for Jb in range(8):
                    self.c_mac(c, s3, [2 + (Jb - K + 7) for K in range(8)], self.xs0[hf], self.b_xs0, hf)
                    fw.dma(fw.q_sp, c.zTf[:], self.zs[hf, Jb], reads=[self.b_zs], writes=[c.b_zTf])
                    fw.dma(fw.q_sp, c.x1T[:], self.x1s[hf, Jb], reads=[self.b_x1s], writes=[c.b_x1T])
                    self.c_inverse(c, 0, hf)
                    fw.dma(fw.q_sp, self.z1s[hf, Jb], c.zTf[:], reads=[c.b_zTf], writes=[self.b_z1s])
                    self.c_totok(c)
                    self.c_fwd(c, self.c_sink_store(c, xst_r, self.xs1[hf, Jb], self.b_xs1))
                self.c_mac(c, s3, [17 + J for J in range(8)], self.xs1[hf], self.b_xs1, hf)
                fw.op(fw.dve, lambda: nc.vector.memset(c.zTf[:], 0.0), writes=[c.b_zTf])
                for J in range(8):
                    fw.dma(fw.q_sp, c.x1T[:], self.z1s[hf, J], reads=[self.b_z1s], writes=[c.b_x1T])
                    fw.op(fw.dve, lambda: nc.vector.scalar_tensor_tensor(out=c.zTf[:], in0=c.x1T[:], scalar=oh[:, J:J + 1], in1=c.zTf[:],
                                                                         op0=ALU.mult, op1=ALU.add),
                          reads=[c.b_x1T, c.b_zTf, b_oh], writes=[c.b_zTf])
                self.c_inverse(c, 1, hf)
            fw.barrier()

    def stage_D(self, job, rings, st):
        nc, fw = self.nc, self.fw
        mT, b_mT = self.sb(st, "mergedT", [128, 8, 512], BF16)
        x2, b_x2 = self.sb(st, "x2", [128, 4, D], F32)
        hmT, b_hmT = self.sb(st, "hmT", [128, 8, 512], BF16)
        aT, b_aT = self.sb(st, "aT", [128, 32, 512], BF16)
        w8 = Ring(nc, st, "w8_", 3, [128, 8, 128], BF16)
        wrow = Ring(nc, st, "wrow_", 3, [128, 1, D], BF16)
        gt = Ring(nc, st, "gt_", 4, [128, 512], F32)
        yt = Ring(nc, st, "yt_", 2, [128, D], F32)
        for tg in range(4):
            tsl = slice(tg * 512, (tg + 1) * 512)
            for c in range(8):
                gs = []
                for gi in range(2):
                    wt, bw = w8.next()
                    col0 = C_G + gi * D + c * 128
                    self.load_w(wt[:], bw, self.w_in[:, col0:col0 + 128], 8)
                    ps, bps = self.bank()
                    self.mm_group(ps[:], [(wt[:, k, :], self.hT[:, k, tsl]) for k in range(8)],
                                  reads=[bw, self.b_hT], writes=[bps])
                    g, bg = gt.next()
                    fw.op(fw.act, lambda g=g, ps=ps: nc.scalar.activation(out=g[:], in_=ps[:], func=ACTF.Sigmoid),
                          reads=[bps], writes=[bg])
                    gs.append((g, bg))
                wt, bw = w8.next()
                self.load_w(wt[:, 0:4, :], bw, self.w_ab[:, c * 128:(c + 1) * 128], 4)
                ps, bps = self.bank()
                self.mm_group(ps[:], [(wt[:, k, :], self.attnT[:, k, tsl]) for k in range(4)],
                              reads=[bw, self.b_attnT], writes=[bps])
                g, bg = gs[0]
                fw.op(fw.dve, lambda g=g, ps=ps: nc.vector.tensor_tensor(out=g[:], in0=g[:], in1=ps[:], op=ALU.mult),
                      reads=[bps, bg], writes=[bg])
                wt, bw = w8.next()
                self.load_w(wt[:], bw, self.w_hb[:, c * 128:(c + 1) * 128], 8)
                ps, bps = self.bank()
                self.mm_group(ps[:], [(wt[:, k, :], self.zfT[:, k, tsl]) for k in range(8)],
                              reads=[bw, self.b_zfT], writes=[bps])
                g2, bg2 = gs[1]
                fw.op(fw.dve, lambda g2=g2, ps=ps: nc.vector.tensor_tensor(out=g2[:], in0=g2[:], in1=ps[:], op=ALU.mult),
                      reads=[bps, bg2], writes=[bg2])
                fw.op(fw.dve, lambda g=g, g2=g2, c=c: nc.vector.tensor_tensor(out=mT[:, c, :], in0=g[:], in1=g2[:], op=ALU.add),
                      reads=[bg, bg2], writes=[b_mT])
            for pss in range(2):
                banks = [[self.bank() for _ in range(2)] for _ in range(2)]
                for kc in range(8):
                    wt, bw = wrow.next()
                    self.load_w(wt[:], bw, self.w_out[kc * 128:(kc + 1) * 128, :], 1)
                    for tt in range(2):
                        tok = (pss * 2 + tt) * 128
                        for hf in range(2):
                            ps, bps = banks[tt][hf]
                            self.mm_group(ps[:], [(mT[:, kc, tok:tok + 128], wt[:, 0, hf * 512:(hf + 1) * 512])],
                                          reads=[bw, b_mT], writes=[bps], start=(kc == 0), stop=(kc == 7))
                for tt in range(2):
                    ti = pss * 2 + tt
                    gtok = tg * 512 + ti * 128
                    xt, bx = rings["x"].next()
                    fw.dma(fw.q_sp, xt[:], self.xm[job.idx, gtok:gtok + 128, :], writes=[bx])
                    for hf in range(2):
                        ps, bps = banks[tt][hf]
                        fw.op(fw.dve, lambda ps=ps, ti=ti, hf=hf, xt=xt: nc.vector.tensor_tensor(
                            out=x2[:, ti, hf * 512:(hf + 1) * 512], in0=xt[:, hf * 512:(hf + 1) * 512], in1=ps[:], op=ALU.add),
                            reads=[bps, bx], writes=[b_x2])
                    self.rmsnorm_T(rings, x2[:, ti, :], b_x2, self.gmlp, self.b_gmlp,
                                   hmT[:, :, ti * 128:(ti + 1) * 128], b_hmT)
            for f in range(32):
                wt, bw = w8.next()
                self.load_w(wt[:], bw, self.w_ff1[:, f * 128:(f + 1) * 128], 8)
                ps, bps = self.bank()
                self.mm_group(ps[:], [(wt[:, k, :], hmT[:, k, :]) for k in range(8)], reads=[bw, b_hmT], writes=[bps])
                g, bg = gt.next()
                fw.op(fw.act, lambda: nc.scalar.activation(out=g[:], in_=ps[:], func=ACTF.Relu), reads=[bps], writes=[bg])
                fw.op(fw.dve, lambda: nc.vector.tensor_tensor(out=aT[:, f, :], in0=g[:], in1=g[:], op=ALU.mult),
                      reads=[bg], writes=[b_aT])
            for pss in range(2):
                banks = [[self.bank() for _ in range(2)] for _ in range(2)]
                for f in range(32):
                    wt, bw = wrow.next()
                    self.load_w(wt[:], bw, self.w_ff2[f * 128:(f + 1) * 128, :], 1)
                    for tt in range(2):
                        tok = (pss * 2 + tt) * 128
                        for hf in range(2):
                            ps, bps = banks[tt][hf]
                            self.mm_group(ps[:], [(aT[:, f, tok:tok + 128], wt[:, 0, hf * 512:(hf + 1) * 512])],
                                          reads=[bw, b_aT], writes=[bps], start=(f == 0), stop=(f == 31))
                for tt in range(2):
                    ti = pss * 2 + tt
                    gtok = tg * 512 + ti * 128
                    o, bo = yt.next()
                    for hf in range(2):
                        ps, bps = banks[tt][hf]
                        fw.op(fw.dve, lambda ps=ps, ti=ti, hf=hf, o=o: nc.vector.tensor_tensor(
                            out=o[:, hf * 512:(hf + 1) * 512], in0=x2[:, ti, hf * 512:(hf + 1) * 512], in1=ps[:], op=ALU.add),
                            reads=[bps, b_x2], writes=[bo])
                    fw.dma(fw.q_sp, self.y[job.idx, gtok:gtok + 128, :], o[:], reads=[bo])


def build_program(njobs, dbg=(), stages="ABCD", fake_branches=False, sample_last=False, nslots=1):
    nc = bass.Bass("TRN2", target_bir_lowering=False)
    with ExitStack() as st:
        kb = KB(nc, st, njobs, dbg, nslots)
        fw = kb.fw
        with ExitStack() as sw:
            kb.precast_weights(sw)
            fw.barrier()
        if "C" in stages:
            with ExitStack() as s0:
                kb.init_hyena(s0)
                fw.barrier()
        kb.alloc_persistent()
        if "B" in stages:
            kb.init_attention()
        for j in range(njobs):
            job = Job(j, sample_last and j == njobs - 1)
            with ExitStack() as sa:
                rings = {
                    "x": Ring(nc, sa, "xr", 2, [128, D], F32),
                    "sq": Ring(nc, sa, "sqr", 2, [128, D], BF16),
                    "col": Ring(nc, sa, "colr", 4, [128, 4], F32),
                    "hn": Ring(nc, sa, "hnr", 2, [128, D], BF16),
                }
                kb.load_gain(sa, "mix")
                kb.stage_A(job, rings)
                fw.barrier()
                if fake_branches:
                    fw.op(fw.dve, lambda: nc.vector.tensor_copy(out=kb.attnT[:], in_=kb.hT[:, 0:4, :]), reads=[kb.b_hT], writes=[kb.b_attnT])
                    fw.op(fw.dve, lambda: nc.vector.tensor_copy(out=kb.zfT[:], in_=kb.hT[:]), reads=[kb.b_hT], writes=[kb.b_zfT])
                if "B" in stages:
                    with ExitStack() as sb_:
                        kb.stage_B(job, rings, sb_)
                        fw.barrier()
                    kb.dbg_dump("attnT", kb.attnT[:], kb.b_attnT, [128, 4, L], BF16)
            if "C" in stages:
                with ExitStack() as sc_:
                    if job.sample:
                        kb.stage_C_sample(job, sc_)
                    else:
                        kb.stage_C(job, sc_)
                    fw.barrier()
                if job.sample:
                    with ExitStack() as sa2:
                        rings = {
                            "x": Ring(nc, sa2, "xr", 2, [128, D], F32),
                            "sq": Ring(nc, sa2, "sqr", 2, [128, D], BF16),
                            "col": Ring(nc, sa2, "colr", 4, [128, 4], F32),
                            "hn": Ring(nc, sa2, "hnr", 2, [128, D], BF16),
                        }
                        kb.load_gain(sa2, "mix")
                        kb.stage_A(job, rings)
                        fw.barrier()
                kb.dbg_dump("zfT", kb.zfT[:], kb.b_zfT, [128, 8, L], BF16)
            if "D" in stages:
                with ExitStack() as sd:
                    rings = {
                        "x": Ring(nc, sd, "xr", 2, [128, D], F32),
                        "sq": Ring(nc, sd, "sqr", 2, [128, D], BF16),
                        "col": Ring(nc, sd, "colr", 4, [128, 4], F32),
                        "hn": Ring(nc, sd, "hnr", 2, [128, D], BF16),
                    }
                    kb.load_gain(sd, "mlp")
                    kb.stage_D(job, rings, sd)
                    fw.barrier()
        fw.barrier()
    return nc, kb


_TABLES = {}


def _sq(a):
    a = np.asarray(a)
    return np.ascontiguousarray(a[0]) if a.shape[0] == 1 else a


def kernel(**inputs):
    f32 = lambda a: np.ascontiguousarray(np.asarray(a), dtype=np.float32)
    xp = f32(inputs["x_prompt"])
    xs = f32(inputs["x_sample"])[0]
    njobs = 5
    if "dft" not in _TABLES:
        _TABLES["dft"] = dft_tables()
    Ftab, Gtab = _TABLES["dft"]
    shared = dict(
        w_in=f32(inputs["w_in"])[0], w_ab=f32(inputs["w_attn_branch"])[0], w_hb=f32(inputs["w_hyena_branch"])[0],
        w_out=f32(inputs["w_out"])[0], w_ff1=f32(inputs["w_ff1"])[0], w_ff2=f32(inputs["w_ff2"])[0],
        g_mix=f32(inputs["g_mix"])[0], g_mlp=f32(inputs["g_mlp"])[0], rel_bias=f32(inputs["rel_bias"]),
        g_q=f32(inputs["g_q"])[0], g_k=f32(inputs["g_k"])[0],
        filt_w1=f32(inputs["filt_w1"])[0], filt_b1=f32(inputs["filt_b1"])[0], filt_w2=f32(inputs["filt_w2"])[0],
        filt_b2=f32(inputs["filt_b2"])[0], filt_w3=f32(inputs["filt_w3"])[0], filt_b3=f32(inputs["filt_b3"])[0],
        filt_w4=f32(inputs["filt_w4"])[0], filt_freq=f32(inputs["filt_freq"])[0], filt_skip=f32(inputs["filt_skip"])[0],
        w_short=f32(inputs["w_short"])[0], b_short=f32(inputs["b_short"])[0], Ftab=Ftab, Gtab=Gtab)
    zf_sh, cols_sh, nad = filter_tables_shared()
    xfull = np.ascontiguousarray(xs.reshape(8, L, D))
    xedge = np.zeros((8, 128, D), np.float32)
    for K in range(8):
        if K > 0:
            xedge[K, 0] = xs[L * K - 1]
        if K < 7:
            xedge[K, 1] = xs[L * (K + 1)]
    shared.update(zfT_tab=zf_sh, fcols_tab=cols_sh, nad_tab=nad, xfull=xfull, xedge=xedge)
    in_maps = []
    for c in range(NCORES):
        xm = np.concatenate([xp[4 * c:4 * c + 4], xs[None, L * c:L * (c + 1)]], axis=0)
        xh = np.zeros((L, D), np.float32)
        if c > 0:
            xh[:1024] = xs[L * c - 1024:L * c]
        if c < NCORES - 1:
            xh[1024:] = xs[L * (c + 1):L * (c + 1) + 1024]
        zfc, colsc, oh = filter_tables_core(c)
        m = dict(shared)
        m.update(xm=np.ascontiguousarray(xm), xh=xh, zfT_core=zfc, fcols_core=colsc, onehot=oh)
        m.update(host_consts(c))
        in_maps.append(m)
    nc, kb = build_program(njobs, stages="ABCD", sample_last=True, nslots=25)
    res = run_bass_kernel_spmd(nc, in_maps, core_ids=list(range(NCORES)))
    y_prompt = np.empty((32, L, D), np.float32)
    y_sample = np.empty((1, 8 * L, D), np.float32)
    for c in range(NCORES):
        y = res.results[c]["y"]
        y_prompt[4 * c:4 * c + 4] = y[0:4]
        y_sample[0, L * c:L * (c + 1)] = y[4]
    return (y_prompt, y_sample)
```

```python
import math
from contextlib import ExitStack
import numpy as np
import ml_dtypes
import concourse.bass as bass
import concourse.mybir as mybir
from concourse.bass_utils import run_bass_kernel_spmd

F32 = mybir.dt.float32
BF16 = mybir.dt.bfloat16
ALU = mybir.AluOpType
ACTF = mybir.ActivationFunctionType

D = 1024
L = 2048
NCORES = 8
EPS = 1e-6
HD = 128
IN_W = 9728
C_Q, C_K, C_V, C_HY, C_G = 0, 1536, 3072, 4608, 7680
MASKV = -30000.0
GROUPS = ((128, 1), (512, 4), (2048, 16))


class Buf:
    __slots__ = ("name", "w", "r")

    def __init__(self, name=""):
        self.name = name
        self.w = None
        self.r = {}


class Eng:
    def __init__(self, name, h, sem):
        self.name, self.h, self.sem = name, h, sem
        self.count = 0
        self.seen = {}


class DmaQ:
    def __init__(self, name, eng, sems):
        self.name, self.eng, self.sems = name, eng, sems
        self.n = 0


class FW:
    def __init__(self, nc, stack, n_dma_sems=8):
        self.nc = nc
        S = lambda n: stack.enter_context(nc.semaphore(n))
        self.pe = Eng("pe", nc.tensor, S("s_pe"))
        self.act = Eng("act", nc.scalar, S("s_act"))
        self.dve = Eng("dve", nc.vector, S("s_dve"))
        self.pool = Eng("pool", nc.gpsimd, S("s_pool"))
        self.sp = Eng("sp", nc.sync, S("s_sp"))
        self.engs = [self.pe, self.act, self.dve, self.pool, self.sp]
        self.q_sp = DmaQ("qsp", self.sp, [S(f"s_qsp{i}") for i in range(n_dma_sems)])
        self.q_pool = DmaQ("qpl", self.pool, [S(f"s_qpl{i}") for i in range(n_dma_sems)])
        self.qs = [self.q_sp, self.q_pool]
        self.n_inst = 0

    def _wait(self, eng, key, sem, val):
        if eng.seen.get(key, 0) >= val:
            return
        eng.h.wait_ge(sem, val)
        eng.seen[key] = val
        self.n_inst += 1

    def _deps(self, eng, reads, writes):
        need = {}

        def add(m):
            if m is None:
                return
            k, s, v = m
            if k not in need or need[k][1] < v:
                need[k] = (s, v)

        for b in reads:
            add(b.w)
        for b in writes:
            add(b.w)
            for k, (s, v) in b.r.items():
                add((k, s, v))
        for k, (s, v) in need.items():
            if k == "pe" and eng is self.pe:
                continue
            self._wait(eng, k, s, v)

    def _mark(self, reads, writes, m):
        k, s, v = m
        for b in reads:
            b.r[k] = (s, v)
        for b in writes:
            b.w = m
            b.r = {}

    def op(self, eng, fn, reads=(), writes=()):
        self._deps(eng, reads, writes)
        inst = fn()
        eng.count += 1
        inst.then_inc(eng.sem, 1)
        m = (eng.name, eng.sem, eng.count)
        self._mark(reads, writes, m)
        self.n_inst += 1
        return m

    def dma(self, q, out, in_, reads=(), writes=(), **kw):
        eng = q.eng
        j = q.n
        ns = len(q.sems)
        sem = q.sems[j % ns]
        key = f"{q.name}{j % ns}"
        prev = 16 * (j // ns)
        if prev > 0:
            self._wait(eng, key, sem, prev)
        self._deps(eng, reads, writes)
        inst = eng.h.dma_start(out=out, in_=in_, **kw)
        inst.then_inc(sem, 16)
        q.n += 1
        m = (key, sem, prev + 16)
        self._mark(reads, writes, m)
        self.n_inst += 1
        return m

    def barrier(self):
        for e in self.engs:
            for o in self.engs:
                if o is e or o.count == 0:
                    continue
                self._wait(e, o.name, o.sem, o.count)
            for q in self.qs:
                ns = len(q.sems)
                for i in range(min(ns, q.n)):
                    last = ((q.n - 1 - i) // ns) * ns + i
                    self._wait(e, f"{q.name}{i}", q.sems[i], 16 * (last // ns + 1))


_UID = [0]


def _uid(name):
    _UID[0] += 1
    return f"{name}_{_UID[0]}"


class Ring:
    def __init__(self, nc, st, name, n, shape, dt):
        self.items = []
        for i in range(n):
            t = st.enter_context(nc.sbuf_tensor(_uid(f"{name}{i}"), shape, dt))
            self.items.append((t, Buf(f"{name}{i}")))
        self.i = 0

    def next(self):
        it = self.items[self.i % len(self.items)]
        self.i += 1
        return it


def _bf(a):
    return np.ascontiguousarray(a.astype(ml_dtypes.bfloat16))


def t5_bucket_np(rel):
    nb, max_exact = 16, 8
    side = np.where(rel > 0, nb, 0)
    n = np.abs(rel)
    nf = np.maximum(n, 1).astype(np.float32)
    large = max_exact + (np.log(nf / np.float32(max_exact)) / np.float32(math.log(1024 / max_exact))
                         * np.float32(nb - max_exact)).astype(np.int32)
    large = np.minimum(large, nb - 1)
    return side + np.where(n < max_exact, n, large)


def host_consts(core):
    c = {}
    c["ident"] = _bf(np.eye(128, dtype=np.float32))
    I = np.eye(128, dtype=np.float32)
    I0 = I.copy(); I0[0, 0] = 0.0
    E00 = np.zeros((128, 128), np.float32); E00[0, 0] = 1.0
    c["signm"] = _bf(np.stack([-I, -I0, I0, E00], axis=1))
    E = np.zeros((3, 33, 384), np.float32)
    for g, (_, d) in enumerate(GROUPS):
        for u in range(384):
            dl = u - 191
            if abs(dl) <= 64:
                E[g, int(t5_bucket_np(np.int32(dl * d))), u] = 1.0
            else:
                E[g, 32, u] = 1.0
    c["Esel"] = E
    kvt = np.zeros((2, 128, 192), np.float32)
    for kind in range(2):
        for g, (_, d) in enumerate(GROUPS):
            Me = 2 * L // d
            for cc in range(64):
                col = 64 * cc + np.arange(128)
                r = col // Me
                mp = col % Me
                e = r + d * mp
                if kind == 0:
                    ok = (e >= 1024) & (e < 3072)
                else:
                    gtok = 2048 * core + e - 1024
                    ok = (gtok >= 0) & (gtok < 16384)
                ok = ok & (col < 2 * L)
                kvt[kind, :, g * 64 + cc] = np.where(ok, 0.0, MASKV)
    c["kvalid"] = kvt
    return c


NFFT = 2 * L


def dft_tables():
    n = np.arange(NFFT, dtype=np.int64)
    k = np.arange(L, dtype=np.int64)
    ph = (np.outer(n, k) % NFFT).astype(np.float64) * (2 * np.pi / NFFT)
    Fre = np.cos(ph)
    Fim = -np.sin(ph)
    Fim[:, 0] = np.where(n % 2 == 0, 1.0, -1.0)
    Fall = np.concatenate([Fre, Fim], axis=1).astype(np.float32)
    Ftab = Fall.reshape(32, 128, 32, 128).transpose(2, 1, 0, 3)
    nn = np.arange(L, dtype=np.int64)
    ph2 = (np.outer(k, nn) % NFFT).astype(np.float64) * (2 * np.pi / NFFT)
    Gre = (2.0 / NFFT) * np.cos(ph2)
    Gre[0, :] = 1.0 / NFFT
    Gim = -(2.0 / NFFT) * np.sin(ph2)
    Gim[0, :] = np.where(nn % 2 == 0, 1.0, -1.0) / NFFT
    Gall = np.concatenate([Gre, Gim], axis=0).astype(np.float32)
    Gtab = Gall.reshape(32, 128, 4, 512).transpose(2, 0, 1, 3)
    return _bf(Ftab), _bf(Gtab)


SLOT_ORDERS = [0, 1] + [0] * 15 + [1] * 8
NSLOT_SH = 17


def _slot_rows(Ls, dd):
    r = np.arange(NFFT, dtype=np.int64)
    bands = np.linspace(1e-4, 15.0, 16, dtype=np.float32)[None, :]
    m = np.where(r < L, L * dd + r, L * dd + r - NFFT)
    pos = np.abs(m)
    ok = (pos < Ls) & (r != L)
    pos = np.where(ok, pos, 0)
    t = (pos.astype(np.float32) / np.float32(Ls - 1)).astype(np.float32)
    ang = (np.float32(2.0 * math.pi) * pos.astype(np.float32) / np.float32(Ls)).astype(np.float32)[:, None]
    z = np.concatenate([t[:, None], np.cos(bands * ang), -np.sin(bands * ang)], axis=1).astype(np.float32)
    ffwd = (ok & (m >= 0)).astype(np.float32)
    fbwd = (ok & (m < 0)).astype(np.float32)
    cols = np.zeros((128, 3, 32), np.float32)
    for a in range(32):
        cols[:, 0, a] = t[a * 128:(a + 1) * 128]
        cols[:, 1, a] = ffwd[a * 128:(a + 1) * 128]
        cols[:, 2, a] = fbwd[a * 128:(a + 1) * 128]
    return np.ascontiguousarray(z.T), cols


def filter_tables_shared():
    specs = [(L, 0), (L, 0)] + [(8 * L, d) for d in range(-7, 8)]
    rows = [_slot_rows(*sp) for sp in specs]
    zf = np.stack([r[0] for r in rows]).astype(np.float32)
    cols = np.stack([r[1] for r in rows]).astype(np.float32)
    min_decay = math.log(1e-2) / 1.5
    max_decay = math.log(1e-2) / 0.3
    nad = -np.abs(np.linspace(min_decay, max_decay, D, dtype=np.float32))
    return zf, cols, nad.astype(np.float32)


def filter_tables_core(core):
    rows = [_slot_rows(8 * L, core - J) for J in range(8)]
    zf = np.stack([r[0] for r in rows]).astype(np.float32)
    cols = np.stack([r[1] for r in rows]).astype(np.float32)
    oh = np.zeros((128, 8), np.float32)
    oh[:, core] = 1.0
    return zf, cols, oh


class Job:
    def __init__(self, idx, sample):
        self.idx, self.sample = idx, sample


class KB:
    def __init__(self, nc, st, njobs, dbg=(), nslots=1):
        self.nc = nc
        self.st = st
        self.fw = FW(nc, st)
        self.njobs = njobs
        self.dbg = set(dbg)
        self.dbg_out = {}
        di = lambda n, s, d=F32: nc.dram_tensor(n, s, d, kind="ExternalInput").ap()
        self.xm = di("xm", [njobs, L, D])
        self.xh = di("xh", [L, D])
        self.w_in = di("w_in", [D, IN_W])
        self.w_ab = di("w_ab", [512, D])
        self.w_hb = di("w_hb", [D, D])
        self.w_out = di("w_out", [D, D])
        self.w_ff1 = di("w_ff1", [D, 4 * D])
        self.w_ff2 = di("w_ff2", [4 * D, D])
        self.g_mix = di("g_mix", [D])
        self.g_mlp = di("g_mlp", [D])
        self.identd = di("ident", [128, 128], BF16)
        self.signmd = di("signm", [128, 4, 128], BF16)
        self.rel_bias = di("rel_bias", [32, 12])
        self.g_q = di("g_q", [128])
        self.g_k = di("g_k", [128])
        self.Esel = di("Esel", [3, 33, 384])
        self.kvd = di("kvalid", [2, 128, 192])
        self.bsc = nc.dram_tensor("bsc", [3, 12, 384], F32, kind="Internal").ap()
        self.nslots = nslots
        self.f_w1 = di("filt_w1", [33, 64]); self.f_b1 = di("filt_b1", [64])
        self.f_w2 = di("filt_w2", [64, 64]); self.f_b2 = di("filt_b2", [64])
        self.f_w3 = di("filt_w3", [64, 64]); self.f_b3 = di("filt_b3", [64])
        self.f_w4 = di("filt_w4", [64, 4096]); self.f_freq = di("filt_freq", [64])
        self.f_skip = di("filt_skip", [2, D])
        self.w_short = di("w_short", [3, 3 * D]); self.b_short = di("b_short", [3 * D])
        self.zfTd = di("zfT_tab", [NSLOT_SH, 33, NFFT])
        self.fcolsd = di("fcols_tab", [NSLOT_SH, 128, 3, 32])
        self.zfTc = di("zfT_core", [8, 33, NFFT])
        self.fcolsc = di("fcols_core", [8, 128, 3, 32])
        self.onehotd = di("onehot", [128, 8])
        self.xfull = di("xfull", [8, L, D])
        self.xedge = di("xedge", [8, 128, D])
        self.nadd = di("nad_tab", [D])
        self.Ftab = di("Ftab", [32, 128, 32, 128], BF16)
        self.Gtab = di("Gtab", [4, 32, 128, 512], BF16)
        self.Hs = [nc.dram_tensor(f"Hs{i}", [2, 16, 128, 1024], BF16, kind="Internal").ap() for i in range(nslots)]
        self.b_Hs = Buf("Hs")
        dsc = lambda n, sh: nc.dram_tensor(n, sh, BF16, kind="Internal").ap()
        self.xs0, self.xs1 = dsc("xs0", [2, 8, 16, 128, 1024]), dsc("xs1", [2, 8, 16, 128, 1024])
        self.zs, self.x1s, self.z1s = dsc("zs", [2, 8, 128, 4, L]), dsc("x1s", [2, 8, 128, 4, L]), dsc("z1s", [2, 8, 128, 4, L])
        self.b_xs0, self.b_xs1, self.b_zs, self.b_x1s, self.b_z1s = [Buf(n) for n in ("xs0", "xs1", "zs", "x1s", "z1s")]
        self.y = nc.dram_tensor("y", [njobs, L, D], F32, kind="ExternalOutput").ap()
        self.pb = []
        for i in range(7):
            t = st.enter_context(nc.psum_tensor(f"pb{i}", [128, 512], F32))
            self.pb.append((t, Buf(f"pb{i}")))
        self.ptr = st.enter_context(nc.psum_tensor("ptr", [128, 1024], BF16))
        self.b_ptr = Buf("ptr")
        self.pbi = 0
        self.ident, self.b_ident = self.sb(st, "identb", [128, 128], BF16)
        self.fw.dma(self.fw.q_sp, self.ident[:], self.identd[:, :], writes=[self.b_ident])
        self.signm, self.b_signm = self.sb(st, "signm", [128, 4, 128], BF16)
        self.fw.dma(self.fw.q_sp, self.signm[:], self.signmd[:, :, :], writes=[self.b_signm])

    def sb(self, st, name, shape, dt):
        t = st.enter_context(self.nc.sbuf_tensor(_uid(name), shape, dt))
        return t, Buf(name)

    def alloc_persistent(self):
        nc, st = self.nc, self.st
        self.hT, self.b_hT = self.sb(st, "hT", [128, 8, L], BF16)
        self.attnT, self.b_attnT = self.sb(st, "attnT", [128, 4, L], BF16)
        self.zfT, self.b_zfT = self.sb(st, "zfT", [128, 8, L], BF16)
        self.edge, self.b_edge = self.sb(st, "edge", [128, 8, 2], BF16)
        fw = self.fw
        fw.op(fw.dve, lambda: nc.vector.memset(self.edge[:], 0.0), writes=[self.b_edge])
        self.wsc, self.b_wsc = self.sb(st, "wsc", [128, 4, 24], F32)
        for k in range(3):
            fw.dma(fw.q_sp, self.wsc[:, k, :], self.w_short[k, :].rearrange("(j p) -> p j", p=128), writes=[self.b_wsc],
                   allow_slow_non_contiguous=True)
        fw.dma(fw.q_sp, self.wsc[:, 3, :], self.b_short.rearrange("(j p) -> p j", p=128), writes=[self.b_wsc],
               allow_slow_non_contiguous=True)
        self.skc, self.b_skc = self.sb(st, "skc", [128, 2, 8], F32)
        for o in range(2):
            fw.dma(fw.q_sp, self.skc[:, o, :], self.f_skip[o, :].rearrange("(j p) -> p j", p=128), writes=[self.b_skc],
                   allow_slow_non_contiguous=True)

    def init_hyena(self, st):
        nc, fw = self.nc, self.fw
        TWO_PI = 2.0 * math.pi
        MAGIC = 12582912.0
        w1, b_w1 = self.sb(st, "fw1", [33, 64], F32)
        w2, b_w2 = self.sb(st, "fw2", [64, 64], F32)
        w3, b_w3 = self.sb(st, "fw3", [64, 64], F32)
        w4b, b_w4 = self.sb(st, "fw4", [64, 4096], BF16)
        frc, b_frc = self.sb(st, "frc", [64, 8], F32)
        nadt, b_nad = self.sb(st, "nadt", [128, D], F32)
        ktime, b_kt = self.sb(st, "ktime", [128, 32, 1024], BF16)
        sc, b_sc = self.sb(st, "fsc", [128, 3, 32], F32)
        fw.dma(fw.q_sp, w1[:], self.f_w1[:, :], writes=[b_w1])
        fw.dma(fw.q_sp, w2[:], self.f_w2[:, :], writes=[b_w2])
        fw.dma(fw.q_sp, w3[:], self.f_w3[:, :], writes=[b_w3])
        fw.dma(fw.q_pool, w4b[:], self.f_w4[:, :], writes=[b_w4])
        fw.dma(fw.q_sp, nadt[:], self.nadd.partition_broadcast(128), writes=[b_nad])
        col1 = lambda v: v.rearrange("(p o) -> p o", o=1)
        fw.dma(fw.q_sp, frc[:, 0:1], col1(self.f_freq), writes=[b_frc])
        for li, bb in enumerate((self.f_b1, self.f_b2, self.f_b3)):
            fw.dma(fw.q_sp, frc[:, 4 + li:5 + li], col1(bb), writes=[b_frc])
        for li in range(3):
            fw.op(fw.dve, lambda: nc.vector.tensor_scalar(out=frc[:, 1 + li:2 + li], in0=frc[:, 4 + li:5 + li], scalar1=frc[:, 0:1],
                                                          scalar2=None, op0=ALU.mult), reads=[b_frc], writes=[b_frc])
        Ws = [(w1, b_w1, 33), (w2, b_w2, 64), (w3, b_w3, 64)]
        zt_r = Ring(nc, st, "zt", 2, [33, 512], F32)
        a_r = Ring(nc, st, "fa", 3, [64, 512], F32)
        r_r = Ring(nc, st, "fr", 2, [64, 512], F32)
        h3_r = Ring(nc, st, "fh3", 2, [64, 512], BF16)
        dec_r = Ring(nc, st, "fdec", 2, [128, D], F32)
        t_r = Ring(nc, st, "ft", 3, [128, 512], F32)
        F_r = Ring(nc, st, "fF", 2, [128, 32, 128], BF16)
        ho_r = Ring(nc, st, "fho", 4, [128, 512], BF16)
        nq_r = Ring(nc, st, "fnq", 2, [1, 512], BF16)
        b_patch = Buf("hpatch")
        for si in range(self.nslots):
            so = SLOT_ORDERS[si]
            zsrc = self.zfTd[si] if si < NSLOT_SH else self.zfTc[si - NSLOT_SH]
            fw.dma(fw.q_sp, sc[:], self.fcolsd[si] if si < NSLOT_SH else self.fcolsc[si - NSLOT_SH], writes=[b_sc])
            for rg in range(8):
                zt, bzt = zt_r.next()
                fw.dma(fw.q_sp, zt[:], zsrc[:, rg * 512:(rg + 1) * 512], writes=[bzt])
                h, bh, K = zt, bzt, 33
                for li in range(3):
                    W, bW, K = Ws[li]
                    ps, bps = self.bank()
                    self.mm_group(ps[0:64, :], [(W[0:K, :], h[0:K, :])], reads=[bW, bh], writes=[bps])
                    a, ba = a_r.next()
                    fw.op(fw.dve, lambda: nc.vector.tensor_scalar(out=a[:], in0=ps[0:64, :], scalar1=frc[:, 0:1],
                                                                  scalar2=frc[:, 1 + li:2 + li], op0=ALU.mult, op1=ALU.add),
                          reads=[bps, b_frc], writes=[ba])
                    r, br = r_r.next()
                    fw.op(fw.dve, lambda: nc.vector.tensor_scalar(out=r[:], in0=a[:], scalar1=1.0 / TWO_PI, scalar2=MAGIC,
                                                                  op0=ALU.mult, op1=ALU.add), reads=[ba], writes=[br])
                    fw.op(fw.dve, lambda: nc.vector.tensor_scalar(out=r[:], in0=r[:], scalar1=MAGIC, scalar2=None,
                                                                  op0=ALU.subtract), reads=[br], writes=[br])
                    fw.op(fw.dve, lambda: nc.vector.scalar_tensor_tensor(out=a[:], in0=r[:], scalar=-TWO_PI, in1=a[:],
                                                                         op0=ALU.mult, op1=ALU.add), reads=[br, ba], writes=[ba])
                    if li < 2:
                        hn_, bhn_ = a_r.next()
                    else:
                        hn_, bhn_ = h3_r.next()
                    fw.op(fw.act, lambda: nc.scalar.activation(out=hn_[:], in_=a[:], func=ACTF.Sin), reads=[ba], writes=[bhn_])
                    h, bh = hn_, bhn_
                for ai in range(4):
                    aidx = rg * 4 + ai
                    dec, bdec = dec_r.next()
                    fw.op(fw.act, lambda: nc.scalar.activation(out=dec[:], in_=nadt[:], func=ACTF.Exp, scale=sc[:, 0, aidx:aidx + 1]),
                          reads=[b_nad, b_sc], writes=[bdec])
                    for o in (so,):
                        for hf in range(2):
                            psf, bpsf = self.bank()
                            c0 = o * 2048 + hf * 512
                            self.mm_group(psf[:], [(h[:, ai * 128:(ai + 1) * 128], w4b[:, c0:c0 + 512])], reads=[bh, b_w4], writes=[bpsf])
                            psb, bpsb = self.bank()
                            c1 = o * 2048 + 1024 + hf * 512
                            self.mm_group(psb[:], [(h[:, ai * 128:(ai + 1) * 128], w4b[:, c1:c1 + 512])], reads=[bh, b_w4], writes=[bpsb])
                            t, bt = t_r.next()
                            fw.op(fw.dve, lambda: nc.vector.tensor_scalar(out=t[:], in0=psf[:], scalar1=sc[:, 1, aidx:aidx + 1],
                                                                          scalar2=None, op0=ALU.mult), reads=[bpsf, b_sc], writes=[bt])
                            fw.op(fw.dve, lambda: nc.vector.scalar_tensor_tensor(out=t[:], in0=psb[:], scalar=sc[:, 2, aidx:aidx + 1], in1=t[:],
                                                                                 op0=ALU.mult, op1=ALU.add),
                                  reads=[bpsb, b_sc, bt], writes=[bt])
                            cg = hf
                            fw.op(fw.dve, lambda: nc.vector.tensor_tensor(out=ktime[:, aidx, cg * 512:(cg + 1) * 512], in0=t[:],
                                                                          in1=dec[:, hf * 512:(hf + 1) * 512], op=ALU.mult),
                                  reads=[bt, bdec], writes=[b_kt])
            for m in range(32):
                Ft, bF = F_r.next()
                fw.dma(fw.q_sp, Ft[:], self.Ftab[m], writes=[bF])
                for cg in range(2):
                    ps, bps = self.bank()
                    self.mm_group(ps[:], [(Ft[:, a, :], ktime[:, a, cg * 512:(cg + 1) * 512]) for a in range(32)],
                                  reads=[bF, b_kt], writes=[bps])
                    ho, bho = ho_r.next()
                    fw.op(fw.act, lambda: nc.scalar.copy(out=ho[:], in_=ps[:]), reads=[bps], writes=[bho])
                    if m < 16:
                        fw.dma(fw.q_sp, self.Hs[si][cg, m, :, 0:512], ho[:], reads=[bho])
                    else:
                        fw.dma(fw.q_sp, self.Hs[si][cg, m - 16, :, 512:1024], ho[:], reads=[bho])

    def init_attention(self):
        nc, fw, st = self.nc, self.fw, self.st
        self.ones, self.b_ones = self.sb(st, "onesb", [128, 128], BF16)
        fw.op(fw.dve, lambda: nc.vector.memset(self.ones[:], 1.0), writes=[self.b_ones])
        self.gqk, self.b_gqk = self.sb(st, "gqk", [128, 2], F32)
        fw.dma(fw.q_sp, self.gqk[:, 0:1], self.g_q.rearrange("(p o) -> p o", o=1), writes=[self.b_gqk])
        fw.dma(fw.q_sp, self.gqk[:, 1:2], self.g_k.rearrange("(p o) -> p o", o=1), writes=[self.b_gqk])
        fw.op(fw.dve, lambda: nc.vector.tensor_scalar(out=self.gqk[:, 1:2], in0=self.gqk[:, 1:2], scalar1=math.sqrt(128.0),
                                                      scalar2=None, op0=ALU.mult), reads=[self.b_gqk], writes=[self.b_gqk])
        self.kv, self.b_kv = self.sb(st, "kv", [128, 2, 192], F32)
        fw.dma(fw.q_sp, self.kv[:], self.kvd.rearrange("k p c -> p k c"), writes=[self.b_kv])
        self.biasd = nc.dram_tensor("biasd", [128, 24, 128], BF16)
        self.b_biasd = Buf("biasd")
        with ExitStack() as s2:
            self.biasT, self.b_biasT = self.sb(s2, "biasT0", [128, 24, 128], BF16)
            rb, b_rb = self.sb(s2, "rb", [33, 12], F32)
            es, b_es = self.sb(s2, "es", [33, 3, 384], F32)
            bv, b_bv = self.sb(s2, "bv", [12, 3, 384], F32)
            stg = Ring(nc, s2, "bstg", 2, [128, 128], F32)
            fw.op(fw.dve, lambda: nc.vector.memset(rb[32:33, :], MASKV), writes=[b_rb])
            fw.dma(fw.q_sp, rb[0:32, :], self.rel_bias[:, :], writes=[b_rb])
            fw.dma(fw.q_sp, es[:], self.Esel.rearrange("g b u -> b g u"), writes=[b_es])
            b_bsc = Buf("bsc")
            for g in range(3):
                ps, bps = self.bank()
                self.mm_group(ps[0:12, 0:384], [(rb[:, :], es[:, g, :])], reads=[b_rb, b_es], writes=[bps])
                fw.op(fw.dve, lambda: nc.vector.tensor_copy(out=bv[:, g, :], in_=ps[0:12, 0:384]), reads=[bps], writes=[b_bv])
            fw.dma(fw.q_sp, self.bsc.rearrange("g h u -> h g u"), bv[:], reads=[b_bv], writes=[b_bsc])
            for g in range(3):
                for i in range(4):
                    for j in range(2):
                        h = 4 * g + i
                        t, bt = stg.next()
                        src = bass.AP(self.bsc.tensor, (g * 12 + h) * 384 + 127 + 128 * j, [[1, 128], [-1, 128]])
                        fw.dma(fw.q_sp, t[:], src, reads=[b_bsc], writes=[bt], allow_slow_non_contiguous=True)
                        fw.op(fw.dve, lambda: nc.vector.tensor_copy(out=self.biasT[:, (g * 4 + i) * 2 + j, :], in_=t[:]),
                              reads=[bt], writes=[self.b_biasT])
            fw.dma(fw.q_sp, self.biasd[:, :, :], self.biasT[:], reads=[self.b_biasT], writes=[self.b_biasd])
            fw.barrier()

    def precast_weights(self, st):
        nc, fw = self.nc, self.fw
        specs = [("w_in", D, IN_W), ("w_ab", 512, D), ("w_hb", D, D), ("w_out", D, D), ("w_ff1", D, 4 * D), ("w_ff2", 4 * D, D)]
        stg = Ring(nc, st, "wcast", 2, [128, IN_W], BF16)
        for name, R, C in specs:
            src = getattr(self, name)
            dst = nc.dram_tensor(name + "_bf", [R, C], BF16, kind="Internal").ap()
            for rb in range(R // 128):
                t, bt = stg.next()
                fw.dma(fw.q_pool, t[:, 0:C], src[rb * 128:(rb + 1) * 128, :], writes=[bt])
                fw.dma(fw.q_sp, dst[rb * 128:(rb + 1) * 128, :], t[:, 0:C], reads=[bt])
            setattr(self, name, dst)

    def load_gain(self, st, which):
        t, b = self.sb(st, "gain_" + which, [128, D], F32)
        src = self.g_mix if which == "mix" else self.g_mlp
        self.fw.dma(self.fw.q_sp, t[:], src.partition_broadcast(128), writes=[b])
        if which == "mix":
            self.gmix, self.b_gmix = t, b
        else:
            self.gmlp, self.b_gmlp = t, b

    def bank(self):
        it = self.pb[self.pbi % len(self.pb)]
        self.pbi += 1
        return it

    def dbg_dump(self, name, tile_ap, buf, shape, dt=F32):
        if name not in self.dbg:
            return
        o = self.nc.dram_tensor("dbg_" + name, list(shape), dt, kind="ExternalOutput").ap()
        self.dbg_out[name] = o
        self.fw.dma(self.fw.q_sp, o, tile_ap, reads=[buf])

    def load_w(self, tile, buf, src2d, nk):
        fw = self.fw
        fw.dma(fw.q_pool, tile, src2d.rearrange("(k p) c -> p k c", p=128), writes=[buf])

    def mm_group(self, out_ap, pairs, reads, writes, start=True, stop=True):
        nc = self.nc
        n = len(pairs)

        def emit():
            inst = None
            for i, (l, r) in enumerate(pairs):
                inst = nc.tensor.matmul(out_ap, lhsT=l, rhs=r, start=(start and i == 0), stop=(stop and i == n - 1))
            return inst
        return self.fw.op(self.fw.pe, emit, reads=reads, writes=writes)

    def rmsnorm_T(self, st_ring, x_ap, bx, g_tile, bg, dst_ap, bdst):
        nc, fw = self.nc, self.fw
        sq, bsq = st_ring["sq"].next()
        col, bcol = st_ring["col"].next()
        hn, bhn = st_ring["hn"].next()
        fw.op(fw.act, lambda: nc.scalar.activation(out=sq[:], in_=x_ap, func=ACTF.Square, accum_out=col[:, 0:1]),
              reads=[bx], writes=[bsq, bcol])
        fw.op(fw.act, lambda: nc.scalar.activation(out=col[:, 1:2], in_=col[:, 0:1], func=ACTF.Sqrt, scale=1.0 / D, bias=EPS),
              reads=[bcol], writes=[bcol])
        fw.op(fw.dve, lambda: nc.vector.reciprocal(out=col[:, 2:3], in_=col[:, 1:2]), reads=[bcol], writes=[bcol])
        fw.op(fw.dve, lambda: nc.vector.scalar_tensor_tensor(out=hn[:], in0=x_ap, scalar=col[:, 2:3], in1=g_tile[:],
                                                             op0=ALU.mult, op1=ALU.mult),
              reads=[bx, bcol, bg], writes=[bhn])

        def tr():
            inst = None
            for k in range(8):
                inst = nc.tensor.transpose(self.ptr[:, k * 128:(k + 1) * 128], hn[:, k * 128:(k + 1) * 128], self.ident[:])
            return inst
        fw.op(fw.pe, tr, reads=[bhn, self.b_ident], writes=[self.b_ptr])
        fw.op(fw.act, lambda: nc.scalar.copy(out=dst_ap, in_=self.ptr[:].rearrange("p (k t) -> p k t", k=8)),
              reads=[self.b_ptr], writes=[bdst])

    def stage_A(self, job, rings):
        fw = self.fw
        for t in range(L // 128):
            xt, bx = rings["x"].next()
            fw.dma(fw.q_sp, xt[:], self.xm[job.idx, t * 128:(t + 1) * 128, :], writes=[bx])
            self.rmsnorm_T(rings, xt[:], bx, self.gmix, self.b_gmix,
                           self.hT[:, :, t * 128:(t + 1) * 128], self.b_hT)

    def qk_norm(self, ps, bps, gcol, dst, bdst, rb):
        nc, fw = self.nc, self.fw
        sq, bsq = rb["sq5"].next()
        rs, brs = rb["rs5"].next()
        fw.op(fw.act, lambda: nc.scalar.activation(out=sq[:], in_=ps[:], func=ACTF.Square), reads=[bps], writes=[bsq])
        ps2, bps2 = self.bank()
        self.mm_group(ps2[:], [(self.ones[:], sq[:])], reads=[self.b_ones, bsq], writes=[bps2])
        fw.op(fw.act, lambda: nc.scalar.activation(out=rs[:], in_=ps2[:], func=ACTF.Sqrt, scale=1.0, bias=128.0 * EPS),
              reads=[bps2], writes=[brs])
        fw.op(fw.dve, lambda: nc.vector.reciprocal(out=rs[:], in_=rs[:]), reads=[brs], writes=[brs])
        fw.op(fw.dve, lambda: nc.vector.scalar_tensor_tensor(out=dst, in0=ps[:], scalar=gcol, in1=rs[:],
                                                             op0=ALU.mult, op1=ALU.mult),
              reads=[bps, brs, self.b_gqk], writes=[bdst])

    def stage_B(self, job, rings, st):
        nc, fw = self.nc, self.fw
        kind = 1 if job.sample else 0
        hTh, b_hTh = self.sb(st, "hTh", [128, 8, 2048], BF16)
        self.biasT, self.b_biasT = self.sb(st, "biasT", [128, 24, 128], BF16)
        fw.dma(fw.q_sp, self.biasT[:], self.biasd[:, :, :], reads=[self.b_biasd], writes=[self.b_biasT])
        if job.sample:
            for t in range(16):
                xt, bx = rings["x"].next()
                fw.dma(fw.q_sp, xt[:], self.xh[t * 128:(t + 1) * 128, :], writes=[bx])
                self.rmsnorm_T(rings, xt[:], bx, self.gmix, self.b_gmix, hTh[:, :, t * 128:(t + 1) * 128], b_hTh)
            fw.op(fw.dve, lambda: nc.vector.tensor_copy(out=self.edge[:, :, 0:1], in_=hTh[:, :, 1023:1024]), reads=[b_hTh], writes=[self.b_edge])
            fw.op(fw.dve, lambda: nc.vector.tensor_copy(out=self.edge[:, :, 1:2], in_=hTh[:, :, 1024:1025]), reads=[b_hTh], writes=[self.b_edge])

        def hsrc(eg, k):
            if eg < 2:
                return hTh[:, k, eg * 512:(eg + 1) * 512], b_hTh
            if eg < 6:
                return self.hT[:, k, (eg - 2) * 512:(eg - 1) * 512], self.b_hT
            return hTh[:, k, 1024 + (eg - 6) * 512:1024 + (eg - 5) * 512], b_hTh

        wr = Ring(nc, st, "wqkv", 6, [128, 8, 128], BF16)
        qn_r = Ring(nc, st, "qn", 1, [128, L], BF16)
        kn_r = Ring(nc, st, "kn", 1, [128, 2 * L], BF16)
        vT_r = Ring(nc, st, "vT", 1, [128, 2 * L], BF16)
        nd, b_nd = self.sb(st, "nd", [128, 2, L], F32)
        pt_r = Ring(nc, st, "pt", 3, [128, 256], BF16)
        vt_r = Ring(nc, st, "vt", 4, [128, 128], BF16)
        rb = {"sq5": Ring(nc, st, "sq5", 2, [128, 512], BF16), "rs5": Ring(nc, st, "rs5", 2, [128, 512], F32)}
        for i in range(4):
            for grp in range(3):
                d = GROUPS[grp][1]
                h = 4 * grp + i
                M, Me = L // d, 2 * L // d
                ws = []
                for c0 in (C_Q, C_K, C_V):
                    wt, bw = wr.next()
                    self.load_w(wt[:], bw, self.w_in[:, c0 + h * 128:c0 + (h + 1) * 128], 8)
                    ws.append((wt, bw))
                qn, bqn = qn_r.next()
                kn, bkn = kn_r.next()
                vT, bvT = vT_r.next()
                if not job.sample:
                    fw.op(fw.dve, lambda: nc.vector.memset(kn[:], 0.0), writes=[bkn])
                    fw.op(fw.dve, lambda: nc.vector.memset(vT[:], 0.0), writes=[bvT])
                for tg in range(4):
                    ps, bps = self.bank()
                    self.mm_group(ps[:], [(ws[0][0][:, k, :], self.hT[:, k, tg * 512:(tg + 1) * 512]) for k in range(8)],
                                  reads=[ws[0][1], self.b_hT], writes=[bps])
                    if d == 1:
                        dst = qn[:, tg * 512:(tg + 1) * 512]
                        self.qk_norm(ps, bps, self.gqk[:, 0:1], dst, bqn, rb)
                    else:
                        self.qk_norm_perm(ps, bps, self.gqk[:, 0:1], qn, bqn, rb, d, M, tg * 512 // d)
                if job.sample:
                    egs = list(range(8)) if d == 16 else list(range(1, 7))
                else:
                    egs = [2, 3, 4, 5]
                for eg in egs:
                    ps, bps = self.bank()
                    prs, rd = [], [ws[1][1]]
                    for k in range(8):
                        a, ba = hsrc(eg, k)
                        prs.append((ws[1][0][:, k, :], a))
                    self.mm_group(ps[:], prs, reads=[ws[1][1], ba], writes=[bps])
                    if d == 1:
                        self.qk_norm(ps, bps, self.gqk[:, 1:2], kn[:, eg * 512:(eg + 1) * 512], bkn, rb)
                    else:
                        self.qk_norm_perm(ps, bps, self.gqk[:, 1:2], kn, bkn, rb, d, Me, eg * 512 // d)
                    ps, bps = self.bank()
                    prs = []
                    for k in range(8):
                        a, ba = hsrc(eg, k)
                        prs.append((ws[2][0][:, k, :], a))
                    self.mm_group(ps[:], prs, reads=[ws[2][1], ba], writes=[bps])
                    if d == 1:
                        dstv = vT[:, eg * 512:(eg + 1) * 512]
                        srcv = ps[:]
                    else:
                        dstv = vT[:].rearrange("p (r m) -> p r m", r=d)[:, :, eg * 512 // d:(eg + 1) * 512 // d]
                        srcv = ps[:].rearrange("p (m r) -> p r m", r=d)
                    fw.op(fw.act, lambda: nc.scalar.copy(out=dstv, in_=srcv), reads=[bps], writes=[bvT])
                for r in range(d):
                    vcache = {}
                    for mt in range(M // 128):
                        m0 = mt * 128
                        kb = 1024 // d + m0 - 64
                        kcs = [r * Me + kb + 128 * j for j in range(2)]
                        vts = []
                        for j in range(2):
                            if kcs[j] in vcache:
                                vts.append(vcache[kcs[j]])
                                continue
                            fw.op(fw.pe, lambda: nc.tensor.transpose(self.ptr[:, 0:128], vT[:, kcs[j]:kcs[j] + 128], self.ident[:]),
                                  reads=[bvT, self.b_ident], writes=[self.b_ptr])
                            vt, bvt = vt_r.next()
                            fw.op(fw.act, lambda: nc.scalar.copy(out=vt[:], in_=self.ptr[:, 0:128]), reads=[self.b_ptr], writes=[bvt])
                            vcache[kcs[j]] = (vt, bvt)
                            vts.append((vt, bvt))
                        vcache = {kcs[1]: vts[1]}
                        ps, bps = self.bank()
                        qsl = qn[:, r * M + m0:r * M + m0 + 128]

                        def sc():
                            inst = None
                            for j in range(2):
                                nc.tensor.matmul(ps[:, j * 128:(j + 1) * 128], lhsT=kn[:, kcs[j]:kcs[j] + 128], rhs=qsl,
                                                 start=True, stop=False)
                                inst = nc.tensor.matmul(ps[:, j * 128:(j + 1) * 128], lhsT=self.ident[:],
                                                        rhs=self.biasT[:, (grp * 4 + i) * 2 + j, :], start=False, stop=True)
                            return inst
                        fw.op(fw.pe, sc, reads=[bkn, bqn, self.b_ident, self.b_biasT], writes=[bps])
                        pt, bpt = pt_r.next()
                        for j in range(2):
                            col = grp * 64 + kcs[j] // 64
                            fw.op(fw.act, lambda: nc.scalar.activation(out=pt[:, j * 128:(j + 1) * 128], in_=ps[:, j * 128:(j + 1) * 128],
                                                                       func=ACTF.Exp, bias=self.kv[:, kind, col:col + 1], scale=1.0),
                                  reads=[bps, self.b_kv], writes=[bpt])
                        ps2, bps2 = self.bank()

                        def pv():
                            inst = None
                            for j in range(2):
                                nc.tensor.matmul(ps2[:, 0:128], lhsT=vts[j][0][:], rhs=pt[:, j * 128:(j + 1) * 128],
                                                 start=(j == 0), stop=(j == 1))
                            for j in range(2):
                                inst = nc.tensor.matmul(ps2[:, 128:256], lhsT=self.ones[:], rhs=pt[:, j * 128:(j + 1) * 128],
                                                        start=(j == 0), stop=(j == 1))
                            return inst
                        fw.op(fw.pe, pv, reads=[vts[0][1], vts[1][1], bpt, self.b_ones], writes=[bps2])
                        ndv = nd[:].rearrange("p a (m r) -> p a m r", r=d)[:, :, m0:m0 + 128, r]
                        src = ps2[:, 0:256].rearrange("p (a m) -> p a m", a=2)
                        if grp == 0:
                            fw.op(fw.dve, lambda: nc.vector.tensor_copy(out=ndv, in_=src), reads=[bps2], writes=[b_nd])
                        else:
                            fw.op(fw.dve, lambda: nc.vector.tensor_tensor(out=ndv, in0=ndv, in1=src, op=ALU.add),
                                  reads=[bps2, b_nd], writes=[b_nd])
            fw.op(fw.dve, lambda: nc.vector.reciprocal(out=nd[:, 1, :], in_=nd[:, 1, :]), reads=[b_nd], writes=[b_nd])
            fw.op(fw.dve, lambda: nc.vector.tensor_tensor(out=self.attnT[:, i, :], in0=nd[:, 0, :], in1=nd[:, 1, :], op=ALU.mult),
                  reads=[b_nd], writes=[self.b_attnT])

    def qk_norm_perm(self, ps, bps, gcol, dst_tile, bdst, rb, d, Mtot, mstart):
        n = 512 // d
        dst = dst_tile[:].rearrange("p (r m) -> p r m", r=d)[:, :, mstart:mstart + n]
        nc, fw = self.nc, self.fw
        sq, bsq = rb["sq5"].next()
        rs, brs = rb["rs5"].next()
        fw.op(fw.act, lambda: nc.scalar.activation(out=sq[:], in_=ps[:], func=ACTF.Square), reads=[bps], writes=[bsq])
        ps2, bps2 = self.bank()
        self.mm_group(ps2[:], [(self.ones[:], sq[:])], reads=[self.b_ones, bsq], writes=[bps2])
        fw.op(fw.act, lambda: nc.scalar.activation(out=rs[:], in_=ps2[:], func=ACTF.Sqrt, scale=1.0, bias=128.0 * EPS),
              reads=[bps2], writes=[brs])
        fw.op(fw.dve, lambda: nc.vector.reciprocal(out=rs[:], in_=rs[:]), reads=[brs], writes=[brs])
        fw.op(fw.dve, lambda: nc.vector.scalar_tensor_tensor(out=dst, in0=ps[:].rearrange("p (m r) -> p r m", r=d), scalar=gcol,
                                                             in1=rs[:].rearrange("p (m r) -> p r m", r=d),
                                                             op0=ALU.mult, op1=ALU.mult),
              reads=[bps, brs, self.b_gqk], writes=[bdst])

    class _NS:
        pass

    def c_alloc(self, st, sample=False):
        nc = self.nc
        c = KB._NS()
        c.u, c.b_u = self.sb(st, "ubuf", [128, L + 2], BF16)
        c.zTf, c.b_zTf = self.sb(st, "zTf", [128, 4, L], BF16)
        c.x1T, c.b_x1T = self.sb(st, "x1T", [128, 4, L], BF16)
        c.ztok, c.b_ztok = self.sb(st, "ztok", [128, 16, 512], BF16)
        c.w_r = Ring(nc, st, "wh", 2, [128, 8, 128], BF16)
        c.F_r = Ring(nc, st, "cF", 2, [128, 16, 128], BF16)
        if not sample:
            c.H_r = Ring(nc, st, "cH", 4, [128, 512], BF16)
        c.t_r = Ring(nc, st, "ct", 3 if sample else 4, [128, 512], F32)
        return c

    def c_alloc_inv(self, c, st):
        c.Y, c.b_Y = self.sb(st, "Yspec", [128, 32, 512], BF16)
        c.G_r = Ring(self.nc, st, "cG", 2, [128, 2, 512], BF16)

    def c_project(self, c, kinds, hf):
        nc, fw = self.nc, self.fw
        u, b_u, t_r = c.u, c.b_u, c.t_r
        for kind in kinds:
            for cc in range(4):
                jcol = kind * 8 + hf * 4 + cc
                col0 = C_HY + jcol * 128
                wt, bw = c.w_r.next()
                self.load_w(wt[:], bw, self.w_in[:, col0:col0 + 128], 8)
                for tg in range(4):
                    ps, bps = self.bank()
                    self.mm_group(ps[:], [(wt[:, k, :], self.hT[:, k, tg * 512:(tg + 1) * 512]) for k in range(8)],
                                  reads=[bw, self.b_hT], writes=[bps])
                    fw.op(fw.act, lambda: nc.scalar.copy(out=u[:, 1 + tg * 512:1 + (tg + 1) * 512], in_=ps[:]),
                          reads=[bps], writes=[b_u])
                ps, bps = self.bank()
                self.mm_group(ps[:, 0:2], [(wt[:, k, :], self.edge[:, k, 0:2]) for k in range(8)],
                              reads=[bw, self.b_edge], writes=[bps])
                fw.op(fw.act, lambda: nc.scalar.copy(out=u[:, 0:1], in_=ps[:, 0:1]), reads=[bps], writes=[b_u])
                fw.op(fw.act, lambda: nc.scalar.copy(out=u[:, L + 1:L + 2], in_=ps[:, 1:2]), reads=[bps], writes=[b_u])
                if kind == 0:
                    dst, bdst = c.zTf[:, cc, :], c.b_zTf
                elif kind == 1:
                    dst, bdst = c.x1T[:, cc, :], c.b_x1T
                else:
                    dst, bdst = self.zfT[:, hf * 4 + cc, :], self.b_zfT
                for tg in range(4):
                    t, bt = t_r.next()
                    o = tg * 512
                    fw.op(fw.dve, lambda: nc.vector.tensor_scalar(out=t[:], in0=u[:, 1 + o:513 + o], scalar1=self.wsc[:, 1, jcol:jcol + 1],
                                                                  scalar2=self.wsc[:, 3, jcol:jcol + 1], op0=ALU.mult, op1=ALU.add),
                          reads=[b_u, self.b_wsc], writes=[bt])
                    fw.op(fw.dve, lambda: nc.vector.scalar_tensor_tensor(out=t[:], in0=u[:, o:512 + o], scalar=self.wsc[:, 0, jcol:jcol + 1],
                                                                         in1=t[:], op0=ALU.mult, op1=ALU.add),
                          reads=[b_u, self.b_wsc, bt], writes=[bt])
                    fw.op(fw.dve, lambda: nc.vector.scalar_tensor_tensor(out=dst[:, o:o + 512], in0=u[:, 2 + o:514 + o],
                                                                         scalar=self.wsc[:, 2, jcol:jcol + 1], in1=t[:],
                                                                         op0=ALU.mult, op1=ALU.add),
                          reads=[b_u, self.b_wsc, bt], writes=[bdst])

    def c_totok(self, c):
        nc, fw = self.nc, self.fw
        for tt in range(16):
            def tr():
                inst = None
                for cc in range(4):
                    inst = nc.tensor.transpose(self.ptr[:, cc * 128:(cc + 1) * 128], c.zTf[:, cc, tt * 128:(tt + 1) * 128], self.ident[:])
                return inst
            fw.op(fw.pe, tr, reads=[c.b_zTf, self.b_ident], writes=[self.b_ptr])
            fw.op(fw.act, lambda: nc.scalar.copy(out=c.ztok[:, tt, :], in_=self.ptr[:, 0:512]), reads=[self.b_ptr], writes=[c.b_ztok])

    def c_fwd(self, c, sink):
        fw = self.fw
        for m in range(16):
            Fre, bFre = c.F_r.next()
            fw.dma(fw.q_sp, Fre[:], self.Ftab[m, :, 0:16, :], writes=[bFre])
            psr, bpsr = self.bank()
            self.mm_group(psr[:], [(Fre[:, a, :], c.ztok[:, a, :]) for a in range(16)], reads=[bFre, c.b_ztok], writes=[bpsr])
            Fim, bFim = c.F_r.next()
            fw.dma(fw.q_sp, Fim[:], self.Ftab[16 + m, :, 0:16, :], writes=[bFim])
            psi, bpsi = self.bank()
            self.mm_group(psi[:], [(Fim[:, a, :], c.ztok[:, a, :]) for a in range(16)], reads=[bFim, c.b_ztok], writes=[bpsi])
            sink(m, psr, bpsr, psi, bpsi)

    def c_sink_mul(self, c, slot, hf):
        nc, fw = self.nc, self.fw
        Y, b_Y = c.Y, c.b_Y
        TT = nc.vector.tensor_tensor

        def sink(m, psr, bpsr, psi, bpsi):
            Hre, bHre = c.H_r.next()
            fw.dma(fw.q_sp, Hre[:], self.Hs[slot][hf, m, :, 0:512], reads=[self.b_Hs], writes=[bHre])
            Him, bHim = c.H_r.next()
            fw.dma(fw.q_sp, Him[:], self.Hs[slot][hf, m, :, 512:1024], reads=[self.b_Hs], writes=[bHim])
            t1, bt1 = c.t_r.next()
            t2, bt2 = c.t_r.next()
            fw.op(fw.dve, lambda: TT(out=t1[:], in0=psr[:], in1=Hre[:], op=ALU.mult), reads=[bpsr, bHre], writes=[bt1])
            fw.op(fw.dve, lambda: TT(out=t2[:], in0=psi[:], in1=Him[:], op=ALU.mult), reads=[bpsi, bHim], writes=[bt2])
            fw.op(fw.dve, lambda: TT(out=Y[:, m, :], in0=t1[:], in1=t2[:], op=ALU.subtract), reads=[bt1, bt2], writes=[b_Y])
            t3, bt3 = c.t_r.next()
            t4, bt4 = c.t_r.next()
            fw.op(fw.dve, lambda: TT(out=t3[:], in0=psr[:], in1=Him[:], op=ALU.mult), reads=[bpsr, bHim], writes=[bt3])
            fw.op(fw.dve, lambda: TT(out=t4[:], in0=psi[:], in1=Hre[:], op=ALU.mult), reads=[bpsi, bHre], writes=[bt4])
            fw.op(fw.dve, lambda: TT(out=Y[:, 16 + m, :], in0=t3[:], in1=t4[:], op=ALU.add), reads=[bt3, bt4], writes=[b_Y])
            if m == 0:
                fw.op(fw.dve, lambda: TT(out=Y[0:1, 0, :], in0=psr[0:1, :], in1=Hre[0:1, :], op=ALU.mult),
                      reads=[bpsr, bHre], writes=[b_Y])
                fw.op(fw.dve, lambda: TT(out=Y[0:1, 16, :], in0=psi[0:1, :], in1=Him[0:1, :], op=ALU.mult),
                      reads=[bpsi, bHim], writes=[b_Y])
        return sink

    def c_sink_store(self, c, xst_r, dst, bdst):
        nc, fw = self.nc, self.fw

        def sink(m, psr, bpsr, psi, bpsi):
            xs, bxs = xst_r.next()
            fw.op(fw.act, lambda: nc.scalar.copy(out=xs[:, 0:512], in_=psr[:]), reads=[bpsr], writes=[bxs])
            fw.op(fw.act, lambda: nc.scalar.copy(out=xs[:, 512:1024], in_=psi[:]), reads=[bpsi], writes=[bxs])
            fw.dma(fw.q_sp, dst[m], xs[:], reads=[bxs], writes=[bdst])
        return sink

    def c_mac(self, c, s3, slots, xsd, b_xsd, hf):
        nc, fw = self.nc, self.fw
        Y, b_Y = c.Y, c.b_Y
        negI, negI0, I0, E00 = [self.signm[:, i, :] for i in range(4)]
        for m in range(16):
            psr, bpsr = self.bank()
            psi, bpsi = self.bank()
            for J in range(8):
                X, bX = s3.X_r.next()
                fw.dma(fw.q_sp, X[:], xsd[J, m], reads=[b_xsd], writes=[bX])
                H, bH = s3.H_r.next()
                fw.dma(fw.q_sp, H[:], self.Hs[slots[J]][hf, m], reads=[self.b_Hs], writes=[bH])
                pA, bpA = s3.P_r.next()
                pB, bpB = s3.P_r.next()
                fw.op(fw.dve, lambda: nc.vector.tensor_tensor(out=pA[:], in0=X[:], in1=H[:], op=ALU.mult), reads=[bX, bH], writes=[bpA])
                fw.op(fw.pool, lambda: nc.gpsimd.tensor_tensor(out=pB[:, 0:512], in0=X[:, 0:512], in1=H[:, 512:1024], op=ALU.mult),
                      reads=[bX, bH], writes=[bpB])
                fw.op(fw.dve, lambda: nc.vector.tensor_tensor(out=pB[:, 512:1024], in0=X[:, 512:1024], in1=H[:, 0:512], op=ALU.mult),
                      reads=[bX, bH], writes=[bpB])

                def acc():
                    nc.tensor.matmul(psr[:], lhsT=self.ident[:], rhs=pA[:, 0:512], start=(J == 0), stop=False)
                    nc.tensor.matmul(psr[:], lhsT=(negI0 if m == 0 else negI), rhs=pA[:, 512:1024], start=False, stop=(J == 7))
                    idm = I0 if m == 0 else self.ident[:]
                    nc.tensor.matmul(psi[:], lhsT=idm, rhs=pB[:, 0:512], start=(J == 0), stop=False)
                    if m == 0:
                        nc.tensor.matmul(psi[:], lhsT=E00, rhs=pA[:, 512:1024], start=False, stop=False)
                    return nc.tensor.matmul(psi[:], lhsT=idm, rhs=pB[:, 512:1024], start=False, stop=(J == 7))
                fw.op(fw.pe, acc, reads=[bpA, bpB, self.b_ident, self.b_signm], writes=[bpsr, bpsi])
            fw.op(fw.act, lambda: nc.scalar.copy(out=Y[:, m, :], in_=psr[:]), reads=[bpsr], writes=[b_Y])
            fw.op(fw.act, lambda: nc.scalar.copy(out=Y[:, 16 + m, :], in_=psi[:]), reads=[bpsi], writes=[b_Y])

    def c_inverse(self, c, order, hf):
        nc, fw = self.nc, self.fw
        Y, b_Y = c.Y, c.b_Y
        for tg in range(4):
            banks = [self.bank() for _ in range(4)]
            tsl = slice(tg * 512, (tg + 1) * 512)
            for fp in range(16):
                Gp, bG = c.G_r.next()
                fw.dma(fw.q_sp, Gp[:], self.Gtab[tg, fp * 2:(fp + 1) * 2].rearrange("f p n -> p f n"), writes=[bG])
                for fi in range(2):
                    f = fp * 2 + fi
                    for cc in range(4):
                        self.mm_group(banks[cc][0][:], [(Y[:, f, cc * 128:(cc + 1) * 128], Gp[:, fi, :])],
                                      reads=[b_Y, bG], writes=[banks[cc][1]], start=(f == 0), stop=(f == 31))
            for cc in range(4):
                ps, bps = banks[cc]
                t, bt = c.t_r.next()
                jc = hf * 4 + cc
                fw.op(fw.dve, lambda: nc.vector.scalar_tensor_tensor(out=t[:], in0=c.zTf[:, cc, tsl], scalar=self.skc[:, order, jc:jc + 1],
                                                                     in1=ps[:], op0=ALU.mult, op1=ALU.add),
                      reads=[c.b_zTf, self.b_skc, bps], writes=[bt])
                if order == 0:
                    fw.op(fw.dve, lambda: nc.vector.tensor_tensor(out=c.zTf[:, cc, tsl], in0=t[:], in1=c.x1T[:, cc, tsl], op=ALU.mult),
                          reads=[bt, c.b_x1T], writes=[c.b_zTf])
                else:
                    fw.op(fw.dve, lambda: nc.vector.tensor_tensor(out=self.zfT[:, jc, tsl], in0=t[:], in1=self.zfT[:, jc, tsl], op=ALU.mult),
                          reads=[bt, self.b_zfT], writes=[self.b_zfT])

    def stage_C(self, job, st):
        c = self.c_alloc(st)
        self.c_alloc_inv(c, st)
        for hf in range(2):
            self.c_project(c, (0, 1, 2), hf)
            for order in range(2):
                self.c_totok(c)
                self.c_fwd(c, self.c_sink_mul(c, order, hf))
                self.c_inverse(c, order, hf)

    def stage_C_sample(self, job, st):
        nc, fw = self.nc, self.fw
        c = self.c_alloc(st, sample=True)
        oh, b_oh = self.sb(st, "onehot", [128, 8], F32)
        fw.dma(fw.q_sp, oh[:], self.onehotd[:, :], writes=[b_oh])
        for hf in range(2):
            self.c_project(c, (2,), hf)
        with ExitStack() as s1:
            rings = {
                "x": Ring(nc, s1, "xr", 2, [128, D], F32),
                "sq": Ring(nc, s1, "sqr", 2, [128, D], BF16),
                "col": Ring(nc, s1, "colr", 4, [128, 4], F32),
                "hn": Ring(nc, s1, "hnr", 2, [128, D], BF16),
            }
            self.load_gain(s1, "mix")
            xst_r = Ring(nc, s1, "xst", 2, [128, 1024], BF16)
            etmp, b_etmp = self.sb(s1, "etmp", [128, 8, 128], BF16)
            for K in range(8):
                for t in range(16):
                    xt, bx = rings["x"].next()
                    fw.dma(fw.q_sp, xt[:], self.xfull[K, t * 128:(t + 1) * 128, :], writes=[bx])
                    self.rmsnorm_T(rings, xt[:], bx, self.gmix, self.b_gmix, self.hT[:, :, t * 128:(t + 1) * 128], self.b_hT)
                xt, bx = rings["x"].next()
                fw.dma(fw.q_sp, xt[:], self.xedge[K], writes=[bx])
                self.rmsnorm_T(rings, xt[:], bx, self.gmix, self.b_gmix, etmp[:], b_etmp)
                fw.op(fw.dve, lambda: nc.vector.tensor_copy(out=self.edge[:], in_=etmp[:, :, 0:2]), reads=[b_etmp], writes=[self.b_edge])
                for hf in range(2):
                    self.c_project(c, (0, 1), hf)
                    fw.dma(fw.q_sp, self.zs[hf, K], c.zTf[:], reads=[c.b_zTf], writes=[self.b_zs])
                    fw.dma(fw.q_sp, self.x1s[hf, K], c.x1T[:], reads=[c.b_x1T], writes=[self.b_x1s])
                    self.c_totok(c)
                    self.c_fwd(c, self.c_sink_store(c, xst_r, self.xs0[hf, K], self.b_xs0))
            fw.barrier()
        with ExitStack() as s3_:
            self.c_alloc_inv(c, s3_)
            s3 = KB._NS()
            s3.X_r = Ring(nc, s3_, "XJ", 2, [128, 1024], BF16)
            s3.H_r = Ring(nc, s3_, "HAB", 2, [128, 1024], BF16)
            s3.P_r = Ring(nc, s3_, "PAB", 4, [128, 1024], BF16)
            xst_r = Ring(nc, s3_, "xst3", 1, [128, 1024], BF16)
            for hf in range(2):
                for Jb in range(8):
                    self.c_mac(c, s3, [2 + (Jb - K + 7) for K in range(8)], self.xs0[hf], self.b_xs0, hf)
                    fw.dma(fw.q_sp, c.zTf[:], self.zs[hf, Jb], reads=[self.b_zs], writes=[c.b_zTf])
                    fw.dma(fw.q_sp, c.x1T[:], self.x1s[hf, Jb], reads=[self.b_x1s], writes=[c.b_x1T])
                    self.c_inverse(c, 0, hf)
                    fw.dma(fw.q_sp, self.z1s[hf, Jb], c.zTf[:], reads=[c.b_zTf], writes=[self.b_z1s])
                    self.c_totok(c)
                    self.c_fwd(c, self.c_sink_store(c, xst_r, self.xs1[hf, Jb], self.b_xs1))
                self.c_mac(c, s3, [17 + J for J in range(8)], self.xs1[hf], self.b_xs1, hf)
                fw.op(fw.dve, lambda: nc.vector.memset(c.zTf[:], 0.0), writes=[c.b_zTf])
                for J in range(8):
                    fw.dma(fw.q_sp, c.x1T[:], self.z1s[hf, J], reads=[self.b_z1s], writes=[c.b_x1T])
                    fw.op(fw.dve, lambda: nc.vector.scalar_tensor_tensor(out=c.zTf[:], in0=c.x1T[:], scalar=oh[:, J:J + 1], in1=c.zTf[:],
                                                                         op0=ALU.mult, op1=ALU.add),
                          reads=[c.b_x1T, c.b_zTf, b_oh], writes=[c.b_zTf])
                self.c_inverse(c, 1, hf)
            fw.barrier()

    def stage_D(self, job, rings, st):
        nc, fw = self.nc, self.fw
        mT, b_mT = self.sb(st, "mergedT", [128, 8, 512], BF16)
        x2, b_x2 = self.sb(st, "x2", [128, 4, D], F32)
        hmT, b_hmT = self.sb(st, "hmT", [128, 8, 512], BF16)
        aT, b_aT = self.sb(st, "aT", [128, 32, 512], BF16)
        w8 = Ring(nc, st, "w8_", 3, [128, 8, 128], BF16)
        wrow = Ring(nc, st, "wrow_", 3, [128, 1, D], BF16)
        gt = Ring(nc, st, "gt_", 4, [128, 512], F32)
        yt = Ring(nc, st, "yt_", 2, [128, D], F32)
        for tg in range(4):
            tsl = slice(tg * 512, (tg + 1) * 512)
            for c in range(8):
                gs = []
                for gi in range(2):
                    wt, bw = w8.next()
                    col0 = C_G + gi * D + c * 128
                    self.load_w(wt[:], bw, self.w_in[:, col0:col0 + 128], 8)
                    ps, bps = self.bank()
                    self.mm_group(ps[:], [(wt[:, k, :], self.hT[:, k, tsl]) for k in range(8)],
                                  reads=[bw, self.b_hT], writes=[bps])
                    g, bg = gt.next()
                    fw.op(fw.act, lambda g=g, ps=ps: nc.scalar.activation(out=g[:], in_=ps[:], func=ACTF.Sigmoid),
                          reads=[bps], writes=[bg])
                    gs.append((g, bg))
                wt, bw = w8.next()
                self.load_w(wt[:, 0:4, :], bw, self.w_ab[:, c * 128:(c + 1) * 128], 4)
                ps, bps = self.bank()
                self.mm_group(ps[:], [(wt[:, k, :], self.attnT[:, k, tsl]) for k in range(4)],
                              reads=[bw, self.b_attnT], writes=[bps])
                g, bg = gs[0]
                fw.op(fw.dve, lambda g=g, ps=ps: nc.vector.tensor_tensor(out=g[:], in0=g[:], in1=ps[:], op=ALU.mult),
                      reads=[bps, bg], writes=[bg])
                wt, bw = w8.next()
                self.load_w(wt[:], bw, self.w_hb[:, c * 128:(c + 1) * 128], 8)
                ps, bps = self.bank()
                self.mm_group(ps[:], [(wt[:, k, :], self.zfT[:, k, tsl]) for k in range(8)],
                              reads=[bw, self.b_zfT], writes=[bps])
                g2, bg2 = gs[1]
                fw.op(fw.dve, lambda g2=g2, ps=ps: nc.vector.tensor_tensor(out=g2[:], in0=g2[:], in1=ps[:], op=ALU.mult),
                      reads=[bps, bg2], writes=[bg2])
                fw.op(fw.dve, lambda g=g, g2=g2, c=c: nc.vector.tensor_tensor(out=mT[:, c, :], in0=g[:], in1=g2[:], op=ALU.add),
                      reads=[bg, bg2], writes=[b_mT])
            for pss in range(2):
                banks = [[self.bank() for _ in range(2)] for _ in range(2)]
                for kc in range(8):
                    wt, bw = wrow.next()
                    self.load_w(wt[:], bw, self.w_out[kc * 128:(kc + 1) * 128, :], 1)
                    for tt in range(2):
                        tok = (pss * 2 + tt) * 128
                        for hf in range(2):
                            ps, bps = banks[tt][hf]
                            self.mm_group(ps[:], [(mT[:, kc, tok:tok + 128], wt[:, 0, hf * 512:(hf + 1) * 512])],
                                          reads=[bw, b_mT], writes=[bps], start=(kc == 0), stop=(kc == 7))
                for tt in range(2):
                    ti = pss * 2 + tt
                    gtok = tg * 512 + ti * 128
                    xt, bx = rings["x"].next()
                    fw.dma(fw.q_sp, xt[:], self.xm[job.idx, gtok:gtok + 128, :], writes=[bx])
                    for hf in range(2):
                        ps, bps = banks[tt][hf]
                        fw.op(fw.dve, lambda ps=ps, ti=ti, hf=hf, xt=xt: nc.vector.tensor_tensor(
                            out=x2[:, ti, hf * 512:(hf + 1) * 512], in0=xt[:, hf * 512:(hf + 1) * 512], in1=ps[:], op=ALU.add),
                            reads=[bps, bx], writes=[b_x2])
                    self.rmsnorm_T(rings, x2[:, ti, :], b_x2, self.gmlp, self.b_gmlp,
                                   hmT[:, :, ti * 128:(ti + 1) * 128], b_hmT)
            for f in range(32):
                wt, bw = w8.next()
                self.load_w(wt[:], bw, self.w_ff1[:, f * 128:(f + 1) * 128], 8)
                ps, bps = self.bank()
                self.mm_group(ps[:], [(wt[:, k, :], hmT[:, k, :]) for k in range(8)], reads=[bw, b_hmT], writes=[bps])
                g, bg = gt.next()
                fw.op(fw.act, lambda: nc.scalar.activation(out=g[:], in_=ps[:], func=ACTF.Relu), reads=[bps], writes=[bg])
                fw.op(fw.dve, lambda: nc.vector.tensor_tensor(out=aT[:, f, :], in0=g[:], in1=g[:], op=ALU.mult),
                      reads=[bg], writes=[b_aT])
            for pss in range(2):
                banks = [[self.bank() for _ in range(2)] for _ in range(2)]
                for f in range(32):
                    wt, bw = wrow.next()
                    self.load_w(wt[:], bw, self.w_ff2[f * 128:(f + 1) * 128, :], 1)
                    for tt in range(2):
                        tok = (pss * 2 + tt) * 128
                        for hf in range(2):
                            ps, bps = banks[tt][hf]
                            self.mm_group(ps[:], [(aT[:, f, tok:tok + 128], wt[:, 0, hf * 512:(hf + 1) * 512])],
                                          reads=[bw, b_aT], writes=[bps], start=(f == 0), stop=(f == 31))
                for tt in range(2):
                    ti = pss * 2 + tt
                    gtok = tg * 512 + ti * 128
                    o, bo = yt.next()
                    for hf in range(2):
                        ps, bps = banks[tt][hf]
                        fw.op(fw.dve, lambda ps=ps, ti=ti, hf=hf, o=o: nc.vector.tensor_tensor(
                            out=o[:, hf * 512:(hf + 1) * 512], in0=x2[:, ti, hf * 512:(hf + 1) * 512], in1=ps[:], op=ALU.add),
                            reads=[bps, b_x2], writes=[bo])
                    fw.dma(fw.q_sp, self.y[job.idx, gtok:gtok + 128, :], o[:], reads=[bo])


def build_program(njobs, dbg=(), stages="ABCD", fake_branches=False, sample_last=False, nslots=1):
    nc = bass.Bass("TRN2", target_bir_lowering=False)
    with ExitStack() as st:
        kb = KB(nc, st, njobs, dbg, nslots)
        fw = kb.fw
        with ExitStack() as sw:
            kb.precast_weights(sw)
            fw.barrier()
        if "C" in stages:
            with ExitStack() as s0:
                kb.init_hyena(s0)
                fw.barrier()
        kb.alloc_persistent()
        if "B" in stages:
            kb.init_attention()
        for j in range(njobs):
            job = Job(j, sample_last and j == njobs - 1)
            with ExitStack() as sa:
                rings = {
                    "x": Ring(nc, sa, "xr", 2, [128, D], F32),
                    "sq": Ring(nc, sa, "sqr", 2, [128, D], BF16),
                    "col": Ring(nc, sa, "colr", 4, [128, 4], F32),
                    "hn": Ring(nc, sa, "hnr", 2, [128, D], BF16),
                }
                kb.load_gain(sa, "mix")
                kb.stage_A(job, rings)
                fw.barrier()
                if fake_branches:
                    fw.op(fw.dve, lambda: nc.vector.tensor_copy(out=kb.attnT[:], in_=kb.hT[:, 0:4, :]), reads=[kb.b_hT], writes=[kb.b_attnT])
                    fw.op(fw.dve, lambda: nc.vector.tensor_copy(out=kb.zfT[:], in_=kb.hT[:]), reads=[kb.b_hT], writes=[kb.b_zfT])
                if "B" in stages:
                    with ExitStack() as sb_:
                        kb.stage_B(job, rings, sb_)
                        fw.barrier()
                    kb.dbg_dump("attnT", kb.attnT[:], kb.b_attnT, [128, 4, L], BF16)
            if "C" in stages:
                with ExitStack() as sc_:
                    if job.sample:
                        kb.stage_C_sample(job, sc_)
                    else:
                        kb.stage_C(job, sc_)
                    fw.barrier()
                if job.sample:
                    with ExitStack() as sa2:
                        rings = {
                            "x": Ring(nc, sa2, "xr", 2, [128, D], F32),
                            "sq": Ring(nc, sa2, "sqr", 2, [128, D], BF16),
                            "col": Ring(nc, sa2, "colr", 4, [128, 4], F32),
                            "hn": Ring(nc, sa2, "hnr", 2, [128, D], BF16),
                        }
                        kb.load_gain(sa2, "mix")
                        kb.stage_A(job, rings)
                        fw.barrier()
                kb.dbg_dump("zfT", kb.zfT[:], kb.b_zfT, [128, 8, L], BF16)
            if "D" in stages:
                with ExitStack() as sd:
                    rings = {
                        "x": Ring(nc, sd, "xr", 2, [128, D], F32),
                        "sq": Ring(nc, sd, "sqr", 2, [128, D], BF16),
                        "col": Ring(nc, sd, "colr", 4, [128, 4], F32),
                        "hn": Ring(nc, sd, "hnr", 2, [128, D], BF16),
                    }
                    kb.load_gain(sd, "mlp")
                    kb.stage_D(job, rings, sd)
                    fw.barrier()
        fw.barrier()
    return nc, kb


_TABLES = {}


def _sq(a):
    a = np.asarray(a)
    return np.ascontiguousarray(a[0]) if a.shape[0] == 1 else a


def kernel(**inputs):
    f32 = lambda a: np.ascontiguousarray(np.asarray(a), dtype=np.float32)
    xp = f32(inputs["x_prompt"])
    xs = f32(inputs["x_sample"])[0]
    njobs = 5
    if "dft" not in _TABLES:
        _TABLES["dft"] = dft_tables()
    Ftab, Gtab = _TABLES["dft"]
    shared = dict(
        w_in=f32(inputs["w_in"])[0], w_ab=f32(inputs["w_attn_branch"])[0], w_hb=f32(inputs["w_hyena_branch"])[0],
        w_out=f32(inputs["w_out"])[0], w_ff1=f32(inputs["w_ff1"])[0], w_ff2=f32(inputs["w_ff2"])[0],
        g_mix=f32(inputs["g_mix"])[0], g_mlp=f32(inputs["g_mlp"])[0], rel_bias=f32(inputs["rel_bias"]),
        g_q=f32(inputs["g_q"])[0], g_k=f32(inputs["g_k"])[0],
        filt_w1=f32(inputs["filt_w1"])[0], filt_b1=f32(inputs["filt_b1"])[0], filt_w2=f32(inputs["filt_w2"])[0],
        filt_b2=f32(inputs["filt_b2"])[0], filt_w3=f32(inputs["filt_w3"])[0], filt_b3=f32(inputs["filt_b3"])[0],
        filt_w4=f32(inputs["filt_w4"])[0], filt_freq=f32(inputs["filt_freq"])[0], filt_skip=f32(inputs["filt_skip"])[0],
        w_short=f32(inputs["w_short"])[0], b_short=f32(inputs["b_short"])[0], Ftab=Ftab, Gtab=Gtab)
    zf_sh, cols_sh, nad = filter_tables_shared()
    xfull = np.ascontiguousarray(xs.reshape(8, L, D))
    xedge = np.zeros((8, 128, D), np.float32)
    for K in range(8):
        if K > 0:
            xedge[K, 0] = xs[L * K - 1]
        if K < 7:
            xedge[K, 1] = xs[L * (K + 1)]
    shared.update(zfT_tab=zf_sh, fcols_tab=cols_sh, nad_tab=nad, xfull=xfull, xedge=xedge)
    in_maps = []
    for c in range(NCORES):
        xm = np.concatenate([xp[4 * c:4 * c + 4], xs[None, L * c:L * (c + 1)]], axis=0)
        xh = np.zeros((L, D), np.float32)
        if c > 0:
            xh[:1024] = xs[L * c - 1024:L * c]
        if c < NCORES - 1:
            xh[1024:] = xs[L * (c + 1):L * (c + 1) + 1024]
        zfc, colsc, oh = filter_tables_core(c)
        m = dict(shared)
        m.update(xm=np.ascontiguousarray(xm), xh=xh, zfT_core=zfc, fcols_core=colsc, onehot=oh)
        m.update(host_consts(c))
        in_maps.append(m)
    nc, kb = build_program(njobs, stages="ABCD", sample_last=True, nslots=25)
    res = run_bass_kernel_spmd(nc, in_maps, core_ids=list(range(NCORES)))
    y_prompt = np.empty((32, L, D), np.float32)
    y_sample = np.empty((1, 8 * L, D), np.float32)
    for c in range(NCORES):
        y = res.results[c]["y"]
        y_prompt[4 * c:4 * c + 4] = y[0:4]
        y_sample[0, L * c:L * (c + 1)] = y[4]
    return (y_prompt, y_sample)
```
